# Optimizing a Trainium2 kernel written in Bass

```python
import math, functools
import jax, jax.numpy as jnp
from jax import lax
import numpy as np

D_MODEL = 1024
BATCH = 16
SEQ = 256
DEPTH = 2
DEC_BATCH = 2
DEC_SEQ = 2048
PAST_LEN = 256

GRID_W = 64
HEAD_DIM = 64
HY_CH = 256
NA_HEADS = 6
GQA_Q_HEADS = 6
GQA_KV_HEADS = 2
MIX_WIDTH = HY_CH + NA_HEADS * HEAD_DIM + GQA_Q_HEADS * HEAD_DIM
IN_WIDTH = 3 * HY_CH + 3 * NA_HEADS * HEAD_DIM + (GQA_Q_HEADS + 2 * GQA_KV_HEADS) * HEAD_DIM
D_FF = 2816
N_ADA = 9
SHORT_CONV = 3
HY_BANDS = 16
HY_EMB = 1 + 2 * HY_BANDS
HY_FILT_W = 64
HY_DECAY_TARGET = 1e-2
HY_FAST_PCT = 0.3
HY_SLOW_PCT = 1.5
NA_ROWS = 8
NA_COLS = 16
NA_QCOLS = 16
NA_KCOLS = 32
GQA_WINDOW = 128
BLK = 128
ROPE_BASE = 10000.0
EPS = 1e-6
NEG_INF = -1e30

kernel_name = "hymba_style_hyena_natten_swa_diffusion_step"


def _rmsnorm(x, g):
    xf = x.astype(jnp.float32)
    xf = xf * lax.rsqrt(jnp.mean(xf * xf, axis=-1, keepdims=True) + EPS)
    return xf.astype(x.dtype) * g


def _swiglu(h, w1, w3, w2):
    return (jax.nn.silu(h @ w1) * (h @ w3)) @ w2


def _short_conv(u, w, b):
    n = u.shape[1]
    up = jnp.pad(u, ((0, 0), (1, 1), (0, 0)))
    return up[:, 0:n] * w[0] + up[:, 1:n + 1] * w[1] + up[:, 2:n + 2] * w[2] + b


def _hyena_filter_fft(L, lp):
    f32 = jnp.float32
    idx = jnp.arange(L, dtype=f32)
    t = idx / (L - 1)
    bands = jnp.linspace(1e-4, HY_BANDS - 1, HY_BANDS, dtype=f32)
    ang = (2.0 * math.pi / L) * idx[:, None] * bands[None, :]
    feats = jnp.concatenate([t[:, None], jnp.cos(ang), -jnp.sin(ang)], axis=-1)
    freq = lp["hy_freq"].astype(f32)
    hid = jnp.sin(freq[0] * (feats @ lp["hy_filt_w1"].astype(f32) + lp["hy_filt_b1"].astype(f32)))
    hid = jnp.sin(freq[1] * (hid @ lp["hy_filt_w2"].astype(f32) + lp["hy_filt_b2"].astype(f32)))
    taps = (hid @ lp["hy_filt_w3"].astype(f32)).reshape(L, 2, HY_CH)
    max_decay = math.log(HY_DECAY_TARGET) / HY_FAST_PCT
    min_decay = math.log(HY_DECAY_TARGET) / HY_SLOW_PCT
    deltas = jnp.abs(jnp.linspace(min_decay, max_decay, HY_CH, dtype=f32))
    taps = taps * jnp.exp(-t[:, None, None] * deltas)
    two_sided = jnp.concatenate([taps[:, 0], jnp.zeros((1, HY_CH), f32), taps[:0:-1, 1]], axis=0)
    return jnp.fft.rfft(two_sided, axis=0)


def _hyena(u, lp):
    L = u.shape[1]
    uc = _short_conv(u, lp["hy_conv_w"], lp["hy_conv_b"])
    x0, x1, v = jnp.split(uc, 3, axis=-1)
    kf = _hyena_filter_fft(L, lp)
    z = (x1 * v).astype(jnp.float32)
    zf = jnp.fft.rfft(z, n=2 * L, axis=1)
    conv = jnp.fft.irfft(zf * kf[None], n=2 * L, axis=1)[:, :L]
    y = conv + z * lp["hy_skip"].astype(jnp.float32)
    return x0 * y.astype(u.dtype)


def _axial_rope(x):
    N, Dh = x.shape[1], x.shape[-1]
    pos = jnp.arange(N)
    quarter = Dh // 4
    half = Dh // 2
    inv = ROPE_BASE ** (-jnp.arange(quarter, dtype=jnp.float32) / quarter)

    def rot(xh, coord):
        ang = coord.astype(jnp.float32)[:, None] * inv[None, :]
        cos = jnp.cos(ang)[None, :, None, :].astype(x.dtype)
        sin = jnp.sin(ang)[None, :, None, :].astype(x.dtype)
        a, b = xh[..., :quarter], xh[..., quarter:]
        return jnp.concatenate([a * cos - b * sin, a * sin + b * cos], axis=-1)

    return jnp.concatenate([rot(x[..., :half], pos // GRID_W), rot(x[..., half:], pos % GRID_W)], axis=-1)


def _dense_attention(q, k, v, sink):
    B, L, Hq, Dh = q.shape
    Hkv = k.shape[2]
    G = Hq // Hkv
    nb = L // BLK
    scale = Dh ** -0.5
    qb = jnp.moveaxis(q.reshape(B, nb, BLK, Hkv, G, Dh), 1, 0)

    def block(qi):
        s = jnp.einsum("bqhgd,bkhd->bhgqk", qi, k).astype(jnp.float32) * scale
        if sink is not None:
            s_sink = jnp.broadcast_to(sink.astype(jnp.float32).reshape(1, Hkv, G, 1, 1), s.shape[:-1] + (1,))
            p = jax.nn.softmax(jnp.concatenate([s, s_sink], axis=-1), axis=-1)[..., :L]
        else:
            p = jax.nn.softmax(s, axis=-1)
        return jnp.einsum("bhgqk,bkhd->bqhgd", p.astype(v.dtype), v)

    out = lax.map(block, qb)
    return jnp.moveaxis(out, 0, 1).reshape(B, L, Hq, Dh)


def _neighbourhood_attention(q, k, v, kc, vc, rpb):
    B, N, H, Dh = q.shape
    rows = N // GRID_W
    wr = min(NA_ROWS, rows)
    nb = GRID_W // NA_QCOLS
    scale = Dh ** -0.5
    r = jnp.arange(rows)
    row_idx = jnp.clip(r - wr // 2, 0, rows - wr)[:, None] + jnp.arange(wr)[None, :]
    blk = jnp.arange(nb)
    col_idx = jnp.clip(blk * NA_QCOLS - NA_COLS // 2, 0, GRID_W - NA_KCOLS)[:, None] + jnp.arange(NA_KCOLS)[None, :]
    q_col = blk[:, None] * NA_QCOLS + jnp.arange(NA_QCOLS)[None, :]
    win_lo = jnp.clip(q_col - NA_COLS // 2, 0, GRID_W - NA_COLS)[:, :, None]
    col_ok = (col_idx[:, None, :] >= win_lo) & (col_idx[:, None, :] < win_lo + NA_COLS)
    n_loc = wr * NA_KCOLS
    mask = jnp.broadcast_to(col_ok[:, :, None, :], (nb, NA_QCOLS, wr, NA_KCOLS)).reshape(nb, NA_QCOLS, n_loc)
    dr = row_idx - r[:, None]
    dc = jnp.clip(col_idx[:, None, :] - q_col[:, :, None], 1 - NA_COLS, NA_COLS - 1)
    bias = rpb.astype(jnp.float32)[:, dr[:, None, None, :, None] + NA_ROWS - 1, dc[None, :, :, None, :] + NA_COLS - 1]
    bias = bias.reshape(H, rows, nb, NA_QCOLS, n_loc)
    gr = row_idx[:, None, :, None]
    gc = col_idx[None, :, None, :]
    kg = k.reshape(B, rows, GRID_W, H, Dh)[:, gr, gc].reshape(B, rows, nb, n_loc, H, Dh)
    vg = v.reshape(B, rows, GRID_W, H, Dh)[:, gr, gc].reshape(B, rows, nb, n_loc, H, Dh)
    qg = q.reshape(B, rows, nb, NA_QCOLS, H, Dh)
    s_loc = jnp.einsum("brnqhd,brnkhd->bhrnqk", qg, kg).astype(jnp.float32) * scale
    s_loc = jnp.where(mask[None, None, None], s_loc + bias[None], NEG_INF)
    s_ctx = jnp.einsum("brnqhd,bkhd->bhrnqk", qg, kc).astype(jnp.float32) * scale
    p = jax.nn.softmax(jnp.concatenate([s_loc, s_ctx], axis=-1), axis=-1).astype(v.dtype)
    out = (jnp.einsum("bhrnqk,brnkhd->brnqhd", p[..., :n_loc], vg)
           + jnp.einsum("bhrnqk,bkhd->brnqhd", p[..., n_loc:], vc))
    return out.reshape(B, N, H, Dh)


def _window_attention(q, k, v, kc, vc, sink):
    B, N, Hq, Dh = q.shape
    Hkv = k.shape[2]
    G = Hq // Hkv
    nb = N // BLK
    scale = Dh ** -0.5
    qb = q.reshape(B, nb, BLK, Hkv, G, Dh)
    key_idx = jnp.arange(nb)[:, None] * BLK + jnp.arange(3 * BLK)[None, :]
    pad = ((0, 0), (BLK, BLK), (0, 0), (0, 0))
    kb = jnp.pad(k, pad)[:, key_idx]
    vb = jnp.pad(v, pad)[:, key_idx]
    q_pos = jnp.arange(nb)[:, None] * BLK + jnp.arange(BLK)[None, :]
    k_pos = (key_idx - BLK)[:, None, :]
    ok = (jnp.abs(q_pos[:, :, None] - k_pos) <= GQA_WINDOW) & (k_pos >= 0) & (k_pos < N)
    s_loc = jnp.einsum("bnqhgd,bnkhd->bhgnqk", qb, kb).astype(jnp.float32) * scale
    s_loc = jnp.where(ok[None, None, None], s_loc, NEG_INF)
    s_ctx = jnp.einsum("bnqhgd,bkhd->bhgnqk", qb, kc).astype(jnp.float32) * scale
    s_sink = jnp.broadcast_to(sink.astype(jnp.float32).reshape(1, Hkv, G, 1, 1, 1), s_loc.shape[:-1] + (1,))
    p = jax.nn.softmax(jnp.concatenate([s_loc, s_ctx, s_sink], axis=-1), axis=-1).astype(v.dtype)
    n_loc = 3 * BLK
    n_ctx = kc.shape[1]
    out = (jnp.einsum("bhgnqk,bnkhd->bnqhgd", p[..., :n_loc], vb)
           + jnp.einsum("bhgnqk,bkhd->bnqhgd", p[..., n_loc:n_loc + n_ctx], vc))
    return out.reshape(B, N, Hq, Dh)


def _project(h, lp):
    u = h @ lp["w_in"]
    B, L, _ = u.shape
    s0 = 3 * HY_CH
    s1 = s0 + 3 * NA_HEADS * HEAD_DIM
    nq = GQA_Q_HEADS * HEAD_DIM
    nk = GQA_KV_HEADS * HEAD_DIM
    u_hy = u[..., :s0]
    na = u[..., s0:s1].reshape(B, L, 3, NA_HEADS, HEAD_DIM)
    na_q = _rmsnorm(na[:, :, 0], lp["na_q_g"])
    na_k = _rmsnorm(na[:, :, 1], lp["na_k_g"])
    na_v = na[:, :, 2]
    gq = u[..., s1:]
    gq_q = _rmsnorm(gq[..., :nq].reshape(B, L, GQA_Q_HEADS, HEAD_DIM), lp["gqa_q_g"])
    gq_k = _rmsnorm(gq[..., nq:nq + nk].reshape(B, L, GQA_KV_HEADS, HEAD_DIM), lp["gqa_k_g"])
    gq_v = gq[..., nq + nk:].reshape(B, L, GQA_KV_HEADS, HEAD_DIM)
    return u_hy, na_q, na_k, na_v, gq_q, gq_k, gq_v


def _merge(y_hy, y_na, y_gq, lp):
    B, L = y_hy.shape[:2]
    y = jnp.concatenate([y_hy, y_na.reshape(B, L, -1), y_gq.reshape(B, L, -1)], axis=-1)
    return y @ lp["w_out"]


def _mixer_context(h, lp):
    u_hy, na_q, na_k, na_v, gq_q, gq_k, gq_v = _project(h, lp)
    y_hy = _hyena(u_hy, lp)
    y_na = _dense_attention(na_q, na_k, na_v, None)
    y_gq = _dense_attention(gq_q, gq_k, gq_v, lp["gqa_sink"])
    return _merge(y_hy, y_na, y_gq, lp), (na_k, na_v, gq_k, gq_v)


def _mixer_latent(h, lp, na_kc, na_vc, gq_kc, gq_vc):
    u_hy, na_q, na_k, na_v, gq_q, gq_k, gq_v = _project(h, lp)
    y_hy = _hyena(u_hy, lp)
    y_na = _neighbourhood_attention(na_q, na_k, na_v, na_kc, na_vc, lp["na_rpb"])
    y_gq = _window_attention(_axial_rope(gq_q), _axial_rope(gq_k), gq_v, gq_kc, gq_vc, lp["gqa_sink"])
    return _merge(y_hy, y_na, y_gq, lp), ()


def _layer(x, cvec, lp, mixer):
    mod = jax.nn.silu(cvec) @ lp["ada_w"] + lp["ada_b"]
    sh1, sc1, g1, sh2, sc2, g2, sh3, sc3, g3 = jnp.split(mod[:, None, :], N_ADA, axis=-1)
    h = _rmsnorm(x, lp["norm_g"][0]) * (1.0 + sc1) + sh1
    x = x + 0.5 * g1 * _swiglu(h, lp["ffn_w1"][0], lp["ffn_w3"][0], lp["ffn_w2"][0])
    h = _rmsnorm(x, lp["norm_g"][1]) * (1.0 + sc2) + sh2
    y, ctx = mixer(h)
    x = x + g2 * y
    h = _rmsnorm(x, lp["norm_g"][2]) * (1.0 + sc3) + sh3
    x = x + 0.5 * g3 * _swiglu(h, lp["ffn_w1"][1], lp["ffn_w3"][1], lp["ffn_w2"][1])
    return x, ctx


def setup_inputs(seed: int = 0) -> dict:
    key = jax.random.key(seed)
    ks = jax.random.split(key, 32)
    f32 = jnp.float32

    def nrm(k, shape, s):
        return jax.random.normal(k, shape, f32) * s

    D = D_MODEL
    return {
        "x_prompt": nrm(ks[0], (BATCH, SEQ, D), 1.0),
        "x_sample": nrm(ks[1], (DEC_BATCH, DEC_SEQ, D), 1.0),
        "cache_na_k": nrm(ks[2], (DEC_BATCH, DEPTH, PAST_LEN, NA_HEADS, HEAD_DIM), 1.0),
        "cache_na_v": nrm(ks[3], (DEC_BATCH, DEPTH, PAST_LEN, NA_HEADS, HEAD_DIM), 1.0),
        "cache_gqa_k": nrm(ks[4], (DEC_BATCH, DEPTH, PAST_LEN, GQA_KV_HEADS, HEAD_DIM), 1.0),
        "cache_gqa_v": nrm(ks[5], (DEC_BATCH, DEPTH, PAST_LEN, GQA_KV_HEADS, HEAD_DIM), 1.0),
        "c": nrm(ks[6], (DEC_BATCH, D), 1.0),
        "c_ctx": nrm(ks[7], (D,), 1.0),
        "ada_w": nrm(ks[8], (DEPTH, D, N_ADA * D), 0.5 * D ** -0.5),
        "ada_b": nrm(ks[9], (DEPTH, N_ADA * D), 0.02),
        "norm_g": 1.0 + nrm(ks[10], (DEPTH, 3, D), 0.05),
        "ffn_w1": nrm(ks[11], (DEPTH, 2, D, D_FF), D ** -0.5),
        "ffn_w3": nrm(ks[12], (DEPTH, 2, D, D_FF), D ** -0.5),
        "ffn_w2": nrm(ks[13], (DEPTH, 2, D_FF, D), D_FF ** -0.5),
        "w_in": nrm(ks[14], (DEPTH, D, IN_WIDTH), D ** -0.5),
        "w_out": nrm(ks[15], (DEPTH, MIX_WIDTH, D), MIX_WIDTH ** -0.5),
        "hy_conv_w": nrm(ks[16], (DEPTH, SHORT_CONV, 3 * HY_CH), SHORT_CONV ** -0.5),
        "hy_conv_b": nrm(ks[17], (DEPTH, 3 * HY_CH), 0.02),
        "hy_filt_w1": nrm(ks[18], (DEPTH, HY_EMB, HY_FILT_W), HY_EMB ** -0.5),
        "hy_filt_b1": nrm(ks[19], (DEPTH, HY_FILT_W), 0.1),
        "hy_filt_w2": nrm(ks[20], (DEPTH, HY_FILT_W, HY_FILT_W), HY_FILT_W ** -0.5),
        "hy_filt_b2": nrm(ks[21], (DEPTH, HY_FILT_W), 0.1),
        "hy_filt_w3": nrm(ks[22], (DEPTH, HY_FILT_W, 2 * HY_CH), 0.1 * HY_FILT_W ** -0.5),
        "hy_freq": 1.0 + nrm(ks[23], (DEPTH, 2, HY_FILT_W), 0.1),
        "hy_skip": nrm(ks[24], (DEPTH, HY_CH), 0.5),
        "na_q_g": 1.0 + nrm(ks[25], (DEPTH, HEAD_DIM), 0.05),
        "na_k_g": 1.0 + nrm(ks[26], (DEPTH, HEAD_DIM), 0.05),
        "na_rpb": nrm(ks[27], (DEPTH, NA_HEADS, 2 * NA_ROWS - 1, 2 * NA_COLS - 1), 0.02),
        "gqa_q_g": 1.0 + nrm(ks[28], (DEPTH, HEAD_DIM), 0.05),
        "gqa_k_g": 1.0 + nrm(ks[29], (DEPTH, HEAD_DIM), 0.05),
        "gqa_sink": nrm(ks[30], (DEPTH, GQA_Q_HEADS), 0.5),
    }


def reference(x_prompt, x_sample, cache_na_k, cache_na_v, cache_gqa_k, cache_gqa_v, c, c_ctx,
              ada_w, ada_b, norm_g, ffn_w1, ffn_w3, ffn_w2, w_in, w_out,
              hy_conv_w, hy_conv_b, hy_filt_w1, hy_filt_b1, hy_filt_w2, hy_filt_b2, hy_filt_w3,
              hy_freq, hy_skip, na_q_g, na_k_g, na_rpb, gqa_q_g, gqa_k_g, gqa_sink):
    y_prompt = x_prompt
    y_sample = x_sample
    nk_list, nv_list, gk_list, gv_list = [], [], [], []
    for l in range(DEPTH):
        lp = {
            "ada_w": ada_w[l], "ada_b": ada_b[l], "norm_g": norm_g[l],
            "ffn_w1": ffn_w1[l], "ffn_w3": ffn_w3[l], "ffn_w2": ffn_w2[l],
            "w_in": w_in[l], "w_out": w_out[l],
            "hy_conv_w": hy_conv_w[l], "hy_conv_b": hy_conv_b[l],
            "hy_filt_w1": hy_filt_w1[l], "hy_filt_b1": hy_filt_b1[l],
            "hy_filt_w2": hy_filt_w2[l], "hy_filt_b2": hy_filt_b2[l], "hy_filt_w3": hy_filt_w3[l],
            "hy_freq": hy_freq[l], "hy_skip": hy_skip[l],
            "na_q_g": na_q_g[l], "na_k_g": na_k_g[l], "na_rpb": na_rpb[l],
            "gqa_q_g": gqa_q_g[l], "gqa_k_g": gqa_k_g[l], "gqa_sink": gqa_sink[l],
        }
        y_prompt, (nk, nv, gk, gv) = _layer(y_prompt, c_ctx[None, :], lp, functools.partial(_mixer_context, lp=lp))
        nk_list.append(nk)
        nv_list.append(nv)
        gk_list.append(gk)
        gv_list.append(gv)
        y_sample, _ = _layer(y_sample, c, lp, functools.partial(
            _mixer_latent, lp=lp, na_kc=cache_na_k[:, l], na_vc=cache_na_v[:, l],
            gq_kc=cache_gqa_k[:, l], gq_vc=cache_gqa_v[:, l]))
    new_na_k = jnp.stack(nk_list, axis=1)
    new_na_v = jnp.stack(nv_list, axis=1)
    new_gqa_k = jnp.stack(gk_list, axis=1)
    new_gqa_v = jnp.stack(gv_list, axis=1)
    return (y_prompt, y_sample, new_na_k, new_na_v, new_gqa_k, new_gqa_v)
```

```python
import numpy as np
import math
from contextlib import ExitStack
import concourse.bass as bass
import concourse.mybir as mybir
from concourse.bass_utils import run_bass_kernel_spmd

F32 = mybir.dt.float32
BF16 = mybir.dt.bfloat16
AF = mybir.ActivationFunctionType
ALU = mybir.AluOpType

D = 1024
DFF = 2816
NFF = 22
DEPTH = 2
NADA = 9
INW = 2560
EPS = 1e-6

STAGES = {"mixer": True}


class Trk:
    __slots__ = ("name", "w", "r", "sem", "cnt", "psum")

    def __init__(self, name):
        self.name = name
        self.psum = False
        self.sem = {}
        self.cnt = {}
        self.w = None
        self.r = []


class T:
    def __init__(self, h, name):
        self.h = h
        self.trk = Trk(name)

    def __getitem__(self, k):
        return self.h[k]


class Op:
    __slots__ = ("eng", "fn", "deps", "signal", "sigval", "dma_trk", "dma_val", "waits", "cc", "kind")

    def __init__(self, eng, fn):
        self.eng = eng
        self.fn = fn
        self.deps = []
        self.signal = False
        self.sigval = 0
        self.dma_trk = None
        self.dma_val = 0
        self.cc = False
        self.kind = None


ENGS = ("pe", "act", "dve", "pool", "sp")


class Prog:
    def __init__(self, nc, es):
        self.nc = nc
        self.es = es
        self.ops = {e: [] for e in ENGS}
        self.nops = 0
        self.final = []

    def _tr(self, x):
        return x.trk if hasattr(x, 'trk') else x

    def op(self, eng, fn, reads=(), writes=(), dma_trk=None):
        o = Op(eng, fn)
        deps = []
        seen = set()
        for t in reads:
            t = self._tr(t)
            if t.w is not None and id(t.w) not in seen:
                seen.add(id(t.w))
                deps.append(t.w)
            if t.psum:
                for r in t.r:
                    if r.eng != eng and id(r) not in seen:
                        seen.add(id(r))
                        deps.append(r)
        dtk = self._tr(dma_trk) if dma_trk is not None else None
        for t in writes:
            t = self._tr(t)
            if t.w is not None and id(t.w) not in seen:
                if not (dtk is not None and t.w.dma_trk is dtk and t is dtk and t.w.kind == ("sw" if eng == "pool" else "hw")):
                    seen.add(id(t.w))
                    deps.append(t.w)
            for r in t.r:
                if id(r) not in seen:
                    seen.add(id(r))
                    deps.append(r)
        o.deps = deps
        for t in reads:
            self._tr(t).r.append(o)
        for t in writes:
            t = self._tr(t)
            t.w = o
            t.r = []
        if dma_trk is not None:
            dma_trk = self._tr(dma_trk)
            o.dma_trk = dma_trk
            o.kind = "sw" if eng == "pool" else "hw"
            dma_trk.cnt[o.kind] = dma_trk.cnt.get(o.kind, 0) + 16
            o.dma_val = dma_trk.cnt[o.kind]
        self.ops[eng].append(o)
        self.nops += 1
        return o

    def dma(self, q, out, in_, reads=(), writes=(), sem_on=None, **kw):
        st = sem_on if sem_on is not None else (writes[0] if writes else reads[0])
        return self.op(q, lambda e: e.dma_start(out=out, in_=in_, **kw), reads, writes, dma_trk=st)

    def collective(self, fn, reads, writes):
        o = self.op("pool", fn, reads, writes)
        o.cc = True
        o.eng = "cc"
        return o

    def store(self, q, out, in_, src, **kw):
        st = self._tr(src)
        if st not in self.final:
            self.final.append(st)
        return self.op(q, lambda e: e.dma_start(out=out, in_=in_, **kw), [src], [], dma_trk=st)

    def emit(self):
        nc = self.nc
        for e in ENGS:
            for o in self.ops[e]:
                for d in o.deps:
                    if d.dma_trk is None:
                        if d.eng == "pe" and o.eng == "pe" and o.dma_trk is None:
                            continue
                        d.signal = True
        for e in ENGS:
            c = 0
            for o in self.ops[e]:
                if o.cc:
                    continue
                if o.signal and o.dma_trk is None:
                    c += 1
                    o.sigval = c
        c = 0
        for o in self.ops["pool"]:
            if o.cc:
                c += 1
                o.sigval = c
                o.signal = True
        esem = {e: self.es.enter_context(nc.semaphore("s_" + e)) for e in ENGS + ("cc",)}
        trks = []
        for e in ENGS:
            for o in self.ops[e]:
                if o.dma_trk is not None and o.kind not in o.dma_trk.sem:
                    o.dma_trk.sem[o.kind] = self.es.enter_context(nc.semaphore("d%d" % len(trks)))
                    trks.append((o.dma_trk, o.kind))
        self.n_dma_sems = len(trks)
        finals = [(t.sem[k], t.cnt[k]) for (t, k) in trks if t in self.final]
        ops = self.ops

        def run(e, eng):
            seen = {}
            for o in ops[e]:
                for d in o.deps:
                    if d.dma_trk is not None:
                        s, v = d.dma_trk.sem[d.kind], d.dma_val
                    else:
                        if d.eng == "pe" and e == "pe" and o.dma_trk is None:
                            continue
                        s, v = esem[d.eng], d.sigval
                    if seen.get(s.num, 0) < v:
                        eng.wait_ge(s, v)
                        seen[s.num] = v
                ins = o.fn(eng)
                if o.cc:
                    ins.then_inc(esem["cc"])
                elif o.dma_trk is not None:
                    ins.then_inc(o.dma_trk.sem[o.kind], 16)
                elif o.signal:
                    ins.then_inc(esem[e], 1)
            if e == "sp":
                for s, v in finals:
                    eng.wait_ge(s, v)

        with nc.Block() as block:
            @block.tensor
            def _(eng):
                run("pe", eng)

            @block.scalar
            def _(eng):
                run("act", eng)

            @block.vector
            def _(eng):
                run("dve", eng)

            @block.gpsimd
            def _(eng):
                run("pool", eng)

            @block.sync
            def _(eng):
                run("sp", eng)


class Builder:
    def __init__(self):
        self.nc = bass.Bass("TRN2", target_bir_lowering=False)
        self.es = ExitStack()
        self.P = Prog(self.nc, self.es)
        self.din = {}
        self.dout = {}
        self._n = 0

    def dram_in(self, name, shape, dt=F32):
        t = self.nc.dram_tensor(name, list(shape), dt, kind="ExternalInput")
        self.din[name] = t
        return t.ap()

    def dram_out(self, name, shape, dt=F32):
        t = self.nc.dram_tensor(name, list(shape), dt, kind="ExternalOutput")
        self.dout[name] = T(t, name)
        return self.dout[name]

    def sb(self, shape, dt, name=None):
        self._n += 1
        name = "sb_" + (name or ("t%d" % self._n))
        h = self.es.enter_context(self.nc.sbuf_tensor(name, list(shape), dt))
        return T(h, name)

    def ps(self, name):
        h = self.es.enter_context(self.nc.psum_tensor(name, [128, 512], F32))
        t = T(h, name)
        t.trk.psum = True
        return t

    def mm(self, out, lhsT, rhs, start, stop, reads, writes, **kw):
        return self.P.op("pe", lambda e: e.matmul(out, lhsT, rhs, start=start, stop=stop,
                                                   skip_group_check=True, **kw), reads, writes)

    def tr(self, out, in_, ident, reads, writes):
        return self.P.op("pe", lambda e: e.transpose(out, in_, ident), reads, writes)

    def act(self, out, in_, func, reads, writes, bias=None, scale=None, eng="act"):
        kw = {}
        if bias is not None:
            kw["bias"] = bias
        if scale is not None:
            kw["scale"] = scale
        return self.P.op("act", lambda e: e.activation(out, in_, func, **kw), reads, writes)

    def tt(self, out, a, b, op, reads, writes, eng="dve"):
        return self.P.op(eng, lambda e: e.tensor_tensor(out, a, b, op), reads, writes)

    def ts(self, out, a, s1, s2, op0, op1, reads, writes, eng="dve"):
        if op1 is None:
            return self.P.op(eng, lambda e: e.tensor_scalar(out, a, s1, None, op0), reads, writes)
        return self.P.op(eng, lambda e: e.tensor_scalar(out, a, s1, s2, op0, op1), reads, writes)

    def stt(self, out, a, s, b, op0, op1, reads, writes, eng="dve"):
        return self.P.op(eng, lambda e: e.scalar_tensor_tensor(out, a, s, b, op0, op1), reads, writes)

    def cp(self, out, in_, reads, writes, eng="dve"):
        if eng == "act":
            return self.P.op("act", lambda e: e.copy(out, in_), reads, writes)
        return self.P.op(eng, lambda e: e.tensor_copy(out, in_), reads, writes)


NEG = -30000.0
DBG = {}
NPAT = 21


def na_pairs_for_m(m):
    def rs(r):
        return min(max(r - 4, 0), 24)
    r0, r1 = 2 * m, 2 * m + 1
    j0 = rs(r0) // 2
    j1 = (rs(r1) + 7) // 2
    out = []
    for j in range(j0, j1 + 1):
        if 2 <= m <= 13:
            pid = (j - m) + 2
            assert 0 <= pid <= 4
        else:
            eidx = {0: 0, 1: 1, 14: 2, 15: 3}[m]
            jb = 0 if m < 2 else 12
            pid = 5 + eidx * 4 + (j - jb)
            assert 0 <= j - jb < 4
        out.append((j, pid))
    return out


def build():
    B = Builder()
    nc = B.nc
    P = B.P
    xp = B.dram_in("xp", [512, D])
    xs = B.dram_in("xs", [512, D])
    cvec = B.dram_in("cvec", [16, 128])
    ada_w = B.dram_in("ada_wq", [DEPTH, D, NADA * D // 4])
    vecs = B.dram_in("vecs", [DEPTH, 128, 128])
    vecs2 = B.dram_in("vecs2", [DEPTH, 16, 128])
    w1 = B.dram_in("ffn_w1", [DEPTH, 2, D, DFF])
    w3 = B.dram_in("ffn_w3", [DEPTH, 2, D, DFF])
    w2 = B.dram_in("ffn_w2", [DEPTH, 2, DFF, D])
    w_in = B.dram_in("w_in", [DEPTH, D, INW])
    w_out = B.dram_in("w_out", [DEPTH, D, D])
    fw1 = B.dram_in("hy_filt_w1", [DEPTH, 33, 64])
    fw2 = B.dram_in("hy_filt_w2", [DEPTH, 64, 64])
    fw3 = B.dram_in("hy_filt_w3", [DEPTH, 64, 512])
    nab = B.dram_in("nab", [DEPTH, 6, 128, NPAT * 128])
    cna_k = B.dram_in("cna_k", [DEPTH, 256, 384])
    cna_v = B.dram_in("cna_v", [DEPTH, 256, 384])
    cgq_k = B.dram_in("cgq_k", [DEPTH, 256, 128])
    cgq_v = B.dram_in("cgq_v", [DEPTH, 256, 128])
    c_ident = B.dram_in("ident", [128, 128])
    c_ones = B.dram_in("onesc", [128, 128])
    c_bones = B.dram_in("bones", [128, 128])
    c_hones = B.dram_in("hones", [128, 256])
    c_tri = B.dram_in("tri", [128, 256])
    c_rmat = B.dram_in("rmat", [128, 128])
    c_ropeC = B.dram_in("ropeC", [128, 2048])
    c_ropeS = B.dram_in("ropeS", [128, 2048])
    FPL, FPP = 2176, 384
    c_cos = {2048: B.dram_in("cosL", [2048, FPL], BF16), 256: B.dram_in("cosP", [256, FPP], BF16)}
    c_sin = {2048: B.dram_in("sinL", [2048, FPL], BF16), 256: B.dram_in("sinP", [256, FPP], BF16)}
    c_icos = {2048: B.dram_in("icosL", [FPL, 2048], BF16), 256: B.dram_in("icosP", [FPP, 256], BF16)}
    c_isin = {2048: B.dram_in("isinL", [FPL, 2048], BF16), 256: B.dram_in("isinP", [FPP, 256], BF16)}
    c_feats = {2048: B.dram_in("featsL", [33, 2048]), 256: B.dram_in("featsP", [33, 256])}
    c_dec = {2048: B.dram_in("decL", [2, 2048, 256]), 256: B.dram_in("decP", [2, 256, 256])}
    yp = B.dram_out("yp", [512, D])
    ys = B.dram_out("ys", [512, D])
    o_nk = B.dram_out("nk", [2, DEPTH, 256, 384])
    o_nv = B.dram_out("nv", [2, DEPTH, 256, 384])
    o_gk = B.dram_out("gk", [2, DEPTH, 256, 128])
    o_gv = B.dram_out("gv", [2, DEPTH, 256, 128])
    if DBG.get("hal"):
        dbg_hal = B.dram_out("dbg_hal", [128, 48], F32)
        dbg_u = B.dram_out("dbg_u", [128, 2048], F32)
    if DBG.get("h"):
        dbg_h = B.dram_out("dbg_h", [128, 4096], BF16)
        dbg_x = B.dram_out("dbg_x", [128, 4096], F32)

    ident_f = B.sb([128, 128], F32, "ident_f")
    ident_b = B.sb([128, 128], BF16, "ident_b")
    ones_b = B.sb([128, 128], BF16, "ones_b")
    bones_b = B.sb([128, 128], BF16, "bones_b")
    hones_b = B.sb([128, 256], BF16, "hones_b")
    tri_b = B.sb([128, 256], BF16, "tri_b")
    rmat = B.sb([128, 128], F32, "rmat")
    P.dma("sp", ident_f[:], c_ident, writes=[ident_f])
    P.dma("sp", rmat[:], c_rmat, writes=[rmat])
    P.dma("pool", ident_b[:], c_ident, writes=[ident_b])
    P.dma("pool", ones_b[:], c_ones, writes=[ones_b])
    P.dma("pool", bones_b[:], c_bones, writes=[bones_b])
    P.dma("pool", hones_b[:], c_hones, writes=[hones_b])
    P.dma("pool", tri_b[:], c_tri, writes=[tri_b])
    epsc = B.sb([128, 4], F32, "epsc")
    P.op("dve", lambda e: e.memset(epsc[:, 0:1], EPS), [], [epsc])
    P.op("dve", lambda e: e.memset(epsc[:, 1:2], -math.pi), [], [epsc])

    banks = [B.ps("bank%d" % i) for i in range(8)]

    def bkb(i):
        return banks[i].h[:].bitcast(BF16)

    big = [B.sb([128, 2048], F32, "big%d" % i) for i in range(16)]
    SC = big[4:16]

    def bfv(t):
        return t.h[:].bitcast(BF16)

    xs_tr = [[Trk("xs%d_%d" % (k, h)) for h in range(2)] for k in range(8)]

    wsl = [B.sb([128, 2048], BF16, "wsl%d" % i) for i in range(4)]
    w2s = [B.sb([128, NFF, 128], BF16, "w2_%d" % i) for i in range(2)]
    tmpf = [B.sb([128, 512], F32, "tmpf%d" % i) for i in range(3)]
    sqb = [B.sb([128, 512], BF16, "sqb%d" % i) for i in range(2)]
    pts = [B.sb([128, 512], BF16, "pt%d" % i) for i in range(3)]
    rstd = B.sb([128, 512], F32, "rstd")
    nabs = [None, None]
    ropes = [B.sb([128, 512], F32, "ropeC"), B.sb([128, 512], F32, "ropeS")]
    ctxk = B.sb([128, 3, 256], BF16, "ctxk")
    ctxv = B.sb([128, 2, 6, 128], BF16, "ctxv")
    hid2 = B.sb([64, 2048], BF16, "hid2")
    hid1 = B.sb([64, 512], BF16, "hid1")
    featb = B.sb([33, 512], BF16, "featb")
    fws = B.sb([64, 64 + 64 + 512], BF16, "fws")
    ytn = B.sb([128, 2, 128], BF16, "ytn")
    kstg = rstd

    class _V:
        def __init__(self, ap, base):
            self.ap = ap
            self.trk = base.trk

        def __getitem__(self, k):
            return self.ap[k]
    argf = _V(tmpf[0][0:64, :], tmpf[0])
    argf2 = _V(tmpf[1][0:64, :], tmpf[1])
    argi = _V(tmpf[2][0:64, :].bitcast(mybir.dt.int32), tmpf[2])
    ctxs = _V(wsl[3][:, 0:1536].bitcast(F32).rearrange("p (a b) -> p a b", b=384), wsl[3])

    _rr = {"pt": 0, "tmp": 0, "sb": 0}

    def nxt(key, lst):
        _rr[key] += 1
        return lst[_rr[key] % len(lst)]

    cv_tok = B.sb([16, 128], F32, "cv_tok")
    P.dma("sp", cv_tok[:], cvec, writes=[cv_tok])
    cvT = B.sb([128, 16], BF16, "cvT")
    B.tr(banks[0][:, 0:16], cv_tok[:], ident_f[0:16, 0:16], [cv_tok, ident_f], [banks[0]])
    B.act(cvT[:], banks[0][:, 0:16], AF.Silu, [banks[0]], [cvT])

    _vtk = B.sb([128, 128], F32, "vec_tok")
    vec_tok = [_vtk for l in range(DEPTH)]
    vecT = [B.sb([128, 144], F32, "vecT%d" % l) for l in range(DEPTH)]
    v2_tok = [B.sb([16, 128], F32, "v2_tok%d" % l) for l in range(DEPTH)]
    for l in range(DEPTH):
        P.dma("sp", vec_tok[l][:], vecs[l], writes=[vec_tok[l]])
        P.dma("sp", v2_tok[l][:], vecs2[l], writes=[v2_tok[l]])
        bk = banks[1 + l]
        B.tr(bk[:, 0:128], vec_tok[l][:], ident_f[:], [vec_tok[l], ident_f], [bk])
        B.tr(bk[:, 128:144], v2_tok[l][:], ident_f[0:16, 0:16], [v2_tok[l], ident_f], [bk])
        B.cp(vecT[l][:], bk[:, 0:144], [bk], [vecT[l]])
    VB_ADA, VB_NG, VB_CW, VB_CB, VB_SK, VB_QG = 0, 72, 96, 114, 120, 122
    VB_F0, VB_F1, VB_B1, VB_B2, VB_SINK = 128, 129, 130, 131, 132
    dv = [B.sb([128, 12], F32, "dv%d" % l) for l in range(DEPTH)]
    for l in range(DEPTH):
        vt = vecT[l]
        B.ts(dv[l][:, 0:1], vt[:, VB_QG:VB_QG + 1], 0.125, None, ALU.mult, None, [vt], [dv[l]])
        B.cp(dv[l][:, 1:2], vt[:, VB_QG + 1:VB_QG + 2], [vt], [dv[l]])
        B.ts(dv[l][:, 2:3], vt[:, VB_QG + 2:VB_QG + 3], 0.125, None, ALU.mult, None, [vt], [dv[l]])
        B.cp(dv[l][:, 3:4], vt[:, VB_QG + 3:VB_QG + 4], [vt], [dv[l]])
        B.tt(dv[l][:, 4:5], vt[:, VB_F0:VB_F0 + 1], vt[:, VB_B1:VB_B1 + 1], ALU.mult, [vt], [dv[l]])
        B.tt(dv[l][:, 5:6], vt[:, VB_F1:VB_F1 + 1], vt[:, VB_B2:VB_B2 + 1], ALU.mult, [vt], [dv[l]])
        B.act(dv[l][:, 6:9], vt[:, VB_SINK:VB_SINK + 3], AF.Exp, [vt], [dv[l]])

    _mod = B.sb([128, 72, 2], F32, "mod")
    mod = [_mod for l in range(DEPTH)]
    der = [B.sb([128, 9, 8, 2], F32, "der%d" % l) for l in range(DEPTH)]

    def wslot_v(i, shape3):
        a, b = shape3
        return wsl[i][:, 0:a * b].rearrange("p (a b) -> p a b", a=a)

    ada_send = [T(nc.dram_tensor("ada_send%d" % l, [128, 36], F32), "ada_send") for l in range(DEPTH)]
    ada_gath = [T(nc.dram_tensor("ada_gath%d" % l, [4 * 128, 36], F32), "ada_gath") for l in range(DEPTH)]
    ada_loc = B.sb([128, 36], F32, "ada_loc")
    ada_all = B.sb([128, 4, 36], F32, "ada_all")

    def ada(l):
        bk = banks[3]
        first = True
        for g in range(9):
            slot = wsl[g % 2]
            sv = wslot_v(g % 2, (8, 256))
            P.dma("pool", sv, ada_w[l].rearrange("(kt p) f -> p kt f", p=128)[:, :, g * 256:(g + 1) * 256],
                  writes=[slot])
            for jj in range(2):
                c = g * 2 + jj
                for k in range(8):
                    B.mm(bk[:, c * 2:c * 2 + 2], sv[:, k, jj * 128:(jj + 1) * 128], cvT[:, k:16:8],
                         first, k == 7, [slot, cvT], [bk])
                    first = False
        B.cp(ada_loc[:, :], bk[:, 0:36], [bk], [ada_loc])
        P.dma("sp", ada_send[l].h.ap(), ada_loc[:, :], reads=[ada_loc], writes=[ada_send[l]], sem_on=ada_loc)
        P.collective(lambda e: e.collective_compute("AllGather", ALU.bypass, replica_groups=[[0, 1, 2, 3], [4, 5, 6, 7]],
                                                    ins=[ada_send[l].h.ap().opt()], outs=[ada_gath[l].h.ap().opt()]),
                     [ada_send[l]], [ada_gath[l]])
        P.dma("sp", ada_all[:, :, :], ada_gath[l].h.ap().rearrange("(r p) f -> p r f", p=128), reads=[ada_gath[l]], writes=[ada_all])
        bk = ada_all[:, :, :].rearrange("p r f -> p (r f)")
        for v in range(2):
            B.tt(mod[l][:, :, v], bk[:, v:144:2], vecT[l][:, VB_ADA:VB_ADA + 72], ALU.add,
                 [ada_all, vecT[l]], [mod[l]])
        m = mod[l]
        d_ = der[l]
        ng = vecT[l]
        for i in range(3):
            sh = m[:, (3 * i) * 8:(3 * i) * 8 + 8, :]
            sc = m[:, (3 * i + 1) * 8:(3 * i + 1) * 8 + 8, :]
            gt = m[:, (3 * i + 2) * 8:(3 * i + 2) * 8 + 8, :]
            for v in range(2):
                B.stt(d_[:, 3 * i, :, v], sc[:, :, v], 1.0, ng[:, VB_NG + i * 8:VB_NG + i * 8 + 8],
                      ALU.add, ALU.mult, [m, ng], [d_])
            B.cp(d_[:, 3 * i + 1, :, :], sh, [m], [d_])
            B.ts(d_[:, 3 * i + 2, :, :], gt, 0.5 if i != 1 else 1.0, None, ALU.mult, None, [m], [d_])

    def rsqrt_from(dst_ap, src_ap, reads, dst_t):
        B.act(dst_ap, src_ap, AF.Ln, reads + [epsc], [dst_t], bias=epsc[:, 0:1])
        B.act(dst_ap, dst_ap, AF.Exp, [dst_t], [dst_t], scale=-1.0 * 0.5)

    def norm_mod(xk, ntok, l, i, v, hout):
        bk = banks[4]
        for k in range(8):
            s = sqb[k % 2]
            B.act(s[:, :ntok], xk[k][0], AF.Square, [xk[k][1]], [s])
            B.mm(bk[:, :ntok], ones_b[:], s[:, :ntok], k == 0, k == 7, [ones_b, s], [bk])
        rsqrt_from(rstd[:, :ntok], bk[:, :ntok], [bk], rstd)
        for k in range(8):
            t = tmpf[k % 3]
            B.tt(t[:, :ntok], xk[k][0], rstd[:, :ntok], ALU.mult, [xk[k][1], rstd], [t])
            B.act(hout[k][0], t[:, :ntok], AF.Identity, [t, der[l]], [hout[k][1]],
                  bias=der[l][:, 3 * i + 1, k, v:v + 1], scale=der[l][:, 3 * i, k, v:v + 1])

    def ffn(l, fi, xk_blocks):
        i_norm = 0 if fi == 0 else 2
        nsb = len(xk_blocks)

        def hbuf(si, k):
            t = SC[si]
            return (bfv(t)[:, k * 512:(k + 1) * 512], t)

        def gbuf(si, j):
            n = j * 2 + si
            t = SC[2 + n // 8]
            return (bfv(t)[:, (n % 8) * 512:(n % 8 + 1) * 512], t)

        for si, (ntok, xk, v) in enumerate(xk_blocks):
            norm_mod(xk, ntok, l, i_norm, v, [hbuf(si, k) for k in range(8)])
        pb = 0
        for g in range(11):
            i1, i3 = (g % 2) * 2, (g % 2) * 2 + 1
            s1, s3 = wsl[i1], wsl[i3]
            v1, v3 = wslot_v(i1, (8, 256)), wslot_v(i3, (8, 256))
            src1 = w1[l, fi].rearrange("(kt p) f -> p kt f", p=128)[:, :, g * 256:(g + 1) * 256]
            src3 = w3[l, fi].rearrange("(kt p) f -> p kt f", p=128)[:, :, g * 256:(g + 1) * 256]
            P.dma("pool", v1, src1, writes=[s1])
            P.dma("pool", v3, src3, writes=[s3])
            for jj in range(2):
                j = 2 * g + jj
                for si, (ntok, xk, v) in enumerate(xk_blocks):
                    pa, pbk = banks[pb % 4], banks[(pb + 1) % 4]
                    pb += 2
                    for k in range(8):
                        hap, ht = hbuf(si, k)
                        B.mm(pa[:, :ntok], v1[:, k, jj * 128:(jj + 1) * 128], hap[:, :ntok], k == 0, k == 7, [s1, ht], [pa])
                    for k in range(8):
                        hap, ht = hbuf(si, k)
                        B.mm(pbk[:, :ntok], v3[:, k, jj * 128:(jj + 1) * 128], hap[:, :ntok], k == 0, k == 7, [s3, ht], [pbk])
                    t = nxt("tmp", tmpf)
                    B.act(t[:, :ntok], pa[:, :ntok], AF.Silu, [pa], [t])
                    gap, gt = gbuf(si, j)
                    B.tt(gap[:, :ntok], t[:, :ntok], pbk[:, :ntok], ALU.mult, [t, pbk], [gt])
        for dch in range(8):
            s2 = w2s[dch % 2]
            src2 = w2[l, fi].rearrange("(j p) d -> p j d", p=128)[:, :, dch * 128:(dch + 1) * 128]
            P.dma("pool", s2[:, 0:11, :], src2[:, 0:11, :], writes=[s2])
            P.dma("pool", s2[:, 11:22, :], src2[:, 11:22, :], writes=[s2])
            for si, (ntok, xk, v) in enumerate(xk_blocks):
                po = banks[4 + (dch * nsb + si) % 4]
                for j in range(NFF):
                    gap, gt = gbuf(si, j)
                    B.mm(po[:, :ntok], s2[:, j, :], gap[:, :ntok], j == 0, j == NFF - 1, [s2, gt], [po])
                xap, xt = xk[dch]
                B.stt(xap, po[:, :ntok], der[l][:, 3 * i_norm + 2, dch, v:v + 1], xap, ALU.mult, ALU.add,
                      [po, der[l], xt], [xt])

    def load_x(src, ntiles, dst4, extra_w=None):
        for t4 in range(0, ntiles, 4):
            stg = SC[(t4 // 4) % 2 * 2:(t4 // 4) % 2 * 2 + 2]
            for i in range(4):
                st = stg[i // 2]
                P.dma("sp", st[:, (i % 2) * 1024:(i % 2 + 1) * 1024], src[(t4 + i) * 128:(t4 + i + 1) * 128, :], writes=[st])
            for k in range(8):
                bk = banks[k % 4]
                for i in range(4):
                    st = stg[i // 2]
                    B.tr(bk[:, i * 128:(i + 1) * 128], st[:, (i % 2) * 1024 + k * 128:(i % 2) * 1024 + (k + 1) * 128],
                         ident_f[:], [st, ident_f], [bk])
                dap, dt_ = dst4(k, t4)
                B.cp(dap, bk[:, :], [bk], dt_, eng=("dve" if k % 2 else "act"))

    def store_x(dstT, ntiles, srcf):
        for tt in range(ntiles):
            stg = SC[tt % 4]
            for half in range(2):
                bk = banks[(tt * 2 + half) % 8]
                for kk in range(4):
                    k = half * 4 + kk
                    sap, st = srcf(k, tt)
                    B.tr(bk[:, kk * 128:(kk + 1) * 128], sap, ident_f[:], [st, ident_f], [bk])
                B.cp(stg[:, half * 512:(half + 1) * 512], bk[:, :], [bk], [stg], eng=("dve" if half else "act"))
            P.store("sp", dstT.h.ap()[tt * 128:(tt + 1) * 128, :], stg[:, 0:1024], stg)

    def wslot_half(ci, sbase=0):
        hs = (ci + sbase) % 8
        t = wsl[hs // 2]
        return t, t[:, (hs % 2) * 1024:(hs % 2 + 1) * 1024].rearrange("p (a b) -> p a b", a=8)

    def wload(l, chunk_specs, ci, sbase=0):
        wi = w_in[l].rearrange("(kt p) f -> p kt f", p=128)
        slot, sv = wslot_half(ci, sbase)
        o = 0
        for (c0, n) in chunk_specs[ci][0]:
            P.dma("pool", sv[:, :, o:o + n], wi[:, :, c0:c0 + n], writes=[slot])
            o += n

    def project_prefetch(l, chunk_specs, n, sbase=0):
        for cj in range(min(n, len(chunk_specs), 8)):
            wload(l, chunk_specs, cj, sbase)
        return min(n, len(chunk_specs), 8)

    def project(l, v, sbs, chunk_specs, tok_major_specs=(), hT=None, skip_norm=False, prefetched=0, sbase=0):
        hT = SC[0] if hT is None else hT
        for si, (ntok, xk) in enumerate(sbs):
            hv = [(bfv(hT)[:, k * 512:k * 512 + ntok], hT) for k in range(8)]
            if not skip_norm:
                norm_mod(xk, ntok, l, 1, v, hv)
            if DBG.get("h") and not DBG.get("h_done"):
                DBG["h_done"] = True
                P.store("sp", dbg_h.h.ap(), bfv(hT)[:, 0:4096], hT)
                for k in range(8):
                    P.store("sp", dbg_x.h.ap()[:, k * 512:(k + 1) * 512], xk[k][0], xk[k][1])
            wi = w_in[l].rearrange("(kt p) f -> p kt f", p=128)
            pend = []
            nch = len(chunk_specs)
            for cj in range(prefetched if si == 0 else 0, min(8, nch)):
                wload(l, chunk_specs, cj, sbase)
            for ci, (cols, handler) in enumerate(chunk_specs):
                slot, sv = wslot_half(ci, sbase)
                bk = banks[ci % 4]
                for k in range(8):
                    B.mm(bk[:, :ntok], sv[:, k, :], hv[k][0], k == 0, k == 7, [slot, hT], [bk])
                for ent in list(pend):
                    ent.pop(0)()
                    if not ent:
                        pend.remove(ent)
                st = handler(si, ntok, bk)
                if st:
                    pend.append(list(st))
                if ci + 8 < nch:
                    wload(l, chunk_specs, ci + 8, sbase)
            while pend:
                for ent in list(pend):
                    ent.pop(0)()
                    if not ent:
                        pend.remove(ent)
            for ti, (cols, ncol, handler) in enumerate(tok_major_specs):
                assert ncol <= 512
                s_a, s_b = wsl[(ti * 2) % 4], wsl[(ti * 2 + 1) % 4]
                va = wslot_v((ti * 2) % 4, (8, 256))
                vb = wslot_v((ti * 2 + 1) % 4, (8, 256))
                o = 0
                for (c0, n) in cols:
                    while n > 0:
                        if o < 256:
                            nn = min(n, 256 - o)
                            P.dma("pool", va[:, :, o:o + nn], wi[:, :, c0:c0 + nn], writes=[s_a])
                        else:
                            nn = min(n, 512 - o)
                            P.dma("pool", vb[:, :, o - 256:o - 256 + nn], wi[:, :, c0:c0 + nn], writes=[s_b])
                        o += nn
                        c0 += nn
                        n -= nn
                for tt in range(ntok // 128):
                    bk = banks[4 + tt % 2]
                    na_ = min(ncol, 256)
                    for k in range(8):
                        B.mm(bk[:, 0:na_], hv[k][0][:, tt * 128:(tt + 1) * 128], va[:, k, 0:na_], k == 0, k == 7, [s_a, hT], [bk])
                    if ncol > 256:
                        for k in range(8):
                            B.mm(bk[:, 256:ncol], hv[k][0][:, tt * 128:(tt + 1) * 128], vb[:, k, 0:ncol - 256], False, k == 7, [s_b, hT], [bk])
                    handler(si, tt, bk)

    def wout_load(l):
        for rc in range(8):
            P.dma("pool", wsl[rc // 2][:, (rc % 2) * 1024:(rc % 2 + 1) * 1024], w_out[l][rc * 128:(rc + 1) * 128, :], writes=[wsl[rc // 2]])

    def wout_all(l, v, mixes, sbs, loaded=False):
        if not loaded:
            wout_load(l)
        slots = [(wsl[rc // 2][:, (rc % 2) * 1024:(rc % 2 + 1) * 1024], wsl[rc // 2]) for rc in range(8)]
        n = 0
        for si, (ntok, xk) in enumerate(sbs):
            for dch in range(8):
                bk = banks[4 + n % 4]
                n += 1
                for i in range(8):
                    map_, mt = mixes[i](si)
                    B.mm(bk[:, :ntok], slots[i][0][:, dch * 128:(dch + 1) * 128], map_, i == 0, i == 7,
                         [slots[i][1], mt], [bk])
                xap, xt = xk[dch]
                B.stt(xap, bk[:, :ntok], der[l][:, 5, dch, v:v + 1], xap, ALU.mult, ALU.add, [bk, der[l], xt], [xt])

    def filter_load(l, L):
        P.dma("pool", fws[0:33, 0:64], fw1[l], writes=[fws])
        P.dma("pool", fws[:, 64:128], fw2[l], writes=[fws])
        P.dma("pool", fws[:, 128:640], fw3[l], writes=[fws])
        P.dma("pool", bfv(SC[7])[0:33, 2048:2048 + L], c_feats[L], writes=[SC[7]])

    def filter_mlp(l, L, loaded=False):
        vt = vecT[l]
        if not loaded:
            filter_load(l, L)

        def sin_stages(items, fcol, fbcol):
            tv = []
            for (ps_ap, n, out_ap, out_t, rd, tt_) in items:
                tv.append((tt_[0:64, 0:n], tt_[0:64, 512:512 + n], tt_[0:64, 1024:1024 + n].bitcast(mybir.dt.int32)))
            for step in range(6):
                for idx, (ps_ap, n, out_ap, out_t, rd, tt_) in enumerate(items):
                    a_, a2_, ai_ = tv[idx]
                    if step == 0:
                        B.ts(a_, ps_ap, vt[0:64, fcol:fcol + 1], dv[l][0:64, fbcol:fbcol + 1], ALU.mult, ALU.add,
                             rd + [vt, dv[l]], [tt_])
                    elif step == 1:
                        B.ts(ai_, a_, 1.0 / (2 * math.pi), None, ALU.mult, None, [tt_], [tt_])
                    elif step == 2:
                        B.cp(a2_, ai_, [tt_], [tt_])
                    elif step == 3:
                        B.stt(a_, a2_, -2 * math.pi, a_, ALU.mult, ALU.add, [tt_], [tt_])
                    elif step == 4:
                        B.ts(a_, a_, 3.1415925, -3.1415925, ALU.min, ALU.max, [tt_], [tt_])
                    else:
                        B.act(out_ap, a_, AF.Sin, [tt_], [out_t])

        chunks = [(c0, min(512, L - c0)) for c0 in range(0, L, 512)]
        h1 = SC[7]
        fb = bfv(h1)[0:33, 2048:2048 + L]
        h1v = bfv(h1)[0:64, 0:L]
        for ci, (c0, n) in enumerate(chunks):
            B.mm(banks[ci % 4][0:64, :n], fws[0:33, 0:64], fb[:, c0:c0 + n], True, True, [fws, h1], [banks[ci % 4]])
        sin_stages([(banks[ci % 4][0:64, :n], n, h1v[:, c0:c0 + n], h1, [banks[ci % 4]], SC[3 + ci]) for ci, (c0, n) in enumerate(chunks)],
                   VB_F0, 4)
        for ci, (c0, n) in enumerate(chunks):
            B.mm(banks[4 + ci % 4][0:64, :n], fws[:, 64:128], h1v[:, c0:c0 + n], True, True, [fws, h1], [banks[4 + ci % 4]])
        sin_stages([(banks[4 + ci % 4][0:64, :n], n, hid2[:, c0:c0 + n], hid2, [banks[4 + ci % 4]], SC[3 + ci]) for ci, (c0, n) in enumerate(chunks)],
                   VB_F1, 5)

    def hyena(l, v, cc, L, seqs, sbs, mix_t, prefetched=0, sbase=0):
        ntot = L * len(seqs)
        ntt = L // 128
        vt = vecT[l]
        U = [SC[1], SC[2], SC[3]]
        x0c, zf, vc = SC[4], SC[5], SC[6]
        off = [0]
        for (ntok, _) in sbs:
            off.append(off[-1] + ntok)

        def mk_handler(ui):
            def h(si, ntok, bk):
                B.cp(U[ui][:, off[si]:off[si] + ntok], bk[:, :ntok], [bk], [U[ui]], eng=("act" if ui % 2 else "dve"))
            return h
        specs = [([((2 * ui + cc) * 128, 128)], mk_handler(ui)) for ui in range(3)]
        project(l, v, sbs, specs, skip_norm=True, prefetched=prefetched, sbase=sbase)
        outs = [x0c, zf, vc]
        for ui in range(3):
            ch = 2 * ui + cc
            wcol = lambda tap: vt[:, VB_CW + tap * 6 + ch:VB_CW + tap * 6 + ch + 1]
            bcol = vt[:, VB_CB + ch:VB_CB + ch + 1]
            o_ = outs[ui]
            u_ = U[ui]
            eng = "dve"
            for s0 in seqs:
                B.act(o_[:, s0:s0 + L], u_[:, s0:s0 + L], AF.Identity, [u_, vt], [o_], bias=bcol, scale=wcol(1))
                B.stt(o_[:, s0 + 1:s0 + L], u_[:, s0:s0 + L - 1], wcol(0), o_[:, s0 + 1:s0 + L], ALU.mult, ALU.add,
                      [u_, vt, o_], [o_], eng=eng)
                B.stt(o_[:, s0:s0 + L - 1], u_[:, s0 + 1:s0 + L], wcol(2), o_[:, s0:s0 + L - 1], ALU.mult, ALU.add,
                      [u_, vt, o_], [o_], eng=eng)
        B.tt(zf[:, 0:ntot], zf[:, 0:ntot], vc[:, 0:ntot], ALU.mult, [zf, vc], [zf])
        mixv = mix_t[:, :, :].rearrange("p a b -> p (a b)")[:, 0:ntot]

        def final(c0, tn, bk):
            t = nxt("tmp", tmpf)
            B.stt(t[:, :tn], zf[:, c0:c0 + tn], vt[:, VB_SK + cc:VB_SK + cc + 1], bk[:, :tn], ALU.mult, ALU.add,
                  [zf, vt, bk], [t])
            B.tt(mixv[:, c0:c0 + tn], t[:, :tn], x0c[:, c0:c0 + tn], ALU.mult, [t, x0c], [mix_t])
        hy_core(l, cc, L, seqs, ntot, zf, {'zb': SC[1], 'sd': SC[2], 'decF': SC[3], 'decB': SC[6], 'yt': SC[7]},
                c_icos[L], c_isin[L], L, final)

    def hy_core(l, cc, L, seqs, ntot, zf, TL, inv_icos, inv_isin, Linv, final):
        ntt = L // 128
        vt = vecT[l]
        zb_t = TL['zb']
        zb = bfv(zb_t)[:, 0:ntot]
        zT = bfv(zb_t)[:, 2048:2048 + ntot].rearrange("p (a b) -> p a b", b=128)
        B.cp(zb, zf[:, 0:ntot], [zf], [zb_t], eng="act")
        for t4 in range(0, ntot // 128, 4):
            nb = min(4, ntot // 128 - t4)
            bi = 2 + (t4 // 4) % 2
            for i in range(nb):
                B.tr(bkb(bi)[:, i * 128:(i + 1) * 128], zb[:, (t4 + i) * 128:(t4 + i + 1) * 128], ident_b[:], [zb_t, ident_b], [banks[bi]])
            B.cp(zT[:, t4:t4 + nb, :], bkb(bi)[:, 0:nb * 128].rearrange("p (a b) -> p a b", b=128), [banks[bi]], [zb_t])
        sd_t = TL['sd']
        Sl = bfv(sd_t)[:, 0:L].rearrange("p (a b) -> p a b", b=128)
        Dl = bfv(sd_t)[:, 2048:2048 + L].rearrange("p (a b) -> p a b", b=128)
        decF, decB = TL['decF'], TL['decB']
        dF = decF[:, 0:ntt * 128].rearrange("p (a b) -> p a b", b=128)
        dB = decB[:, 0:ntt * 128].rearrange("p (a b) -> p a b", b=128)
        P.dma("sp", dF, c_dec[L][0].rearrange("(tt p) c -> p tt c", p=128)[:, :, cc * 128:(cc + 1) * 128], writes=[decF])
        P.dma("sp", dB, c_dec[L][1].rearrange("(tt p) c -> p tt c", p=128)[:, :, cc * 128:(cc + 1) * 128], writes=[decB])
        dFf = decF[:, 0:ntt * 128]
        dBf = decB[:, 0:ntt * 128]
        Slf = bfv(sd_t)[:, 0:L]
        Dlf = bfv(sd_t)[:, 2048:2048 + L]
        for t2 in range(0, ntt, 2):
            bk = banks[(t2 // 2) % 2]
            for j in range(2):
                tt = t2 + j
                B.mm(bk[:, j * 128:(j + 1) * 128], hid2[:, tt * 128:(tt + 1) * 128], fws[:, 128 + cc * 128:128 + (cc + 1) * 128],
                     j == 0, True, [hid2, fws], [bk])
                B.mm(bk[:, 256 + j * 128:256 + (j + 1) * 128], hid2[:, tt * 128:(tt + 1) * 128],
                     fws[:, 128 + 256 + cc * 128:128 + 256 + (cc + 1) * 128], False, True, [hid2, fws], [bk])
            ta = nxt("tmp", tmpf)
            B.tt(ta[:, 0:256], bk[:, 0:256], dFf[:, t2 * 128:(t2 + 2) * 128], ALU.mult, [bk, decF], [ta])
            B.tt(ta[:, 256:512], bk[:, 256:512], dBf[:, t2 * 128:(t2 + 2) * 128], ALU.mult, [bk, decB], [ta])
            B.tt(Slf[:, t2 * 128:(t2 + 2) * 128], ta[:, 0:256], ta[:, 256:512], ALU.add, [ta], [sd_t])
            B.tt(Dlf[:, t2 * 128:(t2 + 2) * 128], ta[:, 0:256], ta[:, 256:512], ALU.subtract, [ta], [sd_t])
        FP = FPL if L == 2048 else FPP
        nft = FP // 128
        yt_t = TL['yt']
        YT = bfv(yt_t)[:, 0:4096].rearrange("p (s f r c) -> p s f r c", s=len(seqs), r=2, c=128)
        nft_main = YT.shape[2]
        cosr = c_cos[L].rearrange("(tt p) f -> p tt f", p=128)
        sinr = c_sin[L].rearrange("(tt p) f -> p tt f", p=128)
        fchunks = [(f0, min(512, FP - f0)) for f0 in range(0, FP, 512)]
        _pend = []
        for (f0, fn) in fchunks:
            for g4 in range(0, ntt, 4):
                ng = min(4, ntt - g4)
                sc_i, ss_i = (g4 // 4) % 2 * 2, (g4 // 4) % 2 * 2 + 1
                cv_ = wslot_v(sc_i, (4, 512))
                sv_ = wslot_v(ss_i, (4, 512))
                P.dma("sp", cv_[:, 0:ng, 0:fn], cosr[:, g4:g4 + ng, f0:f0 + fn], writes=[wsl[sc_i]])
                P.dma("sp", sv_[:, 0:ng, 0:fn], sinr[:, g4:g4 + ng, f0:f0 + fn], writes=[wsl[ss_i]])
                for i in range(ng):
                    tt = g4 + i
                    st, sp_ = (tt == 0), (tt == ntt - 1)
                    B.mm(banks[0][:, :fn], Sl[:, tt, :], cv_[:, i, 0:fn], st, sp_, [sd_t, wsl[sc_i]], [banks[0]])
                    B.mm(banks[1][:, :fn], Dl[:, tt, :], sv_[:, i, 0:fn], st, sp_, [sd_t, wsl[ss_i]], [banks[1]])
                    for s, s0 in enumerate(seqs):
                        tg = s0 // 128 + tt
                        B.mm(banks[2 + 2 * s][:, :fn], zT[:, tg, :], cv_[:, i, 0:fn], st, sp_, [zb_t, wsl[sc_i]], [banks[2 + 2 * s]])
                        B.mm(banks[3 + 2 * s][:, :fn], zT[:, tg, :], sv_[:, i, 0:fn], st, sp_, [zb_t, wsl[ss_i]], [banks[3 + 2 * s]])
            kc_s, ks_s = tmpf[0], tmpf[1]
            pipelined = (len(seqs) == 1 and 'zst' in TL)

            def transposes(f0, fn, s, yre, yim):
                for ri, ysrc in enumerate((yre, yim)):
                    bi = 6 + ri
                    nb = fn // 128
                    for i in range(nb):
                        B.tr(bkb(bi)[:, i * 128:(i + 1) * 128], ysrc[:, i * 128:(i + 1) * 128], ident_b[:], [ysrc, ident_b], [banks[bi]])
                    ft0 = f0 // 128
                    if ft0 + nb <= nft_main:
                        B.cp(YT[:, s, ft0:ft0 + nb, ri, :], bkb(bi)[:, 0:nb * 128].rearrange("p (a b) -> p a b", b=128),
                             [banks[bi]], [yt_t], eng=("act" if ri else "dve"))
                    else:
                        assert nb == 1 and len(seqs) == 1
                        B.cp(ytn[:, ri, :], bkb(bi)[:, 0:128], [banks[bi]], [ytn], eng=("act" if ri else "dve"))

            def products(zc_ap, zc_t, zs_ap, zs_t, fn, yre, yim):
                t1, t2 = tmpf[2], rstd
                B.tt(t1[:, :fn], zc_ap, kc_s[:, :fn], ALU.mult, [zc_t, kc_s], [t1])
                B.tt(t2[:, :fn], zs_ap, ks_s[:, :fn], ALU.mult, [zs_t, ks_s], [t2])
                B.tt(yre[:, :fn], t1[:, :fn], t2[:, :fn], ALU.subtract, [t1, t2], [yre])
                B.tt(t1[:, :fn], zc_ap, ks_s[:, :fn], ALU.mult, [zc_t, ks_s], [t1])
                B.tt(t2[:, :fn], zs_ap, kc_s[:, :fn], ALU.mult, [zs_t, kc_s], [t2])
                B.tt(yim[:, :fn], t1[:, :fn], t2[:, :fn], ALU.add, [t1, t2], [yim])
            yre, yim = pts[0], pts[1]
            if pipelined:
                if _pend:
                    transposes(*_pend.pop())
                zst = TL['zst']
                B.cp(kc_s[:, :fn], banks[0][:, :fn], [banks[0]], [kc_s], eng="act")
                B.cp(ks_s[:, :fn], banks[1][:, :fn], [banks[1]], [ks_s], eng="act")
                B.cp(zst[:, 0:fn], banks[2][:, :fn], [banks[2]], [zst], eng="dve")
                B.cp(zst[:, 512:512 + fn], banks[3][:, :fn], [banks[3]], [zst], eng="dve")
                products(zst[:, 0:fn], zst, zst[:, 512:512 + fn], zst, fn, yre, yim)
                _pend.append((f0, fn, 0, yre, yim))
            else:
                B.cp(kc_s[:, :fn], banks[0][:, :fn], [banks[0]], [kc_s], eng="act")
                B.cp(ks_s[:, :fn], banks[1][:, :fn], [banks[1]], [ks_s], eng="act")
                for s, s0 in enumerate(seqs):
                    zc, zs = banks[2 + 2 * s], banks[3 + 2 * s]
                    products(zc[:, :fn], zc, zs[:, :fn], zs, fn, yre, yim)
                    transposes(f0, fn, s, yre, yim)
        if _pend:
            transposes(*_pend.pop())
        icr = inv_icos.rearrange("(ft p) t -> p ft t", p=128)
        isr = inv_isin.rearrange("(ft p) t -> p ft t", p=128)
        tn = min(512, Linv)
        for tc0 in range(0, Linv, tn):
            for s, s0 in enumerate(seqs):
                bk = banks[(tc0 // tn + s) % 2]
                nmm = 0
                for g4 in range(0, nft, 4):
                    ng = min(4, nft - g4)
                    sc_i, ss_i = (g4 // 4) % 2 * 2, (g4 // 4) % 2 * 2 + 1
                    cv_ = wslot_v(sc_i, (4, 512))
                    sv_ = wslot_v(ss_i, (4, 512))
                    P.dma("sp", cv_[:, 0:ng, 0:tn], icr[:, g4:g4 + ng, tc0:tc0 + tn], writes=[wsl[sc_i]])
                    P.dma("sp", sv_[:, 0:ng, 0:tn], isr[:, g4:g4 + ng, tc0:tc0 + tn], writes=[wsl[ss_i]])
                    for i in range(ng):
                        ft = g4 + i
                        if ft < nft_main:
                            lre, lim, lt = YT[:, s, ft, 0, :], YT[:, s, ft, 1, :], yt_t
                        else:
                            lre, lim, lt = ytn[:, 0, :], ytn[:, 1, :], ytn
                        B.mm(bk[:, :tn], lre, cv_[:, i, 0:tn], nmm == 0, False, [lt, wsl[sc_i]], [bk])
                        nmm += 1
                        B.mm(bk[:, :tn], lim, sv_[:, i, 0:tn], False, nmm == 2 * nft - 1, [lt, wsl[ss_i]], [bk])
                        nmm += 1
                final(s0 + tc0, tn, bk)

    def headnorm(bk, ntok, gcol_ap, l, out_ap, out_t, extra_reads=()):
        s = nxt("sb", sqb)
        B.act(s[:, :ntok], bk[:, :ntok], AF.Square, [bk], [s])
        nbi = 4 + _rr["sb"] % 2

        def fin():
            nb = banks[nbi]
            B.mm(nb[:, :ntok], bones_b[:], s[:, :ntok], True, True, [bones_b, s], [nb])
            r = nxt("tmp", tmpf)
            rsqrt_from(r[:, :ntok], nb[:, :ntok], [nb], r)
            B.tt(r[:, :ntok], bk[:, :ntok], r[:, :ntok], ALU.mult, [bk, r], [r])
            B.act(out_ap, r[:, :ntok], AF.Identity, [r, dv[l]] + list(extra_reads), [out_t], scale=gcol_ap)
        return fin

    def run_attention(groups, nq, sink_ap, sink_t, mix_ap, mix_t):
        acc, den = banks[6], banks[7]
        st = {"first": True, "sb": 0}

        def emit_S(grp):
            st["sb"] += 1
            Sb = banks[st["sb"] % 4]
            col = 0
            for i, (qc0, n, kap, qap, vap, hap, bias, bt, rdt) in enumerate(grp):
                B.mm(Sb[:, col:col + n], kap, qap, i == 0, bias is None, rdt, [Sb])
                if bias is not None:
                    B.mm(Sb[:, col:col + n], ident_b[:], bias, False, True, [ident_b, bt], [Sb])
                col += n
            pt = nxt("pt", pts)
            B.act(pt[:, :col], Sb[:, :col], AF.Exp, [Sb], [pt])
            return pt

        def emit_PV(grp, pt):
            col = 0
            for (qc0, n, kap, qap, vap, hap, bias, bt, rdt) in grp:
                B.mm(acc[:, qc0:qc0 + n], vap, pt[:, col:col + n], st["first"], False, rdt + [pt], [acc])
                B.mm(den[:, qc0:qc0 + n], hap, pt[:, col:col + n], st["first"], False, [hones_b, pt], [den])
                st["first"] = False
                col += n
        pending = None
        for grp in groups:
            pt = emit_S(grp)
            if pending is not None:
                emit_PV(*pending)
            pending = (grp, pt)
        emit_PV(*pending)
        r = nxt("tmp", tmpf)
        if sink_ap is not None:
            B.act(r[:, :nq], den[:, :nq], AF.Ln, [den, sink_t], [r], bias=sink_ap)
        else:
            B.act(r[:, :nq], den[:, :nq], AF.Ln, [den], [r])
        B.act(r[:, :nq], r[:, :nq], AF.Exp, [r], [r], scale=-1.0)
        B.tt(mix_ap, acc[:, :nq], r[:, :nq], ALU.mult, [acc, r], [mix_t])

    def attn_qblock(nq, q_ap, k_ap, v_ap, ctx, pairs, hone, sink_ap, sink_t, mix_ap, mix_t, rd_tiles):
        groups = []
        for hh in range(2):
            for (kc, vc_) in ctx:
                groups.append([(0, nq, kc(hh), q_ap(hh, 0, nq), vc_(hh), hone(hh), None, None, rd_tiles)])
            cur, tot = [], 0
            for (qc0, n, kt, bias, bt) in pairs:
                if tot + n > 512:
                    groups.append(cur)
                    cur, tot = [], 0
                cur.append((qc0, n, k_ap(hh, kt), q_ap(hh, qc0, n), v_ap(hh, kt), hone(hh), bias, bt, rd_tiles))
                tot += n
            if cur:
                groups.append(cur)
        run_attention(groups, nq, sink_ap, sink_t, mix_ap, mix_t)

    def hone(hh):
        return hones_b[:, hh * 128:(hh + 1) * 128]

    def load_ctx(l, kind):
        if kind == "na":
            ksrc, vsrc, ncol = cna_k[l], cna_v[l], 384
        else:
            ksrc, vsrc, ncol = cgq_k[l], cgq_v[l], 128
        P.op("dve", lambda e: e.memset(ctxv[:], 0.0), [], [ctxv])
        if kind == "na":
            P.dma("sp", ctxs[:, :, 0:384], ksrc.rearrange("(tt p) c -> p tt c", p=128), writes=[ctxs])
            for tt in range(2):
                for c in range(3):
                    bk = banks[(tt * 3 + c) % 4]
                    B.tr(bk[:, 0:128], ctxs[:, tt, c * 128:(c + 1) * 128], ident_f[:], [ctxs, ident_f], [bk])
                    B.cp(ctxk[:, c, tt * 128:(tt + 1) * 128], bk[:, 0:128], [bk], [ctxk])
            for h in range(6):
                P.dma("pool", ctxv[:, :, h, (h % 2) * 64:(h % 2) * 64 + 64],
                      vsrc.rearrange("(tt p) c -> p tt c", p=128)[:, :, h * 64:(h + 1) * 64], writes=[ctxv])
        else:
            for kv in range(2):
                for par in range(2):
                    P.dma("sp", ctxs[:, :, kv * 128 + par * 64:kv * 128 + par * 64 + 64],
                          ksrc.rearrange("(tt p) c -> p tt c", p=128)[:, :, kv * 64:(kv + 1) * 64], writes=[ctxs])
            for tt in range(2):
                for kv in range(2):
                    bk = banks[(tt * 2 + kv) % 4]
                    B.tr(bk[:, 0:128], ctxs[:, tt, kv * 128:(kv + 1) * 128], ident_f[:], [ctxs, ident_f], [bk])
                    B.cp(ctxk[:, kv, tt * 128:(tt + 1) * 128], bk[:, 0:128], [bk], [ctxk])
            for kv in range(2):
                for par in range(2):
                    P.dma("pool", ctxv[:, :, kv * 2 + par, par * 64:par * 64 + 64],
                          vsrc.rearrange("(tt p) c -> p tt c", p=128)[:, :, kv * 64:(kv + 1) * 64], writes=[ctxv])

    def na_attention(l, v, is_sample, sbs, ntot, kout, vout, hook=None):
        qk = [SC[1], SC[2], SC[3]]

        def qk_ap(ci):
            return bfv(qk[ci // 2])[:, (ci % 2) * 2048:(ci % 2) * 2048 + ntot], qk[ci // 2]
        vp = [SC[4], SC[5], SC[6]]

        def vpad(tt, h):
            n = h * 16 + tt
            return bfv(vp[n // 32])[:, (n % 32) * 128:(n % 32 + 1) * 128], vp[n // 32]
        for t in vp:
            P.op("dve", lambda e, t=t: e.memset(t[:, :], 0.0), [], [t])
        off = [0]
        for (ntok, _) in sbs:
            off.append(off[-1] + ntok)

        def mk_q(ci):
            def h(si, ntok, bk):
                ap, t = qk_ap(ci)
                return [headnorm(bk, ntok, dv[l][:, 0:1], l, ap[:, off[si]:off[si] + ntok], t)]
            return h

        def mk_k(ci):
            def h(si, ntok, bk):
                ap, t = qk_ap(3 + ci)
                if kout is None:
                    return [headnorm(bk, ntok, dv[l][:, 1:2], l, ap[:, off[si]:off[si] + ntok], t)]
                fin = headnorm(bk, ntok, dv[l][:, 1:2], l, kstg[:, :ntok], kstg)

                def st0():
                    fin()
                    B.cp(ap[:, off[si]:off[si] + ntok], kstg[:, :ntok], [kstg], [t], eng="dve")
                return [st0, lambda: kout(si, ntok, ci, kstg)]
            return h

        def vh(si, tt, bk):
            tg = off[si] // 128 + tt
            for h in range(6):
                ap, t = vpad(tg, h)
                par = h % 2
                B.cp(ap[:, par * 64:par * 64 + 64], bk[:, h * 64:(h + 1) * 64], [bk], [t], eng=("act" if h % 2 else "dve"))
            if vout is not None:
                vout(si, tt, bk, 384)
        specs = [([((6 + c) * 128, 128)], mk_q(c)) for c in range(3)] + [([((9 + c) * 128, 128)], mk_k(c)) for c in range(3)]
        project(l, v, sbs, specs, [([(12 * 128, 384)], 384, vh)], skip_norm=True)
        if hook is not None:
            hook()
        assert ntot <= 1024
        mixA = SC[7]

        def mix_ap(c):
            return bfv(mixA)[:, c * 1024:c * 1024 + ntot], mixA
        if is_sample:
            load_ctx(l, "na")
        for h2 in range(3):
            q_t = qk_ap(h2)
            k_t = qk_ap(3 + h2)
            rd = [q_t[1], k_t[1]] + vp + [ctxk, ctxv]
            map_, mt = mix_ap(h2)
            if not is_sample:
                for s in range(ntot // 256):
                    pairs = [(0, 256, s * 2 + j, None, None) for j in range(2)]
                    attn_qblock(256,
                                lambda hh, c0, n, s=s: q_t[0][hh * 64:(hh + 1) * 64, s * 256 + c0:s * 256 + c0 + n],
                                lambda hh, kt: k_t[0][hh * 64:(hh + 1) * 64, kt * 128:(kt + 1) * 128],
                                lambda hh, kt: vpad(kt, 2 * h2 + hh)[0],
                                [], pairs, hone, None, None, map_[:, s * 256:(s + 1) * 256], mt, rd)
            else:
                na_sample_chunk(l, h2, q_t, k_t, vpad, map_, mt, rd)
        return [lambda si, c=c: (mix_ap(c)[0][:, off[si]:off[si + 1]], mix_ap(c)[1]) for c in range(3)]

    def na_sample_chunk(l, h2, q_t, k_t, vpad, map_, mt, rd):
        for hh in range(2):
            P.dma("pool", nabs[hh][:, :], nab[l, 2 * h2 + hh], writes=[nabs[hh]])
        for QB in range(4):
            pairs_by_head = []
            for hh in range(2):
                prs = []
                for mi in range(4):
                    m = QB * 4 + mi
                    for (j, pid) in na_pairs_for_m(m):
                        prs.append((mi * 128, 128, j, pid))
                pairs_by_head.append(prs)
            attn_qblock_na(l, h2, QB, pairs_by_head, q_t, k_t, vpad, map_, mt, rd)

    def attn_qblock_na(l, h2, QB, pairs_by_head, q_t, k_t, vpad, map_, mt, rd):
        acc, den = banks[6], banks[7]
        first = True
        nq = 512
        q0 = QB * 512
        sbi = 0
        for hh in range(2):
            h = 2 * h2 + hh
            nb_t = nabs[hh]
            nbv = nb_t[:, :].rearrange("p (a b) -> p a b", b=128)
            qv = q_t[0][hh * 64:(hh + 1) * 64, :]
            kv_ = k_t[0][hh * 64:(hh + 1) * 64, :]
            for tt in range(2):
                Sb = banks[sbi % 4]
                sbi += 1
                B.mm(Sb[:, :nq], ctxk[hh * 64:(hh + 1) * 64, h2, tt * 128:(tt + 1) * 128], qv[:, q0:q0 + nq], True, True, rd, [Sb])
                pt = nxt("pt", pts)
                B.act(pt[:, :nq], Sb[:, :nq], AF.Exp, [Sb], [pt])
                B.mm(acc[:, :nq], ctxv[:, tt, h, :], pt[:, :nq], first, False, [ctxv, pt], [acc])
                B.mm(den[:, :nq], hone(hh), pt[:, :nq], first, False, [hones_b, pt], [den])
                first = False
            prs = pairs_by_head[hh]
            for g0 in range(0, len(prs), 4):
                grp = prs[g0:g0 + 4]
                Sb = banks[sbi % 4]
                sbi += 1
                for i, (qc0, n, j, pid) in enumerate(grp):
                    B.mm(Sb[:, i * 128:(i + 1) * 128], kv_[:, j * 128:(j + 1) * 128], qv[:, q0 + qc0:q0 + qc0 + n], i == 0, False, rd, [Sb])
                    B.mm(Sb[:, i * 128:(i + 1) * 128], ident_b[:], nbv[:, pid, :], False, True, [ident_b, nb_t], [Sb])
                pt = nxt("pt", pts)
                w = len(grp) * 128
                B.act(pt[:, :w], Sb[:, :w], AF.Exp, [Sb], [pt])
                for i, (qc0, n, j, pid) in enumerate(grp):
                    vap, vt_ = vpad(j, h)
                    B.mm(acc[:, qc0:qc0 + n], vap, pt[:, i * 128:(i + 1) * 128], False, False, [vt_, pt], [acc])
                    B.mm(den[:, qc0:qc0 + n], hone(hh), pt[:, i * 128:(i + 1) * 128], False, False, [hones_b, pt], [den])
        r = nxt("tmp", tmpf)
        P.op("dve", lambda e: e.reciprocal(r[:, :nq], den[:, :nq]), [den], [r])
        B.tt(map_[:, q0:q0 + nq], acc[:, :nq], r[:, :nq], ALU.mult, [acc, r], [mt])

    def gq_attention(l, v, is_sample, sbs, ntot, kout, vout, after_proj=None):
        qkt = [SC[1], SC[2], SC[3]]

        def qk_ap(ci):
            return bfv(qkt[ci // 2])[:, (ci % 2) * 2048:(ci % 2) * 2048 + ntot], qkt[ci // 2]
        vp = [SC[4], SC[5]]

        def vpad(tt, kv, par):
            n = tt * 4 + kv * 2 + par
            return bfv(vp[n // 32])[:, (n % 32) * 128:(n % 32 + 1) * 128], vp[n // 32]
        off = [0]
        for (ntok, _) in sbs:
            off.append(off[-1] + ntok)

        def rope_to(src_f32_t, ntok, c0, out_ap, out_t):
            rb = banks[4 + _rr["sb"] % 2]
            B.mm(rb[:, :ntok], rmat[:], src_f32_t[:, :ntok], True, True, [rmat, src_f32_t], [rb])
            t1 = nxt("tmp", tmpf)
            B.tt(t1[:, :ntok], src_f32_t[:, :ntok], ropes[0][:, :ntok], ALU.mult, [src_f32_t, ropes[0]], [t1])
            t2 = nxt("tmp", tmpf)
            B.tt(t2[:, :ntok], rb[:, :ntok], ropes[1][:, :ntok], ALU.mult, [rb, ropes[1]], [t2])
            B.tt(out_ap, t1[:, :ntok], t2[:, :ntok], ALU.add, [t1, t2], [out_t])

        def mk_qk(ci, gcol, is_k_out):
            def h(si, ntok, bk):
                ap, t = qk_ap(ci)
                dst = ap[:, off[si]:off[si] + ntok]
                assert not is_sample
                return [headnorm(bk, ntok, dv[l][:, gcol:gcol + 1], l, dst, t)]
            return h

        def kout_h(si, ntok, bk):
            fin = headnorm(bk, ntok, dv[l][:, 3:4], l, kstg[:, :ntok], kstg)
            return [fin, lambda: kout(si, ntok, 0, kstg)]

        def vh(si, tt, bk):
            tg = off[si] // 128 + tt
            for kv in range(2):
                for par in range(2):
                    ap, t = vpad(tg, kv, par)
                    B.cp(ap[:, par * 64:par * 64 + 64], bk[:, kv * 64:(kv + 1) * 64], [bk], [t], eng=("act" if par else "dve"))
            if vout is not None:
                vout(si, tt, bk, 128)
        kc0 = 18 * 128
        specs = [([((15 + c) * 128, 128)], mk_qk(c, 2, False)) for c in range(3)]
        specs += [([(kc0 + kv * 64, 64), (kc0 + kv * 64, 64)], mk_qk(3 + kv, 3, False)) for kv in range(2)]
        if kout is not None:
            specs += [([(kc0, 128)], kout_h)]
        npf = project_prefetch(l, specs, 8)
        yield
        for t in vp:
            P.op("dve", lambda e, t=t: e.memset(t[:, :], 0.0), [], [t])
        project(l, v, sbs, specs, [([(19 * 128, 128)], 128, vh)], hT=SC[0], skip_norm=True, prefetched=npf)
        if after_proj is not None:
            after_proj()
        mixA, mixB = SC[6], SC[3]

        def mix_ap(c):
            if c < 2:
                return bfv(mixA)[:, c * 2048:c * 2048 + ntot], mixA
            return bfv(mixB)[:, 2048:2048 + ntot], mixB
        if is_sample:
            load_ctx(l, "gq")
        for h2 in range(3):
            q_t = qk_ap(h2)
            map_, mt = mix_ap(h2)

            def kvof(hh):
                return (2 * h2 + hh) // 3
            rd = [q_t[1], qkt[1], qkt[2]] + vp + [ctxk, ctxv]
            sink_ap = dv[l][:, 6 + h2:7 + h2]
            if not is_sample:
                for s in range(ntot // 256):
                    pairs = [(0, 256, s * 2 + j, None, None) for j in range(2)]
                    attn_qblock(256,
                                lambda hh, c0, n, s=s: q_t[0][hh * 64:(hh + 1) * 64, s * 256 + c0:s * 256 + c0 + n],
                                lambda hh, kt: qk_ap(3 + kvof(hh))[0][hh * 64:(hh + 1) * 64, kt * 128:(kt + 1) * 128],
                                lambda hh, kt: vpad(kt, kvof(hh), hh)[0],
                                [], pairs, hone, sink_ap, dv[l], map_[:, s * 256:(s + 1) * 256], mt, rd)
            else:
                for QB in range(4):
                    pairs = []
                    for ni in range(4):
                        n_ = QB * 4 + ni
                        for j in (n_ - 1, n_, n_ + 1):
                            if 0 <= j <= 15:
                                bias = None if j == n_ else (tri_b[:, 0:128] if j < n_ else tri_b[:, 128:256])
                                pairs.append((ni * 128, 128, j, bias, tri_b))
                    ctx = [(lambda hh, tt=tt: ctxk[hh * 64:(hh + 1) * 64, kvof(hh), tt * 128:(tt + 1) * 128],
                            lambda hh, tt=tt: ctxv[:, tt, kvof(hh) * 2 + hh, :]) for tt in range(2)]
                    attn_qblock(512,
                                lambda hh, c0, n, QB=QB: q_t[0][hh * 64:(hh + 1) * 64, QB * 512 + c0:QB * 512 + c0 + n],
                                lambda hh, kt: qk_ap(3 + kvof(hh))[0][hh * 64:(hh + 1) * 64, kt * 128:(kt + 1) * 128],
                                lambda hh, kt: vpad(kt, kvof(hh), hh)[0],
                                ctx, pairs, hone, sink_ap, dv[l], map_[:, QB * 512:(QB + 1) * 512], mt, rd)
        return [lambda si, c=c: (mix_ap(c)[0][:, off[si]:off[si + 1]], mix_ap(c)[1]) for c in range(3)]

    def mixer(l, v, is_sample, sbs, ntot, L, seqs, outs, hook=None):
        off = [0]
        for (ntok, _) in sbs:
            off.append(off[-1] + ntok)
        assert len(sbs) == 1
        ntok0, xk0 = sbs[0]
        norm_mod(xk0, ntok0, l, 1, v, [(bfv(SC[0])[:, k * 512:k * 512 + ntok0], SC[0]) for k in range(8)])
        filter_mlp(l, L)
        for cc in range(2):
            hyena(l, v, cc, L, seqs, sbs, w2s[cc])
        mixes = [lambda si, c=c: (w2s[c][:, :, :].rearrange("p a b -> p (a b)")[:, off[si]:off[si + 1]], w2s[c]) for c in range(2)]
        if outs is None:
            raise AssertionError("prompt-only path")
        else:
            o_k_na, o_v_na, o_k_gq, o_v_gq = outs

            def mk_kout(odram, ncol):
                def kout(si, ntok, ci, src_t):
                    for tt in range(ntok // 128):
                        tg = off[si] // 128 + tt
                        bk = banks[4 + tt % 2]
                        B.tr(bk[:, 0:128], src_t[:, tt * 128:(tt + 1) * 128], ident_f[:], [src_t, ident_f], [bk])
                        st = nxt("tmp", tmpf)
                        B.cp(st[:, 0:128], bk[:, 0:128], [bk], [st], eng="act")
                        seq, tl = (tg * 128) // 256, (tg * 128) % 256
                        P.store("sp", odram.h.ap()[seq, l, tl:tl + 128, ci * 128:(ci + 1) * 128], st[:, 0:128], st)
                return kout

            def mk_vout(odram):
                def vout(si, tt, bk, ncol):
                    tg = off[si] // 128 + tt
                    st = nxt("tmp", tmpf)
                    B.cp(st[:, 0:ncol], bk[:, 0:ncol], [bk], [st], eng="act")
                    seq, tl = (tg * 128) // 256, (tg * 128) % 256
                    P.store("sp", odram.h.ap()[seq, l, tl:tl + 128, :], st[:, 0:ncol], st)
                return vout
            def after_gq_proj():
                wout_load(l)
                if hook is not None:
                    hook()
            gq = gq_attention(l, v, is_sample, sbs, ntot, mk_kout(o_k_gq, 128), mk_vout(o_v_gq), after_proj=after_gq_proj)

            def mid():
                next(gq)
            mixes += na_attention(l, v, is_sample, sbs, ntot, mk_kout(o_k_na, 384), mk_vout(o_v_na), hook=mid)
            try:
                next(gq)
                raise AssertionError("gq_attention should have finished")
            except StopIteration as e_:
                mixes += e_.value
        wout_all(l, v, mixes, sbs, loaded=True)

    NPA = 22
    NA_W = [list(range(0, 6)), list(range(1, 6)), list(range(2, 7)), list(range(2, 8))]
    GROUPS = [[0, 1, 2, 3], [4, 5, 6, 7]]
    xoffs = B.dram_in("xoffs", [1, 8], mybir.dt.int32)
    c_edge = B.dram_in("edge", [128, 2])
    c_ropeCo = B.dram_in("ropeCo", [128, 512])
    c_ropeSo = B.dram_in("ropeSo", [128, 512])
    c_icosO = B.dram_in("icosO", [FPL, 512], BF16)
    c_isinO = B.dram_in("isinO", [FPL, 512], BF16)
    nabA = B.dram_in("nabA", [DEPTH, 6, 128, NPA * 128])
    c_triA = B.dram_in("triA", [128, 12 * 128])
    ex1 = [[T(nc.dram_tensor("ex1_%d_%d" % (l, cc), [384, 512], F32), "ex1") for cc in range(2)] for l in range(DEPTH)]
    g1 = [[T(nc.dram_tensor("g1_%d_%d" % (l, cc), [4 * 384, 512], F32), "g1") for cc in range(2)] for l in range(DEPTH)]
    ex2 = [T(nc.dram_tensor("ex2_%d" % l, [1024, 512], BF16), "ex2") for l in range(DEPTH)]
    g2 = [T(nc.dram_tensor("g2_%d" % l, [4 * 1024, 512], BF16), "g2") for l in range(DEPTH)]
    edge = B.sb([128, 2], F32, "edge")
    P.dma("sp", edge[:], c_edge, writes=[edge])
    triA = B.sb([128, 12 * 128], BF16, "triA")
    P.dma("pool", triA[:], c_triA, writes=[triA])
    nabAs = [B.sb([128, NPA * 128], BF16, "nabA%d" % i) for i in range(2)]
    hal = B.sb([128, 3, 16], F32, "hal")
    DYN = {}

    def dyn_init(e):
        for i, nm in enumerate(("b2", "o2", "a2", "b1", "a1")):
            reg = e.alloc_register("r_" + nm)
            ins = e.reg_load(reg, xoffs[0:1, i:i + 1])
            DYN[nm] = e.snap(reg, min_val=0, max_val=(3 * 1024 if nm[1] == "2" else 3 * 384))
        return ins
    P.op("sp", dyn_init, [], [])

    def dyn_dma(out, src_fn, reads, writes, **kw):
        return P.op("sp", lambda e: e.dma_start(out=out, in_=src_fn(), **kw), reads, writes, dma_trk=writes[0])

    c_perm = B.dram_in("permdup", [128, 256])
    perm_b = B.sb([128, 256], BF16, "perm_b")
    P.dma("pool", perm_b[:], c_perm, writes=[perm_b])

    SA = {}

    def mixer_sample_A1(l, xk):
        v = 1
        vt = vecT[l]
        sbs = [(512, xk)]
        hT = SC[7]
        Ut = [SC[8], SC[9]]

        def Uown(ch):
            return Ut[ch // 4][:, (ch % 4) * 512:(ch % 4 + 1) * 512], Ut[ch // 4]
        qna_t, qgq_t = SC[10], SC[11]
        qna = lambda c: bfv(qna_t)[:, c * 512:(c + 1) * 512]
        ksend = bfv(qna_t)[:, 2048:4096].rearrange("p (a b) -> p a b", b=512)
        qgq = lambda c: bfv(qgq_t)[:, c * 512:(c + 1) * 512]
        vsend = bfv(qgq_t)[:, 2048:4096].rearrange("p (a b) -> p a b", b=512)

        def h_u(ch):
            def h(si, ntok, bk):
                ap, t = Uown(ch)
                B.cp(ap, bk[:, :ntok], [bk], [t], eng=("act" if ch % 2 else "dve"))
                cc, ui = ch % 2, ch // 2
                P.dma("sp", ex1[l][cc].h.ap()[ui * 128:(ui + 1) * 128, :], ap, reads=[t], writes=[ex1[l][cc]], sem_on=t)
            return h

        def h_qna(c):
            def h(si, ntok, bk):
                return [headnorm(bk, ntok, dv[l][:, 0:1], l, qna(c), qna_t)]
            return h

        def h_kna(c):
            def h(si, ntok, bk):
                return [headnorm(bk, ntok, dv[l][:, 1:2], l, ksend[:, c, :], qna_t)]
            return h

        def rope_to(src_t, ntok, out_ap, out_t):
            rb = banks[4 + _rr["sb"] % 2]
            B.mm(rb[:, :ntok], rmat[:], src_t[:, :ntok], True, True, [rmat, src_t], [rb])
            t1 = nxt("tmp", tmpf)
            B.tt(t1[:, :ntok], src_t[:, :ntok], ropes[0][:, :ntok], ALU.mult, [src_t, ropes[0]], [t1])
            t2 = nxt("tmp", tmpf)
            B.tt(t2[:, :ntok], rb[:, :ntok], ropes[1][:, :ntok], ALU.mult, [rb, ropes[1]], [t2])
            B.tt(out_ap, t1[:, :ntok], t2[:, :ntok], ALU.add, [t1, t2], [out_t])

        def h_qgq(c):
            def h(si, ntok, bk):
                if c == 0:
                    P.dma("sp", ropes[0][:, :], c_ropeCo, writes=[ropes[0]])
                    P.dma("sp", ropes[1][:, :], c_ropeSo, writes=[ropes[1]])
                fin = headnorm(bk, ntok, dv[l][:, 2:3], l, kstg[:, :ntok], kstg)
                return [fin, lambda: rope_to(kstg, ntok, qgq(c), qgq_t)]
            return h

        def h_kgq(si, ntok, bk):
            fin = headnorm(bk, ntok, dv[l][:, 3:4], l, kstg[:, :ntok], kstg)
            return [fin, lambda: rope_to(kstg, ntok, ksend[:, 3, :], qna_t)]

        def h_v(si, tt, bk):
            B.cp(vsend[:, tt, :], bk[:, 0:512], [bk], [qgq_t], eng=("act" if tt % 2 else "dve"))
        specs = [([(ch * 128, 128)], h_u(ch)) for ch in range(6)]
        specs += [([((6 + c) * 128, 128)], h_qna(c)) for c in range(3)]
        specs += [([((9 + c) * 128, 128)], h_kna(c)) for c in range(3)]
        specs += [([((15 + c) * 128, 128)], h_qgq(c)) for c in range(3)]
        specs += [([(18 * 128, 128)], h_kgq)]
        project(l, v, sbs, specs, [([(12 * 128, 384), (19 * 128, 128)], 512, h_v)], hT=hT)
        for c in range(4):
            P.dma("sp", ex2[l].h.ap()[c * 128:(c + 1) * 128, :], ksend[:, c, :], reads=[qna_t], writes=[ex2[l]], sem_on=qna_t)
        for tt in range(4):
            P.dma("sp", ex2[l].h.ap()[512 + tt * 128:512 + (tt + 1) * 128, :], vsend[:, tt, :], reads=[qgq_t], writes=[ex2[l]], sem_on=qgq_t)
        def exchange(ccs=(0,)):
            for cc in ccs:
                P.collective(lambda e, cc=cc: e.collective_compute("AllGather", ALU.bypass, replica_groups=GROUPS,
                                                                   ins=[ex1[l][cc].h.ap().opt()], outs=[g1[l][cc].h.ap().opt()]),
                             [ex1[l][cc]], [g1[l][cc]])

        def exchange2():
            exchange(ccs=(1,))
            P.collective(lambda e: e.collective_compute("AllGather", ALU.bypass, replica_groups=GROUPS,
                                                        ins=[ex2[l].h.ap().opt()], outs=[g2[l].h.ap().opt()]),
                         [ex2[l]], [g2[l]])
        SA[("x", l)] = exchange
        SA[("x2", l)] = exchange2
        SA[l] = (Uown, qna, qna_t, qgq, qgq_t)

    def mixer_sample_A2(l, xk):
        v = 1
        vt = vecT[l]
        sbs = [(512, xk)]
        Uown, qna, qna_t, qgq, qgq_t = SA[l]
        def load_full(cc):
            Gx = g1[l][cc]
            for rk in range(4):
                P.dma("sp", SC[1][:, rk * 512:(rk + 1) * 512], Gx.h.ap()[rk * 384 + 128:rk * 384 + 256, :], reads=[Gx], writes=[SC[1]])
                P.dma("sp", SC[2][:, rk * 512:(rk + 1) * 512], Gx.h.ap()[rk * 384 + 256:rk * 384 + 384, :], reads=[Gx], writes=[SC[2]])
        load_full(0)
        filter_mlp(l, 2048)
        mixes = []
        for cc in range(2):
            G = g1[l][cc]
            Ga = G.h.ap()
            dyn_dma(hal[:, :, 0:8], lambda Ga=Ga: Ga[bass.ds(DYN["b1"], 384), 504:512].rearrange("(u p) t -> p u t", p=128), [G], [hal])
            dyn_dma(hal[:, :, 8:16], lambda Ga=Ga: Ga[bass.ds(DYN["a1"], 384), 0:8].rearrange("(u p) t -> p u t", p=128), [G], [hal])
            B.ts(hal[:, :, 7], hal[:, :, 7], edge[:, 0:1], None, ALU.mult, None, [hal, edge], [hal])
            B.ts(hal[:, :, 8], hal[:, :, 8], edge[:, 1:2], None, ALU.mult, None, [hal, edge], [hal])
            if DBG.get("hal") and l == 0 and cc == 0:
                P.store("sp", dbg_hal.h.ap(), hal[:, :, :].rearrange("p a b -> p (a b)"), hal)
                P.store("sp", dbg_u.h.ap(), Ut[0][:, :], Ut[0])
            own = SC[0]
            ownv = own[:, :].rearrange("p (a b) -> p a b", b=512)
            for ui in range(3):
                ch = 2 * ui + cc
                wcol = lambda tap: vt[:, VB_CW + tap * 6 + ch:VB_CW + tap * 6 + ch + 1]
                bcol = vt[:, VB_CB + ch:VB_CB + ch + 1]
                uap, ut = Uown(ch)
                o_ = ownv[:, ui, :]
                B.act(o_, uap, AF.Identity, [ut, vt], [own], bias=bcol, scale=wcol(1))
                B.stt(o_[:, 1:512], uap[:, 0:511], wcol(0), o_[:, 1:512], ALU.mult, ALU.add, [ut, vt, own], [own])
                B.stt(o_[:, 0:511], uap[:, 1:512], wcol(2), o_[:, 0:511], ALU.mult, ALU.add, [ut, vt, own], [own])
                B.stt(o_[:, 0:1], hal[:, ui, 7:8], wcol(0), o_[:, 0:1], ALU.mult, ALU.add, [hal, vt, own], [own])
                B.stt(o_[:, 511:512], hal[:, ui, 8:9], wcol(2), o_[:, 511:512], ALU.mult, ALU.add, [hal, vt, own], [own])
            B.tt(ownv[:, 1, :], ownv[:, 1, :], ownv[:, 2, :], ALU.mult, [own], [own])
            x1f, vf, x1c, vc = SC[1], SC[2], SC[3], SC[4]
            if cc == 1:
                load_full(1)
            for (u_, o_, ui) in ((x1f, x1c, 1), (vf, vc, 2)):
                ch = 2 * ui + cc
                wcol = lambda tap, ch=ch: vt[:, VB_CW + tap * 6 + ch:VB_CW + tap * 6 + ch + 1]
                bcol = vt[:, VB_CB + ch:VB_CB + ch + 1]
                L = 2048
                B.act(o_[:, 0:L], u_[:, 0:L], AF.Identity, [u_, vt], [o_], bias=bcol, scale=wcol(1))
                B.stt(o_[:, 1:L], u_[:, 0:L - 1], wcol(0), o_[:, 1:L], ALU.mult, ALU.add, [u_, vt, o_], [o_])
                B.stt(o_[:, 0:L - 1], u_[:, 1:L], wcol(2), o_[:, 0:L - 1], ALU.mult, ALU.add, [u_, vt, o_], [o_])
            B.tt(x1c[:, :], x1c[:, :], vc[:, :], ALU.mult, [x1c, vc], [x1c])
            if cc == 0:
                SA[("x2", l)]()
            mix_t = w2s[cc]
            mixv = mix_t[:, :, :].rearrange("p a b -> p (a b)")[:, 0:512]

            def final(c0, tn, bk, cc=cc, ownv=ownv, own=own, mixv=mixv, mix_t=mix_t):
                t = nxt("tmp", tmpf)
                B.stt(t[:, :tn], ownv[:, 1, :], vt[:, VB_SK + cc:VB_SK + cc + 1], bk[:, :tn], ALU.mult, ALU.add,
                      [own, vt, bk], [t])
                B.tt(mixv, t[:, :tn], ownv[:, 0, :], ALU.mult, [t, own], [mix_t])
            hy_core(l, cc, 2048, [0], 2048, x1c, {'zb': SC[5], 'sd': SC[6], 'decF': SC[1], 'decB': SC[2], 'yt': SC[4], 'zst': SC[7]},
                    c_icosO, c_isinO, 512, final)
            mixes.append(lambda si, mixv=mixv, mix_t=mix_t: (mixv, mix_t))
        G2 = g2[l]
        G2a = G2.h.ap()
        segs = [("b2", 256, 256, 0), ("o2", 0, 512, 256), ("a2", 0, 256, 768)]
        kw_t = SC[0]
        KW = bfv(kw_t)[:, 0:4096].rearrange("p (a b) -> p a b", b=1024)
        for (rg, s0, n, w0) in segs:
            dyn_dma(KW[:, :, w0:w0 + n], lambda rg=rg, s0=s0, n=n: G2a[bass.ds(DYN[rg], 512), s0:s0 + n].rearrange("(c p) t -> p c t", p=128),
                    [G2], [kw_t])
        vw_t = SC[4]
        VW = bfv(vw_t)[:, 0:4096].rearrange("p (a b) -> p a b", b=512)
        Gv = G2a[512:4096, :]
        vsegs = [("b2", 256, 2, 0), ("o2", 0, 4, 2), ("a2", 0, 2, 6)]
        for (rg, r0, nt, w0) in vsegs:
            if r0:
                src = lambda rg=rg, r0=r0, nt=nt: Gv[bass.ds(DYN[rg] + r0, nt * 128), :].rearrange("(w p) c -> p w c", p=128)
            else:
                src = lambda rg=rg, nt=nt: Gv[bass.ds(DYN[rg], nt * 128), :].rearrange("(w p) c -> p w c", p=128)
            dyn_dma(VW[:, w0:w0 + nt, :], src, [G2], [vw_t])
        vp_t = [SC[1], SC[2]]

        def vpadA(w, h):
            n = w * 8 + h
            return bfv(vp_t[n // 32])[:, (n % 32) * 128:(n % 32 + 1) * 128], vp_t[n // 32]
        for t in vp_t:
            P.op("dve", lambda e, t=t: e.memset(t[:, :], 0.0), [], [t])
        ci = 0
        for w in range(8):
            t = vp_t[(w * 8) // 32]
            base = ((w * 8) % 32) * 128
            dstv = bfv(t)[:, base:base + 8 * 128].rearrange("p (s c) -> p s c", c=128)
            srcv = VW[:, w, 0:384].rearrange("p (h c) -> p h c", c=64)
            for par in range(2):
                B.cp(dstv[:, par:6:2, par * 64:par * 64 + 64], srcv[:, par:6:2, :], [vw_t], [t], eng=("act" if ci % 2 else "dve"))
                ci += 1
        kg_t = SC[6]
        KG = bfv(kg_t)[:, 0:2048].rearrange("p (a b) -> p a b", b=1024)
        for kv in range(2):
            for hf in range(2):
                bk = banks[(kv * 2 + hf) % 4]
                B.mm(bk[:, :512], perm_b[:, kv * 128:(kv + 1) * 128], KW[:, 3, hf * 512:(hf + 1) * 512], True, True, [perm_b, kw_t], [bk])
                B.cp(KG[:, kv, hf * 512:(hf + 1) * 512], bk[:, :512], [bk], [kg_t], eng=("act" if hf else "dve"))
        vg_t = SC[5]

        def vpadG(w, kv, par):
            n = w * 4 + kv * 2 + par
            return bfv(vg_t)[:, n * 128:(n + 1) * 128], vg_t
        P.op("dve", lambda e: e.memset(vg_t[:, :], 0.0), [], [vg_t])
        for w in range(8):
            dstv = bfv(vg_t)[:, w * 4 * 128:(w * 4 + 4) * 128].rearrange("p (s c) -> p s c", c=128)
            srcv = VW[:, w, 384:512].rearrange("p (k c) -> p k c", c=64)
            for par in range(2):
                B.cp(dstv[:, par:4:2, par * 64:par * 64 + 64], srcv[:, :, :], [vw_t], [vg_t], eng=("act" if ci % 2 else "dve"))
                ci += 1
        load_ctx(l, "na")
        mixna_t = SC[3]
        for h2 in range(3):
            for hh in range(2):
                P.dma("pool", nabAs[hh][:, :], nabA[l, 2 * h2 + hh], writes=[nabAs[hh]])
            groups = []
            for hh in range(2):
                h = 2 * h2 + hh
                nbv = nabAs[hh][:, :].rearrange("p (a b) -> p a b", b=128)
                qv = qna(h2)[hh * 64:(hh + 1) * 64, :]
                for tt in range(2):
                    groups.append([(0, 512, ctxk[hh * 64:(hh + 1) * 64, h2, tt * 128:(tt + 1) * 128], qv, ctxv[:, tt, h, :], hone(hh),
                                    None, None, [ctxk, ctxv, qna_t])])
                prs = []
                pid = 0
                for mi in range(4):
                    for w in NA_W[mi]:
                        vap, vt_ = vpadA(w, h)
                        prs.append((mi * 128, 128, KW[hh * 64:(hh + 1) * 64, h2, w * 128:(w + 1) * 128], qv[:, mi * 128:(mi + 1) * 128],
                                    vap, hone(hh), nbv[:, pid, :], nabAs[hh], [kw_t, qna_t, vt_]))
                        pid += 1
                assert pid == NPA
                for g0 in range(0, len(prs), 4):
                    groups.append(prs[g0:g0 + 4])
            mo = bfv(mixna_t)[:, h2 * 512:(h2 + 1) * 512]
            run_attention(groups, 512, None, None, mo, mixna_t)
            mixes.append(lambda si, mo=mo: (mo, mixna_t))
        load_ctx(l, "gq")
        mixgq_t = SC[6]
        for h2 in range(3):
            def kvof(hh):
                return (2 * h2 + hh) // 3
            pairs = []
            for ni in range(4):
                for dj in range(3):
                    bias = None if dj == 1 else triA[:, (ni * 3 + dj) * 128:(ni * 3 + dj + 1) * 128]
                    pairs.append((ni * 128, 128, ni + 1 + dj, bias, triA))
            ctx = [(lambda hh, tt=tt: ctxk[hh * 64:(hh + 1) * 64, kvof(hh), tt * 128:(tt + 1) * 128],
                    lambda hh, tt=tt: ctxv[:, tt, kvof(hh) * 2 + hh, :]) for tt in range(2)]
            mo = bfv(mixgq_t)[:, 2048 + h2 * 512:2048 + (h2 + 1) * 512]
            attn_qblock(512,
                        lambda hh, c0, n: qgq(h2)[hh * 64:(hh + 1) * 64, c0:c0 + n],
                        lambda hh, w: KG[hh * 64:(hh + 1) * 64, kvof(hh), w * 128:(w + 1) * 128],
                        lambda hh, w: vpadG(w, kvof(hh), hh)[0],
                        ctx, pairs, hone, dv[l][:, 6 + h2:7 + h2], dv[l], mo, mixgq_t, [qgq_t, kg_t, vg_t, ctxk, ctxv])
            mixes.append(lambda si, mo=mo: (mo, mixgq_t))
        wout_all(l, v, mixes, sbs)

    for l in range(DEPTH):
        ada(l)

    def xp_k(k):
        return big[k // 4][:, (k % 4) * 512:(k % 4 + 1) * 512], big[k // 4]

    def xs_k(k):
        return big[2 + k // 4][:, (k % 4) * 512:(k % 4 + 1) * 512], big[2 + k // 4]
    load_x(xs, 4, lambda k, t4: (xs_k(k)[0], [big[2 + k // 4]]))
    load_x(xp, 4, lambda k, t4: (xp_k(k)[0], [big[k // 4]]))
    pblk = [(512, [xp_k(k) for k in range(8)])]
    sblk1 = [(512, [xs_k(k) for k in range(8)])]
    both = [(512, sblk1[0][1], 1), (512, pblk[0][1], 0)]
    for l in range(DEPTH):
        ffn(l, 0, both)
        mixer_sample_A1(l, sblk1[0][1])
        mixer(l, 0, False, pblk, 512, 256, [0, 256], (o_nk, o_nv, o_gk, o_gv), hook=SA[("x", l)])
        mixer_sample_A2(l, sblk1[0][1])
        ffn(l, 1, both)
    store_x(yp, 4, lambda k, tt: (big[k // 4][:, (k % 4) * 512 + tt * 128:(k % 4) * 512 + (tt + 1) * 128], big[k // 4]))
    store_x(ys, 4, lambda k, tt: (big[2 + k // 4][:, (k % 4) * 512 + tt * 128:(k % 4) * 512 + (tt + 1) * 128], big[2 + k // 4]))

    P.emit()
    return B


_CACHE = {}


def _bf16(a):
    import ml_dtypes
    return np.asarray(a, np.float32).astype(ml_dtypes.bfloat16)


def _consts():
    if "c" in _CACHE:
        return _CACHE["c"]
    c = {}
    c["ident"] = np.eye(128, dtype=np.float32)
    c["onesc"] = np.full((128, 128), 1.0 / 1024.0, np.float32)
    bo = np.zeros((128, 128), np.float32)
    bo[0:64, 0:64] = 1.0 / 64.0
    bo[64:128, 64:128] = 1.0 / 64.0
    c["bones"] = bo
    ho = np.zeros((128, 256), np.float32)
    ho[:, 0:64] = 1.0
    ho[:, 128 + 64:256] = 1.0
    c["hones"] = ho
    ki = np.arange(128)[:, None]
    qi = np.arange(128)[None, :]
    tri = np.zeros((128, 256), np.float32)
    tri[:, 0:128] = np.where(qi <= ki, 0.0, NEG)
    tri[:, 128:256] = np.where(ki <= qi, 0.0, NEG)
    c["tri"] = tri
    rm = np.zeros((128, 128), np.float32)
    for base in (0, 32, 64, 96):
        for d in range(16):
            rm[base + d + 16, base + d] = -1.0
            rm[base + d, base + d + 16] = 1.0
    c["rmat"] = rm
    pd = np.zeros((128, 256), np.float32)
    for d in range(64):
        pd[d, d] = 1.0
        pd[d, 64 + d] = 1.0
        pd[64 + d, 128 + d] = 1.0
        pd[64 + d, 128 + 64 + d] = 1.0
    c["permdup"] = pd
    pos = np.arange(2048)
    inv = (np.float32(10000.0) ** (-(np.arange(16, dtype=np.float32)) / np.float32(16))).astype(np.float32)
    ang_r = (pos // 64).astype(np.float32)[:, None] * inv[None, :]
    ang_c = (pos % 64).astype(np.float32)[:, None] * inv[None, :]
    C = np.zeros((128, 2048), np.float32)
    S = np.zeros((128, 2048), np.float32)
    for d in range(128):
        dd = d % 64
        a = ang_r if dd < 32 else ang_c
        C[d] = np.cos(a[:, dd % 16]).astype(np.float32)
        S[d] = np.sin(a[:, dd % 16]).astype(np.float32)
    c["ropeC"], c["ropeS"] = C, S
    for L, tag, FP in ((2048, "L", 2176), (256, "P", 384)):
        N = 2 * L
        t = np.arange(L, dtype=np.int64)
        f = np.arange(FP, dtype=np.int64)
        ph = 2.0 * np.pi * ((t[:, None] * f[None, :]) % N).astype(np.float64) / N
        valid = (f <= L)[None, :]
        c["cos" + tag] = _bf16(np.where(valid, np.cos(ph), 0.0))
        c["sin" + tag] = _bf16(np.where(valid, np.sin(ph), 0.0))
        wf = np.where((f == 0) | (f == L), 1.0, 2.0) * (f <= L) / N
        c["icos" + tag] = _bf16((np.cos(ph) * wf[None, :]).T)
        c["isin" + tag] = _bf16((np.sin(ph) * wf[None, :]).T)
        idx = np.arange(L, dtype=np.float32)
        tn = idx / np.float32(L - 1)
        bands = np.linspace(1e-4, 15, 16, dtype=np.float32)
        ang = (np.float32(2.0 * math.pi / L)) * idx[:, None] * bands[None, :]
        feats = np.concatenate([tn[:, None], np.cos(ang), -np.sin(ang)], axis=-1).astype(np.float32)
        c["feats" + tag] = np.ascontiguousarray(feats.T)
        max_decay = math.log(1e-2) / 0.3
        min_decay = math.log(1e-2) / 1.5
        deltas = np.abs(np.linspace(min_decay, max_decay, 256, dtype=np.float32))
        dec = np.exp(-tn[:, None] * deltas[None, :]).astype(np.float32)
        decb = dec.copy()
        decb[0, :] = 0.0
        c["dec" + tag] = np.stack([dec, decb]).astype(np.float32)
    def rs(r):
        return min(max(r - 4, 0), 24)
    pats = []
    for p in range(NPAT):
        if p < 5:
            m, j = 6, 6 + (p - 2)
        else:
            e, jj = divmod(p - 5, 4)
            m = [0, 1, 14, 15][e]
            j = (0 if m < 2 else 12) + jj
        pats.append((m, j))
    idx = np.full((128, NPAT, 128), 15 * 31, np.int64)
    for p, (m, j) in enumerate(pats):
        for a in range(2):
            for b in range(2):
                kr, qr = 2 * j + a, 2 * m + b
                rowok = rs(qr) <= kr < rs(qr) + 8
                dr = kr - qr
                for cq in range(64):
                    wl = min(max(cq - 8, 0), 48)
                    for kc in range(wl, wl + 16):
                        if rowok:
                            idx[a * 64 + kc, p, b * 64 + cq] = (dr + 7) * 31 + (kc - cq + 15)
    c["nab_idx"] = idx.reshape(128, NPAT * 128)
    _CACHE["c"] = c
    return c


def _percore(r):
    key = ("pc", r)
    if key in _CACHE:
        return _CACHE[key]
    c = _consts()
    pc = {}
    rb, ra = max(r - 1, 0), min(r + 1, 3)
    pc["xoffs"] = np.array([[rb * 1024, r * 1024, ra * 1024, rb * 384, ra * 384, 0, 0, 0]], np.int32)
    pc["edge"] = np.tile(np.array([[0.0 if r == 0 else 1.0, 0.0 if r == 3 else 1.0]], np.float32), (128, 1))
    pc["ropeCo"] = np.ascontiguousarray(c["ropeC"][:, r * 512:(r + 1) * 512])
    pc["ropeSo"] = np.ascontiguousarray(c["ropeS"][:, r * 512:(r + 1) * 512])
    pc["icosO"] = np.ascontiguousarray(c["icosL"][:, r * 512:(r + 1) * 512])
    pc["isinO"] = np.ascontiguousarray(c["isinL"][:, r * 512:(r + 1) * 512])

    def rs(q):
        return min(max(q - 4, 0), 24)
    NA_W = [list(range(0, 6)), list(range(1, 6)), list(range(2, 7)), list(range(2, 8))]
    idx = np.full((128, 22, 128), 15 * 31, np.int64)
    p = 0
    for mi in range(4):
        for w in NA_W[mi]:
            m_abs, j_abs = 4 * r + mi, 4 * r - 2 + w
            for a in range(2):
                for b in range(2):
                    kr, qr = 2 * j_abs + a, 2 * m_abs + b
                    if kr < 0 or kr > 31 or not (rs(qr) <= kr < rs(qr) + 8):
                        continue
                    dr = kr - qr
                    for cq in range(64):
                        wl = min(max(cq - 8, 0), 48)
                        for kc in range(wl, wl + 16):
                            idx[a * 64 + kc, p, b * 64 + cq] = (dr + 7) * 31 + (kc - cq + 15)
            p += 1
    pc["nabA_idx"] = idx.reshape(128, 22 * 128)
    ki = np.arange(128)[:, None]
    qi = np.arange(128)[None, :]
    tlow = np.where(qi <= ki, 0.0, NEG).astype(np.float32)
    tup = np.where(ki <= qi, 0.0, NEG).astype(np.float32)
    tri = np.zeros((128, 12, 128), np.float32)
    for ni in range(4):
        for dj in range(3):
            j_abs = 4 * r + ni - 1 + dj
            if j_abs < 0 or j_abs > 15:
                tri[:, ni * 3 + dj, :] = NEG
            elif dj == 0:
                tri[:, ni * 3 + dj, :] = tlow
            elif dj == 2:
                tri[:, ni * 3 + dj, :] = tup
    pc["triA"] = tri.reshape(128, 12 * 128)
    _CACHE[key] = pc
    return pc


def _vecs(inp, l):
    v = np.zeros((128, 128), np.float32)
    v[0:72] = inp["ada_b"][l].reshape(72, 128)
    v[72:96] = inp["norm_g"][l].reshape(24, 128)
    v[96:114] = inp["hy_conv_w"][l].reshape(18, 128)
    v[114:120] = inp["hy_conv_b"][l].reshape(6, 128)
    v[120:122] = inp["hy_skip"][l].reshape(2, 128)
    for i, k in enumerate(("na_q_g", "na_k_g", "gqa_q_g", "gqa_k_g")):
        v[122 + i, 0:64] = inp[k][l]
        v[122 + i, 64:128] = inp[k][l]
    return v


def _vecs2(inp, l):
    v = np.zeros((16, 128), np.float32)
    v[0, 0:64] = inp["hy_freq"][l, 0]
    v[1, 0:64] = inp["hy_freq"][l, 1]
    v[2, 0:64] = inp["hy_filt_b1"][l]
    v[3, 0:64] = inp["hy_filt_b2"][l]
    for c in range(3):
        v[4 + c, 0:64] = inp["gqa_sink"][l, 2 * c]
        v[4 + c, 64:128] = inp["gqa_sink"][l, 2 * c + 1]
    return v


def kernel(**inp):
    inp = {k: np.asarray(v) for k, v in inp.items()}
    if "B" not in _CACHE:
        _CACHE["B"] = build()
    B = _CACHE["B"]
    c = _consts()
    vecs = np.stack([_vecs(inp, l) for l in range(DEPTH)])
    vecs2 = np.stack([_vecs2(inp, l) for l in range(DEPTH)])
    rp = inp["na_rpb"].reshape(DEPTH, 6, 15 * 31).astype(np.float32)
    rp = np.concatenate([rp, np.full((DEPTH, 6, 1), NEG, np.float32)], axis=-1)
    nab = np.ascontiguousarray(rp[:, :, c["nab_idx"]])
    shared = {
        "vecs": vecs, "vecs2": vecs2,
        "ffn_w1": inp["ffn_w1"], "ffn_w3": inp["ffn_w3"], "ffn_w2": inp["ffn_w2"],
        "w_in": inp["w_in"], "w_out": inp["w_out"],
        "hy_filt_w1": inp["hy_filt_w1"], "hy_filt_w2": inp["hy_filt_w2"], "hy_filt_w3": inp["hy_filt_w3"],
        "nab": nab,
    }
    for k in ("ident", "onesc", "bones", "hones", "tri", "rmat", "permdup", "ropeC", "ropeS", "cosL", "sinL", "icosL", "isinL",
              "cosP", "sinP", "icosP", "isinP", "featsL", "featsP", "decL", "decP"):
        shared[k] = c[k]
    in_maps = []
    for cid in range(8):
        b = cid // 4
        r = cid % 4
        pc = _percore(r)
        cv = np.concatenate([inp["c_ctx"].reshape(8, 128), inp["c"][b].reshape(8, 128)], 0).astype(np.float32)
        m = dict(shared)
        m["xp"] = np.ascontiguousarray(inp["x_prompt"][2 * cid:2 * cid + 2].reshape(512, D))
        m["xs"] = np.ascontiguousarray(inp["x_sample"][b, r * 512:(r + 1) * 512])
        for k_, v_ in pc.items():
            if k_ != "nabA_idx":
                m[k_] = v_
        m["nabA"] = np.ascontiguousarray(rp[:, :, pc["nabA_idx"]])
        m["cvec"] = cv
        m["ada_wq"] = np.ascontiguousarray(inp["ada_w"][:, :, r * 2304:(r + 1) * 2304])
        m["cna_k"] = np.ascontiguousarray(inp["cache_na_k"][b].reshape(DEPTH, 256, 384))
        m["cna_v"] = np.ascontiguousarray(inp["cache_na_v"][b].reshape(DEPTH, 256, 384))
        m["cgq_k"] = np.ascontiguousarray(inp["cache_gqa_k"][b].reshape(DEPTH, 256, 128))
        m["cgq_v"] = np.ascontiguousarray(inp["cache_gqa_v"][b].reshape(DEPTH, 256, 128))
        m = {k: v for k, v in m.items() if k in B.din}
        in_maps.append(m)
    res = run_bass_kernel_spmd(B.nc, in_maps, core_ids=list(range(8)))
    R = res.results
    _CACHE["R"] = R
    y_prompt = np.concatenate([R[i]["yp"].reshape(2, 256, D) for i in range(8)], 0)
    y_sample = np.stack([np.concatenate([R[b * 4 + r]["ys"] for r in range(4)], 0) for b in range(2)], 0)
    if "nk" not in R[0]:
        return y_prompt, y_sample
    nk = np.concatenate([R[i]["nk"].reshape(2, DEPTH, 256, 6, 64) for i in range(8)], 0)
    nv = np.concatenate([R[i]["nv"].reshape(2, DEPTH, 256, 6, 64) for i in range(8)], 0)
    gk = np.concatenate([R[i]["gk"].reshape(2, DEPTH, 256, 2, 64) for i in range(8)], 0)
    gv = np.concatenate([R[i]["gv"].reshape(2, DEPTH, 256, 2, 64) for i in range(8)], 0)
    return y_prompt, y_sample, nk, nv, gk, gv
```

```python
import numpy as np
import math
from contextlib import ExitStack
import concourse.bass as bass
import concourse.mybir as mybir
from concourse.bass_utils import run_bass_kernel_spmd

F32 = mybir.dt.float32
BF16 = mybir.dt.bfloat16
AF = mybir.ActivationFunctionType
ALU = mybir.AluOpType

D = 1024
DFF = 2816
NFF = 22
DEPTH = 2
NADA = 9
INW = 2560
EPS = 1e-6

STAGES = {"mixer": True}


class Trk:
    __slots__ = ("name", "w", "r", "sem", "cnt", "psum")

    def __init__(self, name):
        self.name = name
        self.psum = False
        self.sem = {}
        self.cnt = {}
        self.w = None
        self.r = []


class T:
    def __init__(self, h, name):
        self.h = h
        self.trk = Trk(name)

    def __getitem__(self, k):
        return self.h[k]


class Op:
    __slots__ = ("eng", "fn", "deps", "signal", "sigval", "dma_trk", "dma_val", "waits", "cc", "kind")

    def __init__(self, eng, fn):
        self.eng = eng
        self.fn = fn
        self.deps = []
        self.signal = False
        self.sigval = 0
        self.dma_trk = None
        self.dma_val = 0
        self.cc = False
        self.kind = None


ENGS = ("pe", "act", "dve", "pool", "sp")


class Prog:
    def __init__(self, nc, es):
        self.nc = nc
        self.es = es
        self.ops = {e: [] for e in ENGS}
        self.nops = 0
        self.final = []

    def _tr(self, x):
        return x.trk if hasattr(x, 'trk') else x

    def op(self, eng, fn, reads=(), writes=(), dma_trk=None):
        o = Op(eng, fn)
        deps = []
        seen = set()
        for t in reads:
            t = self._tr(t)
            if t.w is not None and id(t.w) not in seen:
                seen.add(id(t.w))
                deps.append(t.w)
            if t.psum:
                for r in t.r:
                    if r.eng != eng and id(r) not in seen:
                        seen.add(id(r))
                        deps.append(r)
        dtk = self._tr(dma_trk) if dma_trk is not None else None
        for t in writes:
            t = self._tr(t)
            if t.w is not None and id(t.w) not in seen:
                if not (dtk is not None and t.w.dma_trk is dtk and t is dtk and t.w.kind == ("sw" if eng == "pool" else "hw")):
                    seen.add(id(t.w))
                    deps.append(t.w)
            for r in t.r:
                if id(r) not in seen:
                    seen.add(id(r))
                    deps.append(r)
        o.deps = deps
        for t in reads:
            self._tr(t).r.append(o)
        for t in writes:
            t = self._tr(t)
            t.w = o
            t.r = []
        if dma_trk is not None:
            dma_trk = self._tr(dma_trk)
            o.dma_trk = dma_trk
            o.kind = "sw" if eng == "pool" else "hw"
            dma_trk.cnt[o.kind] = dma_trk.cnt.get(o.kind, 0) + 16
            o.dma_val = dma_trk.cnt[o.kind]
        self.ops[eng].append(o)
        self.nops += 1
        return o

    def dma(self, q, out, in_, reads=(), writes=(), sem_on=None, **kw):
        st = sem_on if sem_on is not None else (writes[0] if writes else reads[0])
        return self.op(q, lambda e: e.dma_start(out=out, in_=in_, **kw), reads, writes, dma_trk=st)

    def collective(self, fn, reads, writes):
        o = self.op("pool", fn, reads, writes)
        o.cc = True
        o.eng = "cc"
        return o

    def store(self, q, out, in_, src, **kw):
        st = self._tr(src)
        if st not in self.final:
            self.final.append(st)
        return self.op(q, lambda e: e.dma_start(out=out, in_=in_, **kw), [src], [], dma_trk=st)

    def emit(self):
        nc = self.nc
        for e in ENGS:
            for o in self.ops[e]:
                for d in o.deps:
                    if d.dma_trk is None:
                        if d.eng == "pe" and o.eng == "pe" and o.dma_trk is None:
                            continue
                        d.signal = True
        for e in ENGS:
            c = 0
            for o in self.ops[e]:
                if o.cc:
                    continue
                if o.signal and o.dma_trk is None:
                    c += 1
                    o.sigval = c
        c = 0
        for o in self.ops["pool"]:
            if o.cc:
                c += 1
                o.sigval = c
                o.signal = True
        esem = {e: self.es.enter_context(nc.semaphore("s_" + e)) for e in ENGS + ("cc",)}
        trks = []
        for e in ENGS:
            for o in self.ops[e]:
                if o.dma_trk is not None and o.kind not in o.dma_trk.sem:
                    o.dma_trk.sem[o.kind] = self.es.enter_context(nc.semaphore("d%d" % len(trks)))
                    trks.append((o.dma_trk, o.kind))
        self.n_dma_sems = len(trks)
        finals = [(t.sem[k], t.cnt[k]) for (t, k) in trks if t in self.final]
        ops = self.ops

        def run(e, eng):
            seen = {}
            for o in ops[e]:
                for d in o.deps:
                    if d.dma_trk is not None:
                        s, v = d.dma_trk.sem[d.kind], d.dma_val
                    else:
                        if d.eng == "pe" and e == "pe" and o.dma_trk is None:
                            continue
                        s, v = esem[d.eng], d.sigval
                    if seen.get(s.num, 0) < v:
                        eng.wait_ge(s, v)
                        seen[s.num] = v
                ins = o.fn(eng)
                if o.cc:
                    ins.then_inc(esem["cc"])
                elif o.dma_trk is not None:
                    ins.then_inc(o.dma_trk.sem[o.kind], 16)
                elif o.signal:
                    ins.then_inc(esem[e], 1)
            if e == "sp":
                for s, v in finals:
                    eng.wait_ge(s, v)

        with nc.Block() as block:
            @block.tensor
            def _(eng):
                run("pe", eng)

            @block.scalar
            def _(eng):
                run("act", eng)

            @block.vector
            def _(eng):
                run("dve", eng)

            @block.gpsimd
            def _(eng):
                run("pool", eng)

            @block.sync
            def _(eng):
                run("sp", eng)


class Builder:
    def __init__(self):
        self.nc = bass.Bass("TRN2", target_bir_lowering=False)
        self.es = ExitStack()
        self.P = Prog(self.nc, self.es)
        self.din = {}
        self.dout = {}
        self._n = 0

    def dram_in(self, name, shape, dt=F32):
        t = self.nc.dram_tensor(name, list(shape), dt, kind="ExternalInput")
        self.din[name] = t
        return t.ap()

    def dram_out(self, name, shape, dt=F32):
        t = self.nc.dram_tensor(name, list(shape), dt, kind="ExternalOutput")
        self.dout[name] = T(t, name)
        return self.dout[name]

    def sb(self, shape, dt, name=None):
        self._n += 1
        name = "sb_" + (name or ("t%d" % self._n))
        h = self.es.enter_context(self.nc.sbuf_tensor(name, list(shape), dt))
        return T(h, name)

    def ps(self, name):
        h = self.es.enter_context(self.nc.psum_tensor(name, [128, 512], F32))
        t = T(h, name)
        t.trk.psum = True
        return t

    def mm(self, out, lhsT, rhs, start, stop, reads, writes, **kw):
        return self.P.op("pe", lambda e: e.matmul(out, lhsT, rhs, start=start, stop=stop,
                                                   skip_group_check=True, **kw), reads, writes)

    def tr(self, out, in_, ident, reads, writes):
        return self.P.op("pe", lambda e: e.transpose(out, in_, ident), reads, writes)

    def act(self, out, in_, func, reads, writes, bias=None, scale=None, eng="act"):
        kw = {}
        if bias is not None:
            kw["bias"] = bias
        if scale is not None:
            kw["scale"] = scale
        return self.P.op("act", lambda e: e.activation(out, in_, func, **kw), reads, writes)

    def tt(self, out, a, b, op, reads, writes, eng="dve"):
        return self.P.op(eng, lambda e: e.tensor_tensor(out, a, b, op), reads, writes)

    def ts(self, out, a, s1, s2, op0, op1, reads, writes, eng="dve"):
        if op1 is None:
            return self.P.op(eng, lambda e: e.tensor_scalar(out, a, s1, None, op0), reads, writes)
        return self.P.op(eng, lambda e: e.tensor_scalar(out, a, s1, s2, op0, op1), reads, writes)

    def stt(self, out, a, s, b, op0, op1, reads, writes, eng="dve"):
        return self.P.op(eng, lambda e: e.scalar_tensor_tensor(out, a, s, b, op0, op1), reads, writes)

    def cp(self, out, in_, reads, writes, eng="dve"):
        if eng == "act":
            return self.P.op("act", lambda e: e.copy(out, in_), reads, writes)
        return self.P.op(eng, lambda e: e.tensor_copy(out, in_), reads, writes)


NEG = -30000.0
DBG = {}
NPAT = 21


def na_pairs_for_m(m):
    def rs(r):
        return min(max(r - 4, 0), 24)
    r0, r1 = 2 * m, 2 * m + 1
    j0 = rs(r0) // 2
    j1 = (rs(r1) + 7) // 2
    out = []
    for j in range(j0, j1 + 1):
        if 2 <= m <= 13:
            pid = (j - m) + 2
            assert 0 <= pid <= 4
        else:
            eidx = {0: 0, 1: 1, 14: 2, 15: 3}[m]
            jb = 0 if m < 2 else 12
            pid = 5 + eidx * 4 + (j - jb)
            assert 0 <= j - jb < 4
        out.append((j, pid))
    return out


def build():
    B = Builder()
    nc = B.nc
    P = B.P
    xp = B.dram_in("xp", [512, D])
    xs = B.dram_in("xs", [512, D])
    cvec = B.dram_in("cvec", [16, 128])
    ada_w = B.dram_in("ada_wq", [DEPTH, D, NADA * D // 4])
    vecs = B.dram_in("vecs", [DEPTH, 128, 128])
    vecs2 = B.dram_in("vecs2", [DEPTH, 16, 128])
    w1 = B.dram_in("ffn_w1", [DEPTH, 2, D, DFF])
    w3 = B.dram_in("ffn_w3", [DEPTH, 2, D, DFF])
    w2 = B.dram_in("ffn_w2", [DEPTH, 2, DFF, D])
    w_in = B.dram_in("w_in", [DEPTH, D, INW])
    w_out = B.dram_in("w_out", [DEPTH, D, D])
    fw1 = B.dram_in("hy_filt_w1", [DEPTH, 33, 64])
    fw2 = B.dram_in("hy_filt_w2", [DEPTH, 64, 64])
    fw3 = B.dram_in("hy_filt_w3", [DEPTH, 64, 512])
    nab = B.dram_in("nab", [DEPTH, 6, 128, NPAT * 128])
    cna_k = B.dram_in("cna_k", [DEPTH, 256, 384])
    cna_v = B.dram_in("cna_v", [DEPTH, 256, 384])
    cgq_k = B.dram_in("cgq_k", [DEPTH, 256, 128])
    cgq_v = B.dram_in("cgq_v", [DEPTH, 256, 128])
    c_ident = B.dram_in("ident", [128, 128])
    c_ones = B.dram_in("onesc", [128, 128])
    c_bones = B.dram_in("bones", [128, 128])
    c_hones = B.dram_in("hones", [128, 256])
    c_tri = B.dram_in("tri", [128, 256])
    c_rmat = B.dram_in("rmat", [128, 128])
    c_ropeC = B.dram_in("ropeC", [128, 2048])
    c_ropeS = B.dram_in("ropeS", [128, 2048])
    FPL, FPP = 2176, 384
    c_cos = {2048: B.dram_in("cosL", [2048, FPL], BF16), 256: B.dram_in("cosP", [256, FPP], BF16)}
    c_sin = {2048: B.dram_in("sinL", [2048, FPL], BF16), 256: B.dram_in("sinP", [256, FPP], BF16)}
    c_icos = {2048: B.dram_in("icosL", [FPL, 2048], BF16), 256: B.dram_in("icosP", [FPP, 256], BF16)}
    c_isin = {2048: B.dram_in("isinL", [FPL, 2048], BF16), 256: B.dram_in("isinP", [FPP, 256], BF16)}
    c_feats = {2048: B.dram_in("featsL", [33, 2048]), 256: B.dram_in("featsP", [33, 256])}
    c_dec = {2048: B.dram_in("decL", [2, 2048, 256]), 256: B.dram_in("decP", [2, 256, 256])}
    yp = B.dram_out("yp", [512, D])
    ys = B.dram_out("ys", [512, D])
    o_nk = B.dram_out("nk", [2, DEPTH, 256, 384])
    o_nv = B.dram_out("nv", [2, DEPTH, 256, 384])
    o_gk = B.dram_out("gk", [2, DEPTH, 256, 128])
    o_gv = B.dram_out("gv", [2, DEPTH, 256, 128])
    if DBG.get("hal"):
        dbg_hal = B.dram_out("dbg_hal", [128, 48], F32)
        dbg_u = B.dram_out("dbg_u", [128, 2048], F32)
    if DBG.get("h"):
        dbg_h = B.dram_out("dbg_h", [128, 4096], BF16)
        dbg_x = B.dram_out("dbg_x", [128, 4096], F32)

    ident_f = B.sb([128, 128], F32, "ident_f")
    ident_b = B.sb([128, 128], BF16, "ident_b")
    ones_b = B.sb([128, 128], BF16, "ones_b")
    bones_b = B.sb([128, 128], BF16, "bones_b")
    hones_b = B.sb([128, 256], BF16, "hones_b")
    tri_b = B.sb([128, 256], BF16, "tri_b")
    rmat = B.sb([128, 128], F32, "rmat")
    P.dma("sp", ident_f[:], c_ident, writes=[ident_f])
    P.dma("sp", rmat[:], c_rmat, writes=[rmat])
    P.dma("pool", ident_b[:], c_ident, writes=[ident_b])
    P.dma("pool", ones_b[:], c_ones, writes=[ones_b])
    P.dma("pool", bones_b[:], c_bones, writes=[bones_b])
    P.dma("pool", hones_b[:], c_hones, writes=[hones_b])
    P.dma("pool", tri_b[:], c_tri, writes=[tri_b])
    epsc = B.sb([128, 4], F32, "epsc")
    P.op("dve", lambda e: e.memset(epsc[:, 0:1], EPS), [], [epsc])
    P.op("dve", lambda e: e.memset(epsc[:, 1:2], -math.pi), [], [epsc])

    banks = [B.ps("bank%d" % i) for i in range(8)]

    def bkb(i):
        return banks[i].h[:].bitcast(BF16)

    big = [B.sb([128, 2048], F32, "big%d" % i) for i in range(16)]
    SC = big[4:16]

    def bfv(t):
        return t.h[:].bitcast(BF16)

    xs_tr = [[Trk("xs%d_%d" % (k, h)) for h in range(2)] for k in range(8)]

    wsl = [B.sb([128, 2048], BF16, "wsl%d" % i) for i in range(4)]
    w2s = [B.sb([128, NFF, 128], BF16, "w2_%d" % i) for i in range(2)]
    tmpf = [B.sb([128, 512], F32, "tmpf%d" % i) for i in range(3)]
    sqb = [B.sb([128, 512], BF16, "sqb%d" % i) for i in range(2)]
    pts = [B.sb([128, 512], BF16, "pt%d" % i) for i in range(3)]
    rstd = B.sb([128, 512], F32, "rstd")
    nabs = [None, None]
    ropes = [B.sb([128, 512], F32, "ropeC"), B.sb([128, 512], F32, "ropeS")]
    ctxk = B.sb([128, 3, 256], BF16, "ctxk")
    ctxv = B.sb([128, 2, 6, 128], BF16, "ctxv")
    hid2 = B.sb([64, 2048], BF16, "hid2")
    hid1 = B.sb([64, 512], BF16, "hid1")
    featb = B.sb([33, 512], BF16, "featb")
    fws = B.sb([64, 64 + 64 + 512], BF16, "fws")
    ytn = B.sb([128, 2, 128], BF16, "ytn")
    kstg = rstd

    class _V:
        def __init__(self, ap, base):
            self.ap = ap
            self.trk = base.trk

        def __getitem__(self, k):
            return self.ap[k]
    argf = _V(tmpf[0][0:64, :], tmpf[0])
    argf2 = _V(tmpf[1][0:64, :], tmpf[1])
    argi = _V(tmpf[2][0:64, :].bitcast(mybir.dt.int32), tmpf[2])
    ctxs = _V(wsl[3][:, 0:1536].bitcast(F32).rearrange("p (a b) -> p a b", b=384), wsl[3])

    _rr = {"pt": 0, "tmp": 0, "sb": 0}

    def nxt(key, lst):
        _rr[key] += 1
        return lst[_rr[key] % len(lst)]

    cv_tok = B.sb([16, 128], F32, "cv_tok")
    P.dma("sp", cv_tok[:], cvec, writes=[cv_tok])
    cvT = B.sb([128, 16], BF16, "cvT")
    B.tr(banks[0][:, 0:16], cv_tok[:], ident_f[0:16, 0:16], [cv_tok, ident_f], [banks[0]])
    B.act(cvT[:], banks[0][:, 0:16], AF.Silu, [banks[0]], [cvT])

    _vtk = B.sb([128, 128], F32, "vec_tok")
    vec_tok = [_vtk for l in range(DEPTH)]
    vecT = [B.sb([128, 144], F32, "vecT%d" % l) for l in range(DEPTH)]
    v2_tok = [B.sb([16, 128], F32, "v2_tok%d" % l) for l in range(DEPTH)]
    for l in range(DEPTH):
        P.dma("sp", vec_tok[l][:], vecs[l], writes=[vec_tok[l]])
        P.dma("sp", v2_tok[l][:], vecs2[l], writes=[v2_tok[l]])
        bk = banks[1 + l]
        B.tr(bk[:, 0:128], vec_tok[l][:], ident_f[:], [vec_tok[l], ident_f], [bk])
        B.tr(bk[:, 128:144], v2_tok[l][:], ident_f[0:16, 0:16], [v2_tok[l], ident_f], [bk])
        B.cp(vecT[l][:], bk[:, 0:144], [bk], [vecT[l]])
    VB_ADA, VB_NG, VB_CW, VB_CB, VB_SK, VB_QG = 0, 72, 96, 114, 120, 122
    VB_F0, VB_F1, VB_B1, VB_B2, VB_SINK = 128, 129, 130, 131, 132
    dv = [B.sb([128, 12], F32, "dv%d" % l) for l in range(DEPTH)]
    for l in range(DEPTH):
        vt = vecT[l]
        B.ts(dv[l][:, 0:1], vt[:, VB_QG:VB_QG + 1], 0.125, None, ALU.mult, None, [vt], [dv[l]])
        B.cp(dv[l][:, 1:2], vt[:, VB_QG + 1:VB_QG + 2], [vt], [dv[l]])
        B.ts(dv[l][:, 2:3], vt[:, VB_QG + 2:VB_QG + 3], 0.125, None, ALU.mult, None, [vt], [dv[l]])
        B.cp(dv[l][:, 3:4], vt[:, VB_QG + 3:VB_QG + 4], [vt], [dv[l]])
        B.tt(dv[l][:, 4:5], vt[:, VB_F0:VB_F0 + 1], vt[:, VB_B1:VB_B1 + 1], ALU.mult, [vt], [dv[l]])
        B.tt(dv[l][:, 5:6], vt[:, VB_F1:VB_F1 + 1], vt[:, VB_B2:VB_B2 + 1], ALU.mult, [vt], [dv[l]])
        B.act(dv[l][:, 6:9], vt[:, VB_SINK:VB_SINK + 3], AF.Exp, [vt], [dv[l]])

    _mod = B.sb([128, 72, 2], F32, "mod")
    mod = [_mod for l in range(DEPTH)]
    der = [B.sb([128, 9, 8, 2], F32, "der%d" % l) for l in range(DEPTH)]

    def wslot_v(i, shape3):
        a, b = shape3
        return wsl[i][:, 0:a * b].rearrange("p (a b) -> p a b", a=a)

    ada_send = [T(nc.dram_tensor("ada_send%d" % l, [128, 36], F32), "ada_send") for l in range(DEPTH)]
    ada_gath = [T(nc.dram_tensor("ada_gath%d" % l, [4 * 128, 36], F32), "ada_gath") for l in range(DEPTH)]
    ada_loc = B.sb([128, 36], F32, "ada_loc")
    ada_all = B.sb([128, 4, 36], F32, "ada_all")

    def ada(l):
        bk = banks[3]
        first = True
        for g in range(9):
            slot = wsl[g % 2]
            sv = wslot_v(g % 2, (8, 256))
            P.dma("pool", sv, ada_w[l].rearrange("(kt p) f -> p kt f", p=128)[:, :, g * 256:(g + 1) * 256],
                  writes=[slot])
            for jj in range(2):
                c = g * 2 + jj
                for k in range(8):
                    B.mm(bk[:, c * 2:c * 2 + 2], sv[:, k, jj * 128:(jj + 1) * 128], cvT[:, k:16:8],
                         first, k == 7, [slot, cvT], [bk])
                    first = False
        B.cp(ada_loc[:, :], bk[:, 0:36], [bk], [ada_loc])
        P.dma("sp", ada_send[l].h.ap(), ada_loc[:, :], reads=[ada_loc], writes=[ada_send[l]], sem_on=ada_loc)
        P.collective(lambda e: e.collective_compute("AllGather", ALU.bypass, replica_groups=[[0, 1, 2, 3], [4, 5, 6, 7]],
                                                    ins=[ada_send[l].h.ap().opt()], outs=[ada_gath[l].h.ap().opt()]),
                     [ada_send[l]], [ada_gath[l]])
        P.dma("sp", ada_all[:, :, :], ada_gath[l].h.ap().rearrange("(r p) f -> p r f", p=128), reads=[ada_gath[l]], writes=[ada_all])
        bk = ada_all[:, :, :].rearrange("p r f -> p (r f)")
        for v in range(2):
            B.tt(mod[l][:, :, v], bk[:, v:144:2], vecT[l][:, VB_ADA:VB_ADA + 72], ALU.add,
                 [ada_all, vecT[l]], [mod[l]])
        m = mod[l]
        d_ = der[l]
        ng = vecT[l]
        for i in range(3):
            sh = m[:, (3 * i) * 8:(3 * i) * 8 + 8, :]
            sc = m[:, (3 * i + 1) * 8:(3 * i + 1) * 8 + 8, :]
            gt = m[:, (3 * i + 2) * 8:(3 * i + 2) * 8 + 8, :]
            for v in range(2):
                B.stt(d_[:, 3 * i, :, v], sc[:, :, v], 1.0, ng[:, VB_NG + i * 8:VB_NG + i * 8 + 8],
                      ALU.add, ALU.mult, [m, ng], [d_])
            B.cp(d_[:, 3 * i + 1, :, :], sh, [m], [d_])
            B.ts(d_[:, 3 * i + 2, :, :], gt, 0.5 if i != 1 else 1.0, None, ALU.mult, None, [m], [d_])

    def rsqrt_from(dst_ap, src_ap, reads, dst_t):
        B.act(dst_ap, src_ap, AF.Ln, reads + [epsc], [dst_t], bias=epsc[:, 0:1])
        B.act(dst_ap, dst_ap, AF.Exp, [dst_t], [dst_t], scale=-1.0 * 0.5)

    def norm_mod(xk, ntok, l, i, v, hout):
        bk = banks[4]
        for k in range(8):
            s = sqb[k % 2]
            if k % 2:
                B.tt(s[:, :ntok], xk[k][0], xk[k][0], ALU.mult, [xk[k][1]], [s])
            else:
                B.act(s[:, :ntok], xk[k][0], AF.Square, [xk[k][1]], [s])
            B.mm(bk[:, :ntok], ones_b[:], s[:, :ntok], k == 0, k == 7, [ones_b, s], [bk])
        rsqrt_from(rstd[:, :ntok], bk[:, :ntok], [bk], rstd)
        for k in range(8):
            t = tmpf[k % 3]
            B.tt(t[:, :ntok], xk[k][0], rstd[:, :ntok], ALU.mult, [xk[k][1], rstd], [t])
            B.act(hout[k][0], t[:, :ntok], AF.Identity, [t, der[l]], [hout[k][1]],
                  bias=der[l][:, 3 * i + 1, k, v:v + 1], scale=der[l][:, 3 * i, k, v:v + 1])

    def ffn(l, fi, xk_blocks):
        i_norm = 0 if fi == 0 else 2
        nsb = len(xk_blocks)

        def hbuf(si, k):
            t = SC[si]
            return (bfv(t)[:, k * 512:(k + 1) * 512], t)

        def gbuf(si, j):
            n = j * 2 + si
            t = SC[2 + n // 8]
            return (bfv(t)[:, (n % 8) * 512:(n % 8 + 1) * 512], t)

        for si, (ntok, xk, v) in enumerate(xk_blocks):
            norm_mod(xk, ntok, l, i_norm, v, [hbuf(si, k) for k in range(8)])
        pb = 0
        for g in range(11):
            i1, i3 = (g % 2) * 2, (g % 2) * 2 + 1
            s1, s3 = wsl[i1], wsl[i3]
            v1, v3 = wslot_v(i1, (8, 256)), wslot_v(i3, (8, 256))
            src1 = w1[l, fi].rearrange("(kt p) f -> p kt f", p=128)[:, :, g * 256:(g + 1) * 256]
            src3 = w3[l, fi].rearrange("(kt p) f -> p kt f", p=128)[:, :, g * 256:(g + 1) * 256]
            P.dma("pool", v1, src1, writes=[s1])
            P.dma("pool", v3, src3, writes=[s3])
            for jj in range(2):
                j = 2 * g + jj
                for si, (ntok, xk, v) in enumerate(xk_blocks):
                    pa, pbk = banks[pb % 4], banks[(pb + 1) % 4]
                    pb += 2
                    for k in range(8):
                        hap, ht = hbuf(si, k)
                        B.mm(pa[:, :ntok], v1[:, k, jj * 128:(jj + 1) * 128], hap[:, :ntok], k == 0, k == 7, [s1, ht], [pa])
                    for k in range(8):
                        hap, ht = hbuf(si, k)
                        B.mm(pbk[:, :ntok], v3[:, k, jj * 128:(jj + 1) * 128], hap[:, :ntok], k == 0, k == 7, [s3, ht], [pbk])
                    t = nxt("tmp", tmpf)
                    B.act(t[:, :ntok], pa[:, :ntok], AF.Silu, [pa], [t])
                    gap, gt = gbuf(si, j)
                    B.tt(gap[:, :ntok], t[:, :ntok], pbk[:, :ntok], ALU.mult, [t, pbk], [gt])
        for dch in range(8):
            s2 = w2s[dch % 2]
            src2 = w2[l, fi].rearrange("(j p) d -> p j d", p=128)[:, :, dch * 128:(dch + 1) * 128]
            P.dma("pool", s2[:, 0:11, :], src2[:, 0:11, :], writes=[s2])
            P.dma("pool", s2[:, 11:22, :], src2[:, 11:22, :], writes=[s2])
            for si, (ntok, xk, v) in enumerate(xk_blocks):
                po = banks[4 + (dch * nsb + si) % 4]
                for j in range(NFF):
                    gap, gt = gbuf(si, j)
                    B.mm(po[:, :ntok], s2[:, j, :], gap[:, :ntok], j == 0, j == NFF - 1, [s2, gt], [po])
                xap, xt = xk[dch]
                B.stt(xap, po[:, :ntok], der[l][:, 3 * i_norm + 2, dch, v:v + 1], xap, ALU.mult, ALU.add,
                      [po, der[l], xt], [xt])

    def load_x(src, ntiles, dst4, extra_w=None):
        for t4 in range(0, ntiles, 4):
            stg = SC[(t4 // 4) % 2 * 2:(t4 // 4) % 2 * 2 + 2]
            for i in range(4):
                st = stg[i // 2]
                P.dma("sp", st[:, (i % 2) * 1024:(i % 2 + 1) * 1024], src[(t4 + i) * 128:(t4 + i + 1) * 128, :], writes=[st])
            for k in range(8):
                bk = banks[k % 4]
                for i in range(4):
                    st = stg[i // 2]
                    B.tr(bk[:, i * 128:(i + 1) * 128], st[:, (i % 2) * 1024 + k * 128:(i % 2) * 1024 + (k + 1) * 128],
                         ident_f[:], [st, ident_f], [bk])
                dap, dt_ = dst4(k, t4)
                B.cp(dap, bk[:, :], [bk], dt_, eng=("dve" if k % 2 else "act"))

    def store_x(dstT, ntiles, srcf):
        for tt in range(ntiles):
            stg = SC[tt % 4]
            for half in range(2):
                bk = banks[(tt * 2 + half) % 8]
                for kk in range(4):
                    k = half * 4 + kk
                    sap, st = srcf(k, tt)
                    B.tr(bk[:, kk * 128:(kk + 1) * 128], sap, ident_f[:], [st, ident_f], [bk])
                B.cp(stg[:, half * 512:(half + 1) * 512], bk[:, :], [bk], [stg], eng=("dve" if half else "act"))
            P.store("sp", dstT.h.ap()[tt * 128:(tt + 1) * 128, :], stg[:, 0:1024], stg)

    def wslot_half(ci, sbase=0):
        hs = (ci + sbase) % 8
        t = wsl[hs // 2]
        return t, t[:, (hs % 2) * 1024:(hs % 2 + 1) * 1024].rearrange("p (a b) -> p a b", a=8)

    def wload(l, chunk_specs, ci, sbase=0):
        wi = w_in[l].rearrange("(kt p) f -> p kt f", p=128)
        slot, sv = wslot_half(ci, sbase)
        o = 0
        for (c0, n) in chunk_specs[ci][0]:
            P.dma("pool", sv[:, :, o:o + n], wi[:, :, c0:c0 + n], writes=[slot])
            o += n

    def project_prefetch(l, chunk_specs, n, sbase=0):
        for cj in range(min(n, len(chunk_specs), 8)):
            wload(l, chunk_specs, cj, sbase)
        return min(n, len(chunk_specs), 8)

    def project(l, v, sbs, chunk_specs, tok_major_specs=(), hT=None, skip_norm=False, prefetched=0, sbase=0):
        hT = SC[0] if hT is None else hT
        for si, (ntok, xk) in enumerate(sbs):
            hv = [(bfv(hT)[:, k * 512:k * 512 + ntok], hT) for k in range(8)]
            if not skip_norm:
                norm_mod(xk, ntok, l, 1, v, hv)
            if DBG.get("h") and not DBG.get("h_done"):
                DBG["h_done"] = True
                P.store("sp", dbg_h.h.ap(), bfv(hT)[:, 0:4096], hT)
                for k in range(8):
                    P.store("sp", dbg_x.h.ap()[:, k * 512:(k + 1) * 512], xk[k][0], xk[k][1])
            wi = w_in[l].rearrange("(kt p) f -> p kt f", p=128)
            pend = []
            nch = len(chunk_specs)
            for cj in range(prefetched if si == 0 else 0, min(8, nch)):
                wload(l, chunk_specs, cj, sbase)
            for ci, (cols, handler) in enumerate(chunk_specs):
                slot, sv = wslot_half(ci, sbase)
                bk = banks[ci % 4]
                for k in range(8):
                    B.mm(bk[:, :ntok], sv[:, k, :], hv[k][0], k == 0, k == 7, [slot, hT], [bk])
                for ent in list(pend):
                    ent.pop(0)()
                    if not ent:
                        pend.remove(ent)
                st = handler(si, ntok, bk)
                if st:
                    pend.append(list(st))
                if ci + 8 < nch:
                    wload(l, chunk_specs, ci + 8, sbase)
            while pend:
                for ent in list(pend):
                    ent.pop(0)()
                    if not ent:
                        pend.remove(ent)
            for ti, (cols, ncol, handler) in enumerate(tok_major_specs):
                assert ncol <= 512
                s_a, s_b = wsl[(ti * 2) % 4], wsl[(ti * 2 + 1) % 4]
                va = wslot_v((ti * 2) % 4, (8, 256))
                vb = wslot_v((ti * 2 + 1) % 4, (8, 256))
                o = 0
                for (c0, n) in cols:
                    while n > 0:
                        if o < 256:
                            nn = min(n, 256 - o)
                            P.dma("pool", va[:, :, o:o + nn], wi[:, :, c0:c0 + nn], writes=[s_a])
                        else:
                            nn = min(n, 512 - o)
                            P.dma("pool", vb[:, :, o - 256:o - 256 + nn], wi[:, :, c0:c0 + nn], writes=[s_b])
                        o += nn
                        c0 += nn
                        n -= nn
                for tt in range(ntok // 128):
                    bk = banks[4 + tt % 2]
                    na_ = min(ncol, 256)
                    for k in range(8):
                        B.mm(bk[:, 0:na_], hv[k][0][:, tt * 128:(tt + 1) * 128], va[:, k, 0:na_], k == 0, k == 7, [s_a, hT], [bk])
                    if ncol > 256:
                        for k in range(8):
                            B.mm(bk[:, 256:ncol], hv[k][0][:, tt * 128:(tt + 1) * 128], vb[:, k, 0:ncol - 256], False, k == 7, [s_b, hT], [bk])
                    handler(si, tt, bk)

    def wout_load(l):
        for rc in range(8):
            P.dma("pool", wsl[rc // 2][:, (rc % 2) * 1024:(rc % 2 + 1) * 1024], w_out[l][rc * 128:(rc + 1) * 128, :], writes=[wsl[rc // 2]])

    def wout_all(l, v, mixes, sbs, loaded=False):
        if not loaded:
            wout_load(l)
        slots = [(wsl[rc // 2][:, (rc % 2) * 1024:(rc % 2 + 1) * 1024], wsl[rc // 2]) for rc in range(8)]
        n = 0
        for si, (ntok, xk) in enumerate(sbs):
            for dch in range(8):
                bk = banks[4 + n % 4]
                n += 1
                for i in range(8):
                    map_, mt = mixes[i](si)
                    B.mm(bk[:, :ntok], slots[i][0][:, dch * 128:(dch + 1) * 128], map_, i == 0, i == 7,
                         [slots[i][1], mt], [bk])
                xap, xt = xk[dch]
                B.stt(xap, bk[:, :ntok], der[l][:, 5, dch, v:v + 1], xap, ALU.mult, ALU.add, [bk, der[l], xt], [xt])

    def filter_load(l, L):
        P.dma("pool", fws[0:33, 0:64], fw1[l], writes=[fws])
        P.dma("pool", fws[:, 64:128], fw2[l], writes=[fws])
        P.dma("pool", fws[:, 128:640], fw3[l], writes=[fws])
        P.dma("pool", bfv(SC[7])[0:33, 2048:2048 + L], c_feats[L], writes=[SC[7]])

    def filter_mlp(l, L, loaded=False):
        vt = vecT[l]
        if not loaded:
            filter_load(l, L)

        def sin_stages(items, fcol, fbcol):
            tv = []
            for (ps_ap, n, out_ap, out_t, rd, tt_) in items:
                tv.append((tt_[0:64, 0:n], tt_[0:64, 512:512 + n], tt_[0:64, 1024:1024 + n].bitcast(mybir.dt.int32)))
            for step in range(6):
                for idx, (ps_ap, n, out_ap, out_t, rd, tt_) in enumerate(items):
                    a_, a2_, ai_ = tv[idx]
                    if step == 0:
                        B.ts(a_, ps_ap, vt[0:64, fcol:fcol + 1], dv[l][0:64, fbcol:fbcol + 1], ALU.mult, ALU.add,
                             rd + [vt, dv[l]], [tt_])
                    elif step == 1:
                        B.ts(ai_, a_, 1.0 / (2 * math.pi), None, ALU.mult, None, [tt_], [tt_])
                    elif step == 2:
                        B.cp(a2_, ai_, [tt_], [tt_])
                    elif step == 3:
                        B.stt(a_, a2_, -2 * math.pi, a_, ALU.mult, ALU.add, [tt_], [tt_])
                    elif step == 4:
                        B.ts(a_, a_, 3.1415925, -3.1415925, ALU.min, ALU.max, [tt_], [tt_])
                    else:
                        B.act(out_ap, a_, AF.Sin, [tt_], [out_t])

        chunks = [(c0, min(512, L - c0)) for c0 in range(0, L, 512)]
        h1 = SC[7]
        fb = bfv(h1)[0:33, 2048:2048 + L]
        h1v = bfv(h1)[0:64, 0:L]
        for ci, (c0, n) in enumerate(chunks):
            B.mm(banks[ci % 4][0:64, :n], fws[0:33, 0:64], fb[:, c0:c0 + n], True, True, [fws, h1], [banks[ci % 4]])
        sin_stages([(banks[ci % 4][0:64, :n], n, h1v[:, c0:c0 + n], h1, [banks[ci % 4]], SC[3 + ci]) for ci, (c0, n) in enumerate(chunks)],
                   VB_F0, 4)
        for ci, (c0, n) in enumerate(chunks):
            B.mm(banks[4 + ci % 4][0:64, :n], fws[:, 64:128], h1v[:, c0:c0 + n], True, True, [fws, h1], [banks[4 + ci % 4]])
        sin_stages([(banks[4 + ci % 4][0:64, :n], n, hid2[:, c0:c0 + n], hid2, [banks[4 + ci % 4]], SC[3 + ci]) for ci, (c0, n) in enumerate(chunks)],
                   VB_F1, 5)

    def hyena(l, v, cc, L, seqs, sbs, mix_t, prefetched=0, sbase=0):
        ntot = L * len(seqs)
        ntt = L // 128
        vt = vecT[l]
        U = [SC[1], SC[2], SC[3]]
        x0c, zf, vc = SC[4], SC[5], SC[6]
        off = [0]
        for (ntok, _) in sbs:
            off.append(off[-1] + ntok)

        def mk_handler(ui):
            def h(si, ntok, bk):
                B.cp(U[ui][:, off[si]:off[si] + ntok], bk[:, :ntok], [bk], [U[ui]], eng=("act" if ui % 2 else "dve"))
            return h
        specs = [([((2 * ui + cc) * 128, 128)], mk_handler(ui)) for ui in range(3)]
        project(l, v, sbs, specs, skip_norm=True, prefetched=prefetched, sbase=sbase)
        outs = [x0c, zf, vc]
        for ui in range(3):
            ch = 2 * ui + cc
            wcol = lambda tap: vt[:, VB_CW + tap * 6 + ch:VB_CW + tap * 6 + ch + 1]
            bcol = vt[:, VB_CB + ch:VB_CB + ch + 1]
            o_ = outs[ui]
            u_ = U[ui]
            eng = "dve"
            for s0 in seqs:
                B.act(o_[:, s0:s0 + L], u_[:, s0:s0 + L], AF.Identity, [u_, vt], [o_], bias=bcol, scale=wcol(1))
                B.stt(o_[:, s0 + 1:s0 + L], u_[:, s0:s0 + L - 1], wcol(0), o_[:, s0 + 1:s0 + L], ALU.mult, ALU.add,
                      [u_, vt, o_], [o_], eng=eng)
                B.stt(o_[:, s0:s0 + L - 1], u_[:, s0 + 1:s0 + L], wcol(2), o_[:, s0:s0 + L - 1], ALU.mult, ALU.add,
                      [u_, vt, o_], [o_], eng=eng)
        B.tt(zf[:, 0:ntot], zf[:, 0:ntot], vc[:, 0:ntot], ALU.mult, [zf, vc], [zf])
        mixv = mix_t[:, :, :].rearrange("p a b -> p (a b)")[:, 0:ntot]

        def final(c0, tn, bk):
            t = nxt("tmp", tmpf)
            B.stt(t[:, :tn], zf[:, c0:c0 + tn], vt[:, VB_SK + cc:VB_SK + cc + 1], bk[:, :tn], ALU.mult, ALU.add,
                  [zf, vt, bk], [t])
            B.tt(mixv[:, c0:c0 + tn], t[:, :tn], x0c[:, c0:c0 + tn], ALU.mult, [t, x0c], [mix_t])
        hy_core(l, cc, L, seqs, ntot, zf, {'zb': SC[1], 'sd': SC[2], 'decF': SC[3], 'decB': SC[6], 'yt': SC[7]},
                c_icos[L], c_isin[L], L, final)

    def hy_core(l, cc, L, seqs, ntot, zf, TL, inv_icos, inv_isin, Linv, final):
        ntt = L // 128
        vt = vecT[l]
        zb_t = TL['zb']
        zb = bfv(zb_t)[:, 0:ntot]
        zT = bfv(zb_t)[:, 2048:2048 + ntot].rearrange("p (a b) -> p a b", b=128)
        B.cp(zb, zf[:, 0:ntot], [zf], [zb_t], eng="act")
        for t4 in range(0, ntot // 128, 4):
            nb = min(4, ntot // 128 - t4)
            bi = 2 + (t4 // 4) % 2
            for i in range(nb):
                B.tr(bkb(bi)[:, i * 128:(i + 1) * 128], zb[:, (t4 + i) * 128:(t4 + i + 1) * 128], ident_b[:], [zb_t, ident_b], [banks[bi]])
            B.cp(zT[:, t4:t4 + nb, :], bkb(bi)[:, 0:nb * 128].rearrange("p (a b) -> p a b", b=128), [banks[bi]], [zb_t])
        sd_t = TL['sd']
        Sl = bfv(sd_t)[:, 0:L].rearrange("p (a b) -> p a b", b=128)
        Dl = bfv(sd_t)[:, 2048:2048 + L].rearrange("p (a b) -> p a b", b=128)
        decF, decB = TL['decF'], TL['decB']
        dF = decF[:, 0:ntt * 128].rearrange("p (a b) -> p a b", b=128)
        dB = decB[:, 0:ntt * 128].rearrange("p (a b) -> p a b", b=128)
        P.dma("sp", dF, c_dec[L][0].rearrange("(tt p) c -> p tt c", p=128)[:, :, cc * 128:(cc + 1) * 128], writes=[decF])
        P.dma("sp", dB, c_dec[L][1].rearrange("(tt p) c -> p tt c", p=128)[:, :, cc * 128:(cc + 1) * 128], writes=[decB])
        dFf = decF[:, 0:ntt * 128]
        dBf = decB[:, 0:ntt * 128]
        Slf = bfv(sd_t)[:, 0:L]
        Dlf = bfv(sd_t)[:, 2048:2048 + L]
        for t2 in range(0, ntt, 2):
            bk = banks[(t2 // 2) % 2]
            for j in range(2):
                tt = t2 + j
                B.mm(bk[:, j * 128:(j + 1) * 128], hid2[:, tt * 128:(tt + 1) * 128], fws[:, 128 + cc * 128:128 + (cc + 1) * 128],
                     j == 0, True, [hid2, fws], [bk])
                B.mm(bk[:, 256 + j * 128:256 + (j + 1) * 128], hid2[:, tt * 128:(tt + 1) * 128],
                     fws[:, 128 + 256 + cc * 128:128 + 256 + (cc + 1) * 128], False, True, [hid2, fws], [bk])
            ta = nxt("tmp", tmpf)
            B.tt(ta[:, 0:256], bk[:, 0:256], dFf[:, t2 * 128:(t2 + 2) * 128], ALU.mult, [bk, decF], [ta])
            B.tt(ta[:, 256:512], bk[:, 256:512], dBf[:, t2 * 128:(t2 + 2) * 128], ALU.mult, [bk, decB], [ta])
            B.tt(Slf[:, t2 * 128:(t2 + 2) * 128], ta[:, 0:256], ta[:, 256:512], ALU.add, [ta], [sd_t])
            B.tt(Dlf[:, t2 * 128:(t2 + 2) * 128], ta[:, 0:256], ta[:, 256:512], ALU.subtract, [ta], [sd_t])
        FP = FPL if L == 2048 else FPP
        nft = FP // 128
        yt_t = TL['yt']
        YT = bfv(yt_t)[:, 0:4096].rearrange("p (s f r c) -> p s f r c", s=len(seqs), r=2, c=128)
        nft_main = YT.shape[2]
        cosr = c_cos[L].rearrange("(tt p) f -> p tt f", p=128)
        sinr = c_sin[L].rearrange("(tt p) f -> p tt f", p=128)
        fchunks = [(f0, min(512, FP - f0)) for f0 in range(0, FP, 512)]
        _pend = []
        for (f0, fn) in fchunks:
            for g4 in range(0, ntt, 4):
                ng = min(4, ntt - g4)
                sc_i, ss_i = (g4 // 4) % 2 * 2, (g4 // 4) % 2 * 2 + 1
                cv_ = wslot_v(sc_i, (4, 512))
                sv_ = wslot_v(ss_i, (4, 512))
                P.dma("sp", cv_[:, 0:ng, 0:fn], cosr[:, g4:g4 + ng, f0:f0 + fn], writes=[wsl[sc_i]])
                P.dma("sp", sv_[:, 0:ng, 0:fn], sinr[:, g4:g4 + ng, f0:f0 + fn], writes=[wsl[ss_i]])
                for i in range(ng):
                    tt = g4 + i
                    st, sp_ = (tt == 0), (tt == ntt - 1)
                    B.mm(banks[0][:, :fn], Sl[:, tt, :], cv_[:, i, 0:fn], st, sp_, [sd_t, wsl[sc_i]], [banks[0]])
                    B.mm(banks[1][:, :fn], Dl[:, tt, :], sv_[:, i, 0:fn], st, sp_, [sd_t, wsl[ss_i]], [banks[1]])
                    for s, s0 in enumerate(seqs):
                        tg = s0 // 128 + tt
                        B.mm(banks[2 + 2 * s][:, :fn], zT[:, tg, :], cv_[:, i, 0:fn], st, sp_, [zb_t, wsl[sc_i]], [banks[2 + 2 * s]])
                        B.mm(banks[3 + 2 * s][:, :fn], zT[:, tg, :], sv_[:, i, 0:fn], st, sp_, [zb_t, wsl[ss_i]], [banks[3 + 2 * s]])
            kc_s, ks_s = tmpf[0], tmpf[1]
            pipelined = (len(seqs) == 1 and 'zst' in TL)

            def transposes(f0, fn, s, yre, yim):
                for ri, ysrc in enumerate((yre, yim)):
                    bi = 6 + ri
                    nb = fn // 128
                    for i in range(nb):
                        B.tr(bkb(bi)[:, i * 128:(i + 1) * 128], ysrc[:, i * 128:(i + 1) * 128], ident_b[:], [ysrc, ident_b], [banks[bi]])
                    ft0 = f0 // 128
                    if ft0 + nb <= nft_main:
                        B.cp(YT[:, s, ft0:ft0 + nb, ri, :], bkb(bi)[:, 0:nb * 128].rearrange("p (a b) -> p a b", b=128),
                             [banks[bi]], [yt_t], eng=("act" if ri else "dve"))
                    else:
                        assert nb == 1 and len(seqs) == 1
                        B.cp(ytn[:, ri, :], bkb(bi)[:, 0:128], [banks[bi]], [ytn], eng=("act" if ri else "dve"))

            def products(zc_ap, zc_t, zs_ap, zs_t, fn, yre, yim):
                t1, t2 = tmpf[2], rstd
                B.tt(t1[:, :fn], zc_ap, kc_s[:, :fn], ALU.mult, [zc_t, kc_s], [t1])
                B.tt(t2[:, :fn], zs_ap, ks_s[:, :fn], ALU.mult, [zs_t, ks_s], [t2])
                B.tt(yre[:, :fn], t1[:, :fn], t2[:, :fn], ALU.subtract, [t1, t2], [yre])
                B.tt(t1[:, :fn], zc_ap, ks_s[:, :fn], ALU.mult, [zc_t, ks_s], [t1])
                B.tt(t2[:, :fn], zs_ap, kc_s[:, :fn], ALU.mult, [zs_t, kc_s], [t2])
                B.tt(yim[:, :fn], t1[:, :fn], t2[:, :fn], ALU.add, [t1, t2], [yim])
            yre, yim = pts[0], pts[1]
            if pipelined:
                if _pend:
                    transposes(*_pend.pop())
                zst = TL['zst']
                B.cp(kc_s[:, :fn], banks[0][:, :fn], [banks[0]], [kc_s], eng="act")
                B.cp(ks_s[:, :fn], banks[1][:, :fn], [banks[1]], [ks_s], eng="act")
                B.cp(zst[:, 0:fn], banks[2][:, :fn], [banks[2]], [zst], eng="dve")
                B.cp(zst[:, 512:512 + fn], banks[3][:, :fn], [banks[3]], [zst], eng="dve")
                products(zst[:, 0:fn], zst, zst[:, 512:512 + fn], zst, fn, yre, yim)
                _pend.append((f0, fn, 0, yre, yim))
            else:
                B.cp(kc_s[:, :fn], banks[0][:, :fn], [banks[0]], [kc_s], eng="act")
                B.cp(ks_s[:, :fn], banks[1][:, :fn], [banks[1]], [ks_s], eng="act")
                for s, s0 in enumerate(seqs):
                    zc, zs = banks[2 + 2 * s], banks[3 + 2 * s]
                    products(zc[:, :fn], zc, zs[:, :fn], zs, fn, yre, yim)
                    transposes(f0, fn, s, yre, yim)
        if _pend:
            transposes(*_pend.pop())
        icr = inv_icos.rearrange("(ft p) t -> p ft t", p=128)
        isr = inv_isin.rearrange("(ft p) t -> p ft t", p=128)
        tn = min(512, Linv)
        for tc0 in range(0, Linv, tn):
            for s, s0 in enumerate(seqs):
                bk = banks[(tc0 // tn + s) % 2]
                nmm = 0
                for g4 in range(0, nft, 4):
                    ng = min(4, nft - g4)
                    sc_i, ss_i = (g4 // 4) % 2 * 2, (g4 // 4) % 2 * 2 + 1
                    cv_ = wslot_v(sc_i, (4, 512))
                    sv_ = wslot_v(ss_i, (4, 512))
                    P.dma("sp", cv_[:, 0:ng, 0:tn], icr[:, g4:g4 + ng, tc0:tc0 + tn], writes=[wsl[sc_i]])
                    P.dma("sp", sv_[:, 0:ng, 0:tn], isr[:, g4:g4 + ng, tc0:tc0 + tn], writes=[wsl[ss_i]])
                    for i in range(ng):
                        ft = g4 + i
                        if ft < nft_main:
                            lre, lim, lt = YT[:, s, ft, 0, :], YT[:, s, ft, 1, :], yt_t
                        else:
                            lre, lim, lt = ytn[:, 0, :], ytn[:, 1, :], ytn
                        B.mm(bk[:, :tn], lre, cv_[:, i, 0:tn], nmm == 0, False, [lt, wsl[sc_i]], [bk])
                        nmm += 1
                        B.mm(bk[:, :tn], lim, sv_[:, i, 0:tn], False, nmm == 2 * nft - 1, [lt, wsl[ss_i]], [bk])
                        nmm += 1
                final(s0 + tc0, tn, bk)

    def headnorm(bk, ntok, gcol_ap, l, out_ap, out_t, extra_reads=()):
        s = nxt("sb", sqb)
        B.act(s[:, :ntok], bk[:, :ntok], AF.Square, [bk], [s])
        nbi = 4 + _rr["sb"] % 2

        def fin():
            nb = banks[nbi]
            B.mm(nb[:, :ntok], bones_b[:], s[:, :ntok], True, True, [bones_b, s], [nb])
            r = nxt("tmp", tmpf)
            rsqrt_from(r[:, :ntok], nb[:, :ntok], [nb], r)
            B.tt(r[:, :ntok], bk[:, :ntok], r[:, :ntok], ALU.mult, [bk, r], [r])
            B.act(out_ap, r[:, :ntok], AF.Identity, [r, dv[l]] + list(extra_reads), [out_t], scale=gcol_ap)
        return fin

    def run_attention(groups, nq, sink_ap, sink_t, mix_ap, mix_t):
        acc, den = banks[6], banks[7]
        st = {"first": True, "sb": 0}

        def emit_S(grp):
            st["sb"] += 1
            Sb = banks[st["sb"] % 4]
            col = 0
            for i, (qc0, n, kap, qap, vap, hap, bias, bt, rdt) in enumerate(grp):
                B.mm(Sb[:, col:col + n], kap, qap, i == 0, bias is None, rdt, [Sb])
                if bias is not None:
                    B.mm(Sb[:, col:col + n], ident_b[:], bias, False, True, [ident_b, bt], [Sb])
                col += n
            pt = nxt("pt", pts)
            B.act(pt[:, :col], Sb[:, :col], AF.Exp, [Sb], [pt])
            return pt

        def emit_PV(grp, pt):
            col = 0
            for (qc0, n, kap, qap, vap, hap, bias, bt, rdt) in grp:
                B.mm(acc[:, qc0:qc0 + n], vap, pt[:, col:col + n], st["first"], False, rdt + [pt], [acc])
                B.mm(den[:, qc0:qc0 + n], hap, pt[:, col:col + n], st["first"], False, [hones_b, pt], [den])
                st["first"] = False
                col += n
        pending = None
        for grp in groups:
            pt = emit_S(grp)
            if pending is not None:
                emit_PV(*pending)
            pending = (grp, pt)
        emit_PV(*pending)
        r = nxt("tmp", tmpf)
        if sink_ap is not None:
            B.act(r[:, :nq], den[:, :nq], AF.Ln, [den, sink_t], [r], bias=sink_ap)
        else:
            B.act(r[:, :nq], den[:, :nq], AF.Ln, [den], [r])
        B.act(r[:, :nq], r[:, :nq], AF.Exp, [r], [r], scale=-1.0)
        B.tt(mix_ap, acc[:, :nq], r[:, :nq], ALU.mult, [acc, r], [mix_t])

    def attn_qblock(nq, q_ap, k_ap, v_ap, ctx, pairs, hone, sink_ap, sink_t, mix_ap, mix_t, rd_tiles):
        groups = []
        for hh in range(2):
            for (kc, vc_) in ctx:
                groups.append([(0, nq, kc(hh), q_ap(hh, 0, nq), vc_(hh), hone(hh), None, None, rd_tiles)])
            cur, tot = [], 0
            for (qc0, n, kt, bias, bt) in pairs:
                if tot + n > 512:
                    groups.append(cur)
                    cur, tot = [], 0
                cur.append((qc0, n, k_ap(hh, kt), q_ap(hh, qc0, n), v_ap(hh, kt), hone(hh), bias, bt, rd_tiles))
                tot += n
            if cur:
                groups.append(cur)
        run_attention(groups, nq, sink_ap, sink_t, mix_ap, mix_t)

    def hone(hh):
        return hones_b[:, hh * 128:(hh + 1) * 128]

    def load_ctx(l, kind):
        if kind == "na":
            ksrc, vsrc, ncol = cna_k[l], cna_v[l], 384
        else:
            ksrc, vsrc, ncol = cgq_k[l], cgq_v[l], 128
        P.op("dve", lambda e: e.memset(ctxv[:], 0.0), [], [ctxv])
        if kind == "na":
            P.dma("sp", ctxs[:, :, 0:384], ksrc.rearrange("(tt p) c -> p tt c", p=128), writes=[ctxs])
            for tt in range(2):
                for c in range(3):
                    bk = banks[(tt * 3 + c) % 4]
                    B.tr(bk[:, 0:128], ctxs[:, tt, c * 128:(c + 1) * 128], ident_f[:], [ctxs, ident_f], [bk])
                    B.cp(ctxk[:, c, tt * 128:(tt + 1) * 128], bk[:, 0:128], [bk], [ctxk])
            for h in range(6):
                P.dma("pool", ctxv[:, :, h, (h % 2) * 64:(h % 2) * 64 + 64],
                      vsrc.rearrange("(tt p) c -> p tt c", p=128)[:, :, h * 64:(h + 1) * 64], writes=[ctxv])
        else:
            for kv in range(2):
                for par in range(2):
                    P.dma("sp", ctxs[:, :, kv * 128 + par * 64:kv * 128 + par * 64 + 64],
                          ksrc.rearrange("(tt p) c -> p tt c", p=128)[:, :, kv * 64:(kv + 1) * 64], writes=[ctxs])
            for tt in range(2):
                for kv in range(2):
                    bk = banks[(tt * 2 + kv) % 4]
                    B.tr(bk[:, 0:128], ctxs[:, tt, kv * 128:(kv + 1) * 128], ident_f[:], [ctxs, ident_f], [bk])
                    B.cp(ctxk[:, kv, tt * 128:(tt + 1) * 128], bk[:, 0:128], [bk], [ctxk])
            for kv in range(2):
                for par in range(2):
                    P.dma("pool", ctxv[:, :, kv * 2 + par, par * 64:par * 64 + 64],
                          vsrc.rearrange("(tt p) c -> p tt c", p=128)[:, :, kv * 64:(kv + 1) * 64], writes=[ctxv])

    def na_attention(l, v, is_sample, sbs, ntot, kout, vout, hook=None):
        qk = [SC[1], SC[2], SC[3]]

        def qk_ap(ci):
            return bfv(qk[ci // 2])[:, (ci % 2) * 2048:(ci % 2) * 2048 + ntot], qk[ci // 2]
        vp = [SC[4], SC[5], SC[6]]

        assert ntot <= 512

        def vpad(tt, h):
            n = tt * 8 + h
            return bfv(vp[0])[:, n * 128:(n + 1) * 128], vp[0]
        P.op("dve", lambda e: e.memset(vp[0][:, :], 0.0), [], [vp[0]])
        off = [0]
        for (ntok, _) in sbs:
            off.append(off[-1] + ntok)

        def mk_q(ci):
            def h(si, ntok, bk):
                ap, t = qk_ap(ci)
                return [headnorm(bk, ntok, dv[l][:, 0:1], l, ap[:, off[si]:off[si] + ntok], t)]
            return h

        def mk_k(ci):
            def h(si, ntok, bk):
                ap, t = qk_ap(3 + ci)
                if kout is None:
                    return [headnorm(bk, ntok, dv[l][:, 1:2], l, ap[:, off[si]:off[si] + ntok], t)]
                fin = headnorm(bk, ntok, dv[l][:, 1:2], l, kstg[:, :ntok], kstg)

                def st0():
                    fin()
                    B.cp(ap[:, off[si]:off[si] + ntok], kstg[:, :ntok], [kstg], [t], eng="dve")
                return [st0, lambda: kout(si, ntok, ci, kstg)]
            return h

        def vh(si, tt, bk):
            tg = off[si] // 128 + tt
            dstv = bfv(vp[0])[:, tg * 8 * 128:(tg * 8 + 8) * 128].rearrange("p (s c) -> p s c", c=128)
            srcv = bk[:, 0:384].rearrange("p (h c) -> p h c", c=64)
            for par in range(2):
                B.cp(dstv[:, par:6:2, par * 64:par * 64 + 64], srcv[:, par:6:2, :], [bk], [vp[0]], eng=("act" if par else "dve"))
            if vout is not None:
                vout(si, tt, bk, 384)
        specs = [([((6 + c) * 128, 128)], mk_q(c)) for c in range(3)] + [([((9 + c) * 128, 128)], mk_k(c)) for c in range(3)]
        project(l, v, sbs, specs, [([(12 * 128, 384)], 384, vh)], skip_norm=True)
        if hook is not None:
            hook()
        assert ntot <= 1024
        mixA = SC[7]

        def mix_ap(c):
            return bfv(mixA)[:, c * 1024:c * 1024 + ntot], mixA
        if is_sample:
            load_ctx(l, "na")
        for h2 in range(3):
            q_t = qk_ap(h2)
            k_t = qk_ap(3 + h2)
            rd = [q_t[1], k_t[1]] + vp + [ctxk, ctxv]
            map_, mt = mix_ap(h2)
            if not is_sample:
                for s in range(ntot // 256):
                    pairs = [(0, 256, s * 2 + j, None, None) for j in range(2)]
                    attn_qblock(256,
                                lambda hh, c0, n, s=s: q_t[0][hh * 64:(hh + 1) * 64, s * 256 + c0:s * 256 + c0 + n],
                                lambda hh, kt: k_t[0][hh * 64:(hh + 1) * 64, kt * 128:(kt + 1) * 128],
                                lambda hh, kt: vpad(kt, 2 * h2 + hh)[0],
                                [], pairs, hone, None, None, map_[:, s * 256:(s + 1) * 256], mt, rd)
            else:
                na_sample_chunk(l, h2, q_t, k_t, vpad, map_, mt, rd)
        return [lambda si, c=c: (mix_ap(c)[0][:, off[si]:off[si + 1]], mix_ap(c)[1]) for c in range(3)]

    def na_sample_chunk(l, h2, q_t, k_t, vpad, map_, mt, rd):
        for hh in range(2):
            P.dma("pool", nabs[hh][:, :], nab[l, 2 * h2 + hh], writes=[nabs[hh]])
        for QB in range(4):
            pairs_by_head = []
            for hh in range(2):
                prs = []
                for mi in range(4):
                    m = QB * 4 + mi
                    for (j, pid) in na_pairs_for_m(m):
                        prs.append((mi * 128, 128, j, pid))
                pairs_by_head.append(prs)
            attn_qblock_na(l, h2, QB, pairs_by_head, q_t, k_t, vpad, map_, mt, rd)

    def attn_qblock_na(l, h2, QB, pairs_by_head, q_t, k_t, vpad, map_, mt, rd):
        acc, den = banks[6], banks[7]
        first = True
        nq = 512
        q0 = QB * 512
        sbi = 0
        for hh in range(2):
            h = 2 * h2 + hh
            nb_t = nabs[hh]
            nbv = nb_t[:, :].rearrange("p (a b) -> p a b", b=128)
            qv = q_t[0][hh * 64:(hh + 1) * 64, :]
            kv_ = k_t[0][hh * 64:(hh + 1) * 64, :]
            for tt in range(2):
                Sb = banks[sbi % 4]
                sbi += 1
                B.mm(Sb[:, :nq], ctxk[hh * 64:(hh + 1) * 64, h2, tt * 128:(tt + 1) * 128], qv[:, q0:q0 + nq], True, True, rd, [Sb])
                pt = nxt("pt", pts)
                B.act(pt[:, :nq], Sb[:, :nq], AF.Exp, [Sb], [pt])
                B.mm(acc[:, :nq], ctxv[:, tt, h, :], pt[:, :nq], first, False, [ctxv, pt], [acc])
                B.mm(den[:, :nq], hone(hh), pt[:, :nq], first, False, [hones_b, pt], [den])
                first = False
            prs = pairs_by_head[hh]
            for g0 in range(0, len(prs), 4):
                grp = prs[g0:g0 + 4]
                Sb = banks[sbi % 4]
                sbi += 1
                for i, (qc0, n, j, pid) in enumerate(grp):
                    B.mm(Sb[:, i * 128:(i + 1) * 128], kv_[:, j * 128:(j + 1) * 128], qv[:, q0 + qc0:q0 + qc0 + n], i == 0, False, rd, [Sb])
                    B.mm(Sb[:, i * 128:(i + 1) * 128], ident_b[:], nbv[:, pid, :], False, True, [ident_b, nb_t], [Sb])
                pt = nxt("pt", pts)
                w = len(grp) * 128
                B.act(pt[:, :w], Sb[:, :w], AF.Exp, [Sb], [pt])
                for i, (qc0, n, j, pid) in enumerate(grp):
                    vap, vt_ = vpad(j, h)
                    B.mm(acc[:, qc0:qc0 + n], vap, pt[:, i * 128:(i + 1) * 128], False, False, [vt_, pt], [acc])
                    B.mm(den[:, qc0:qc0 + n], hone(hh), pt[:, i * 128:(i + 1) * 128], False, False, [hones_b, pt], [den])
        r = nxt("tmp", tmpf)
        P.op("dve", lambda e: e.reciprocal(r[:, :nq], den[:, :nq]), [den], [r])
        B.tt(map_[:, q0:q0 + nq], acc[:, :nq], r[:, :nq], ALU.mult, [acc, r], [mt])

    def gq_attention(l, v, is_sample, sbs, ntot, kout, vout, after_proj=None):
        qkt = [SC[1], SC[2], SC[3]]

        def qk_ap(ci):
            return bfv(qkt[ci // 2])[:, (ci % 2) * 2048:(ci % 2) * 2048 + ntot], qkt[ci // 2]
        vp = [SC[4], SC[5]]

        def vpad(tt, kv, par):
            n = tt * 4 + kv * 2 + par
            return bfv(vp[n // 32])[:, (n % 32) * 128:(n % 32 + 1) * 128], vp[n // 32]
        off = [0]
        for (ntok, _) in sbs:
            off.append(off[-1] + ntok)

        def rope_to(src_f32_t, ntok, c0, out_ap, out_t):
            rb = banks[4 + _rr["sb"] % 2]
            B.mm(rb[:, :ntok], rmat[:], src_f32_t[:, :ntok], True, True, [rmat, src_f32_t], [rb])
            t1 = nxt("tmp", tmpf)
            B.tt(t1[:, :ntok], src_f32_t[:, :ntok], ropes[0][:, :ntok], ALU.mult, [src_f32_t, ropes[0]], [t1])
            t2 = nxt("tmp", tmpf)
            B.tt(t2[:, :ntok], rb[:, :ntok], ropes[1][:, :ntok], ALU.mult, [rb, ropes[1]], [t2])
            B.tt(out_ap, t1[:, :ntok], t2[:, :ntok], ALU.add, [t1, t2], [out_t])

        def mk_qk(ci, gcol, is_k_out):
            def h(si, ntok, bk):
                ap, t = qk_ap(ci)
                dst = ap[:, off[si]:off[si] + ntok]
                assert not is_sample
                return [headnorm(bk, ntok, dv[l][:, gcol:gcol + 1], l, dst, t)]
            return h

        def kout_h(si, ntok, bk):
            fin = headnorm(bk, ntok, dv[l][:, 3:4], l, kstg[:, :ntok], kstg)
            return [fin, lambda: kout(si, ntok, 0, kstg)]

        def vh(si, tt, bk):
            tg = off[si] // 128 + tt
            n0 = tg * 4
            t = vp[n0 // 32]
            dstv = bfv(t)[:, (n0 % 32) * 128:(n0 % 32 + 4) * 128].rearrange("p (s c) -> p s c", c=128)
            srcv = bk[:, 0:128].rearrange("p (k c) -> p k c", c=64)
            for par in range(2):
                B.cp(dstv[:, par:4:2, par * 64:par * 64 + 64], srcv[:, :, :], [bk], [t], eng=("act" if par else "dve"))
            if vout is not None:
                vout(si, tt, bk, 128)
        kc0 = 18 * 128
        specs = [([((15 + c) * 128, 128)], mk_qk(c, 2, False)) for c in range(3)]
        specs += [([(kc0 + kv * 64, 64), (kc0 + kv * 64, 64)], mk_qk(3 + kv, 3, False)) for kv in range(2)]
        if kout is not None:
            specs += [([(kc0, 128)], kout_h)]
        npf = project_prefetch(l, specs, 8)
        yield
        for t in vp:
            P.op("dve", lambda e, t=t: e.memset(t[:, :], 0.0), [], [t])
        project(l, v, sbs, specs, [([(19 * 128, 128)], 128, vh)], hT=SC[0], skip_norm=True, prefetched=npf)
        if after_proj is not None:
            after_proj()
        mixA, mixB = SC[6], SC[3]

        def mix_ap(c):
            if c < 2:
                return bfv(mixA)[:, c * 2048:c * 2048 + ntot], mixA
            return bfv(mixB)[:, 2048:2048 + ntot], mixB
        if is_sample:
            load_ctx(l, "gq")
        for h2 in range(3):
            q_t = qk_ap(h2)
            map_, mt = mix_ap(h2)

            def kvof(hh):
                return (2 * h2 + hh) // 3
            rd = [q_t[1], qkt[1], qkt[2]] + vp + [ctxk, ctxv]
            sink_ap = dv[l][:, 6 + h2:7 + h2]
            if not is_sample:
                for s in range(ntot // 256):
                    pairs = [(0, 256, s * 2 + j, None, None) for j in range(2)]
                    attn_qblock(256,
                                lambda hh, c0, n, s=s: q_t[0][hh * 64:(hh + 1) * 64, s * 256 + c0:s * 256 + c0 + n],
                                lambda hh, kt: qk_ap(3 + kvof(hh))[0][hh * 64:(hh + 1) * 64, kt * 128:(kt + 1) * 128],
                                lambda hh, kt: vpad(kt, kvof(hh), hh)[0],
                                [], pairs, hone, sink_ap, dv[l], map_[:, s * 256:(s + 1) * 256], mt, rd)
            else:
                for QB in range(4):
                    pairs = []
                    for ni in range(4):
                        n_ = QB * 4 + ni
                        for j in (n_ - 1, n_, n_ + 1):
                            if 0 <= j <= 15:
                                bias = None if j == n_ else (tri_b[:, 0:128] if j < n_ else tri_b[:, 128:256])
                                pairs.append((ni * 128, 128, j, bias, tri_b))
                    ctx = [(lambda hh, tt=tt: ctxk[hh * 64:(hh + 1) * 64, kvof(hh), tt * 128:(tt + 1) * 128],
                            lambda hh, tt=tt: ctxv[:, tt, kvof(hh) * 2 + hh, :]) for tt in range(2)]
                    attn_qblock(512,
                                lambda hh, c0, n, QB=QB: q_t[0][hh * 64:(hh + 1) * 64, QB * 512 + c0:QB * 512 + c0 + n],
                                lambda hh, kt: qk_ap(3 + kvof(hh))[0][hh * 64:(hh + 1) * 64, kt * 128:(kt + 1) * 128],
                                lambda hh, kt: vpad(kt, kvof(hh), hh)[0],
                                ctx, pairs, hone, sink_ap, dv[l], map_[:, QB * 512:(QB + 1) * 512], mt, rd)
        return [lambda si, c=c: (mix_ap(c)[0][:, off[si]:off[si + 1]], mix_ap(c)[1]) for c in range(3)]

    def mixer(l, v, is_sample, sbs, ntot, L, seqs, outs, hook=None):
        off = [0]
        for (ntok, _) in sbs:
            off.append(off[-1] + ntok)
        assert len(sbs) == 1
        ntok0, xk0 = sbs[0]
        norm_mod(xk0, ntok0, l, 1, v, [(bfv(SC[0])[:, k * 512:k * 512 + ntok0], SC[0]) for k in range(8)])
        filter_mlp(l, L)
        for cc in range(2):
            hyena(l, v, cc, L, seqs, sbs, w2s[cc])
        mixes = [lambda si, c=c: (w2s[c][:, :, :].rearrange("p a b -> p (a b)")[:, off[si]:off[si + 1]], w2s[c]) for c in range(2)]
        if outs is None:
            raise AssertionError("prompt-only path")
        else:
            o_k_na, o_v_na, o_k_gq, o_v_gq = outs

            def mk_kout(odram, ncol):
                def kout(si, ntok, ci, src_t):
                    for tt in range(ntok // 128):
                        tg = off[si] // 128 + tt
                        bk = banks[4 + tt % 2]
                        B.tr(bk[:, 0:128], src_t[:, tt * 128:(tt + 1) * 128], ident_f[:], [src_t, ident_f], [bk])
                        st = nxt("tmp", tmpf)
                        B.cp(st[:, 0:128], bk[:, 0:128], [bk], [st], eng="act")
                        seq, tl = (tg * 128) // 256, (tg * 128) % 256
                        P.store("sp", odram.h.ap()[seq, l, tl:tl + 128, ci * 128:(ci + 1) * 128], st[:, 0:128], st)
                return kout

            def mk_vout(odram):
                def vout(si, tt, bk, ncol):
                    tg = off[si] // 128 + tt
                    st = nxt("tmp", tmpf)
                    B.cp(st[:, 0:ncol], bk[:, 0:ncol], [bk], [st], eng="act")
                    seq, tl = (tg * 128) // 256, (tg * 128) % 256
                    P.store("sp", odram.h.ap()[seq, l, tl:tl + 128, :], st[:, 0:ncol], st)
                return vout
            def after_gq_proj():
                wout_load(l)
                if hook is not None:
                    hook()
            gq = gq_attention(l, v, is_sample, sbs, ntot, mk_kout(o_k_gq, 128), mk_vout(o_v_gq), after_proj=after_gq_proj)

            def mid():
                next(gq)
            mixes += na_attention(l, v, is_sample, sbs, ntot, mk_kout(o_k_na, 384), mk_vout(o_v_na), hook=mid)
            try:
                next(gq)
                raise AssertionError("gq_attention should have finished")
            except StopIteration as e_:
                mixes += e_.value
        wout_all(l, v, mixes, sbs, loaded=True)

    NPA = 22
    NA_W = [list(range(0, 6)), list(range(1, 6)), list(range(2, 7)), list(range(2, 8))]
    GROUPS = [[0, 1, 2, 3], [4, 5, 6, 7]]
    xoffs = B.dram_in("xoffs", [1, 8], mybir.dt.int32)
    c_edge = B.dram_in("edge", [128, 2])
    c_ropeCo = B.dram_in("ropeCo", [128, 512])
    c_ropeSo = B.dram_in("ropeSo", [128, 512])
    c_icosO = B.dram_in("icosO", [FPL, 512], BF16)
    c_isinO = B.dram_in("isinO", [FPL, 512], BF16)
    nabA = B.dram_in("nabA", [DEPTH, 6, 128, NPA * 128])
    c_triA = B.dram_in("triA", [128, 12 * 128])
    ex1 = [[T(nc.dram_tensor("ex1_%d_%d" % (l, cc), [384, 512], F32), "ex1") for cc in range(2)] for l in range(DEPTH)]
    g1 = [[T(nc.dram_tensor("g1_%d_%d" % (l, cc), [4 * 384, 512], F32), "g1") for cc in range(2)] for l in range(DEPTH)]
    ex2 = [T(nc.dram_tensor("ex2_%d" % l, [1024, 512], BF16), "ex2") for l in range(DEPTH)]
    g2 = [T(nc.dram_tensor("g2_%d" % l, [4 * 1024, 512], BF16), "g2") for l in range(DEPTH)]
    edge = B.sb([128, 2], F32, "edge")
    P.dma("sp", edge[:], c_edge, writes=[edge])
    triA = B.sb([128, 12 * 128], BF16, "triA")
    P.dma("pool", triA[:], c_triA, writes=[triA])
    nabAs = [B.sb([128, NPA * 128], BF16, "nabA%d" % i) for i in range(2)]
    hal = B.sb([128, 3, 16], F32, "hal")
    DYN = {}

    def dyn_init(e):
        for i, nm in enumerate(("b2", "o2", "a2", "b1", "a1")):
            reg = e.alloc_register("r_" + nm)
            ins = e.reg_load(reg, xoffs[0:1, i:i + 1])
            DYN[nm] = e.snap(reg, min_val=0, max_val=(3 * 1024 if nm[1] == "2" else 3 * 384))
        return ins
    P.op("sp", dyn_init, [], [])

    def dyn_dma(out, src_fn, reads, writes, **kw):
        return P.op("sp", lambda e: e.dma_start(out=out, in_=src_fn(), **kw), reads, writes, dma_trk=writes[0])

    c_perm = B.dram_in("permdup", [128, 256])
    perm_b = B.sb([128, 256], BF16, "perm_b")
    P.dma("pool", perm_b[:], c_perm, writes=[perm_b])

    SA = {}

    def mixer_sample_A1(l, xk):
        v = 1
        vt = vecT[l]
        sbs = [(512, xk)]
        hT = SC[7]
        Ut = [SC[8], SC[9]]

        def Uown(ch):
            return Ut[ch // 4][:, (ch % 4) * 512:(ch % 4 + 1) * 512], Ut[ch // 4]
        qna_t, qgq_t = SC[10], SC[11]
        qna = lambda c: bfv(qna_t)[:, c * 512:(c + 1) * 512]
        ksend = bfv(qna_t)[:, 2048:4096].rearrange("p (a b) -> p a b", b=512)
        qgq = lambda c: bfv(qgq_t)[:, c * 512:(c + 1) * 512]
        vsend = bfv(qgq_t)[:, 2048:4096].rearrange("p (a b) -> p a b", b=512)

        def h_u(ch):
            def h(si, ntok, bk):
                ap, t = Uown(ch)
                B.cp(ap, bk[:, :ntok], [bk], [t], eng=("act" if ch % 2 else "dve"))
                cc, ui = ch % 2, ch // 2
                P.dma("sp", ex1[l][cc].h.ap()[ui * 128:(ui + 1) * 128, :], ap, reads=[t], writes=[ex1[l][cc]], sem_on=t)
            return h

        def h_qna(c):
            def h(si, ntok, bk):
                return [headnorm(bk, ntok, dv[l][:, 0:1], l, qna(c), qna_t)]
            return h

        def h_kna(c):
            def h(si, ntok, bk):
                return [headnorm(bk, ntok, dv[l][:, 1:2], l, ksend[:, c, :], qna_t)]
            return h

        def rope_to(src_t, ntok, out_ap, out_t):
            rb = banks[4 + _rr["sb"] % 2]
            B.mm(rb[:, :ntok], rmat[:], src_t[:, :ntok], True, True, [rmat, src_t], [rb])
            t1 = nxt("tmp", tmpf)
            B.tt(t1[:, :ntok], src_t[:, :ntok], ropes[0][:, :ntok], ALU.mult, [src_t, ropes[0]], [t1])
            t2 = nxt("tmp", tmpf)
            B.tt(t2[:, :ntok], rb[:, :ntok], ropes[1][:, :ntok], ALU.mult, [rb, ropes[1]], [t2])
            B.tt(out_ap, t1[:, :ntok], t2[:, :ntok], ALU.add, [t1, t2], [out_t])

        def h_qgq(c):
            def h(si, ntok, bk):
                if c == 0:
                    P.dma("sp", ropes[0][:, :], c_ropeCo, writes=[ropes[0]])
                    P.dma("sp", ropes[1][:, :], c_ropeSo, writes=[ropes[1]])
                fin = headnorm(bk, ntok, dv[l][:, 2:3], l, kstg[:, :ntok], kstg)
                return [fin, lambda: rope_to(kstg, ntok, qgq(c), qgq_t)]
            return h

        def h_kgq(si, ntok, bk):
            fin = headnorm(bk, ntok, dv[l][:, 3:4], l, kstg[:, :ntok], kstg)
            return [fin, lambda: rope_to(kstg, ntok, ksend[:, 3, :], qna_t)]

        def h_v(si, tt, bk):
            B.cp(vsend[:, tt, :], bk[:, 0:512], [bk], [qgq_t], eng=("act" if tt % 2 else "dve"))
        specs = [([(ch * 128, 128)], h_u(ch)) for ch in range(6)]
        specs += [([((6 + c) * 128, 128)], h_qna(c)) for c in range(3)]
        specs += [([((9 + c) * 128, 128)], h_kna(c)) for c in range(3)]
        specs += [([((15 + c) * 128, 128)], h_qgq(c)) for c in range(3)]
        specs += [([(18 * 128, 128)], h_kgq)]
        project(l, v, sbs, specs, [([(12 * 128, 384), (19 * 128, 128)], 512, h_v)], hT=hT)
        for c in range(4):
            P.dma("sp", ex2[l].h.ap()[c * 128:(c + 1) * 128, :], ksend[:, c, :], reads=[qna_t], writes=[ex2[l]], sem_on=qna_t)
        for tt in range(4):
            P.dma("sp", ex2[l].h.ap()[512 + tt * 128:512 + (tt + 1) * 128, :], vsend[:, tt, :], reads=[qgq_t], writes=[ex2[l]], sem_on=qgq_t)
        def exchange(ccs=(0,)):
            for cc in ccs:
                P.collective(lambda e, cc=cc: e.collective_compute("AllGather", ALU.bypass, replica_groups=GROUPS,
                                                                   ins=[ex1[l][cc].h.ap().opt()], outs=[g1[l][cc].h.ap().opt()]),
                             [ex1[l][cc]], [g1[l][cc]])

        def exchange2():
            exchange(ccs=(1,))
            P.collective(lambda e: e.collective_compute("AllGather", ALU.bypass, replica_groups=GROUPS,
                                                        ins=[ex2[l].h.ap().opt()], outs=[g2[l].h.ap().opt()]),
                         [ex2[l]], [g2[l]])
        SA[("x", l)] = exchange
        SA[("x2", l)] = exchange2
        SA[l] = (Uown, qna, qna_t, qgq, qgq_t)

    def mixer_sample_A2(l, xk):
        v = 1
        vt = vecT[l]
        sbs = [(512, xk)]
        Uown, qna, qna_t, qgq, qgq_t = SA[l]
        def load_full(cc):
            Gx = g1[l][cc]
            for rk in range(4):
                P.dma("sp", SC[1][:, rk * 512:(rk + 1) * 512], Gx.h.ap()[rk * 384 + 128:rk * 384 + 256, :], reads=[Gx], writes=[SC[1]])
                P.dma("sp", SC[2][:, rk * 512:(rk + 1) * 512], Gx.h.ap()[rk * 384 + 256:rk * 384 + 384, :], reads=[Gx], writes=[SC[2]])
        load_full(0)
        filter_mlp(l, 2048)
        mixes = []
        for cc in range(2):
            G = g1[l][cc]
            Ga = G.h.ap()
            dyn_dma(hal[:, :, 0:8], lambda Ga=Ga: Ga[bass.ds(DYN["b1"], 384), 504:512].rearrange("(u p) t -> p u t", p=128), [G], [hal])
            dyn_dma(hal[:, :, 8:16], lambda Ga=Ga: Ga[bass.ds(DYN["a1"], 384), 0:8].rearrange("(u p) t -> p u t", p=128), [G], [hal])
            B.ts(hal[:, :, 7], hal[:, :, 7], edge[:, 0:1], None, ALU.mult, None, [hal, edge], [hal])
            B.ts(hal[:, :, 8], hal[:, :, 8], edge[:, 1:2], None, ALU.mult, None, [hal, edge], [hal])
            if DBG.get("hal") and l == 0 and cc == 0:
                P.store("sp", dbg_hal.h.ap(), hal[:, :, :].rearrange("p a b -> p (a b)"), hal)
                P.store("sp", dbg_u.h.ap(), Ut[0][:, :], Ut[0])
            own = SC[0]
            ownv = own[:, :].rearrange("p (a b) -> p a b", b=512)
            for ui in range(3):
                ch = 2 * ui + cc
                wcol = lambda tap: vt[:, VB_CW + tap * 6 + ch:VB_CW + tap * 6 + ch + 1]
                bcol = vt[:, VB_CB + ch:VB_CB + ch + 1]
                uap, ut = Uown(ch)
                o_ = ownv[:, ui, :]
                B.act(o_, uap, AF.Identity, [ut, vt], [own], bias=bcol, scale=wcol(1))
                B.stt(o_[:, 1:512], uap[:, 0:511], wcol(0), o_[:, 1:512], ALU.mult, ALU.add, [ut, vt, own], [own])
                B.stt(o_[:, 0:511], uap[:, 1:512], wcol(2), o_[:, 0:511], ALU.mult, ALU.add, [ut, vt, own], [own])
                B.stt(o_[:, 0:1], hal[:, ui, 7:8], wcol(0), o_[:, 0:1], ALU.mult, ALU.add, [hal, vt, own], [own])
                B.stt(o_[:, 511:512], hal[:, ui, 8:9], wcol(2), o_[:, 511:512], ALU.mult, ALU.add, [hal, vt, own], [own])
            B.tt(ownv[:, 1, :], ownv[:, 1, :], ownv[:, 2, :], ALU.mult, [own], [own])
            x1f, vf, x1c, vc = SC[1], SC[2], SC[3], SC[4]
            if cc == 1:
                load_full(1)
            for (u_, o_, ui) in ((x1f, x1c, 1), (vf, vc, 2)):
                ch = 2 * ui + cc
                wcol = lambda tap, ch=ch: vt[:, VB_CW + tap * 6 + ch:VB_CW + tap * 6 + ch + 1]
                bcol = vt[:, VB_CB + ch:VB_CB + ch + 1]
                L = 2048
                B.act(o_[:, 0:L], u_[:, 0:L], AF.Identity, [u_, vt], [o_], bias=bcol, scale=wcol(1))
                B.stt(o_[:, 1:L], u_[:, 0:L - 1], wcol(0), o_[:, 1:L], ALU.mult, ALU.add, [u_, vt, o_], [o_])
                B.stt(o_[:, 0:L - 1], u_[:, 1:L], wcol(2), o_[:, 0:L - 1], ALU.mult, ALU.add, [u_, vt, o_], [o_])
            B.tt(x1c[:, :], x1c[:, :], vc[:, :], ALU.mult, [x1c, vc], [x1c])
            if cc == 0:
                SA[("x2", l)]()
            mix_t = w2s[cc]
            mixv = mix_t[:, :, :].rearrange("p a b -> p (a b)")[:, 0:512]

            def final(c0, tn, bk, cc=cc, ownv=ownv, own=own, mixv=mixv, mix_t=mix_t):
                t = nxt("tmp", tmpf)
                B.stt(t[:, :tn], ownv[:, 1, :], vt[:, VB_SK + cc:VB_SK + cc + 1], bk[:, :tn], ALU.mult, ALU.add,
                      [own, vt, bk], [t])
                B.tt(mixv, t[:, :tn], ownv[:, 0, :], ALU.mult, [t, own], [mix_t])
            hy_core(l, cc, 2048, [0], 2048, x1c, {'zb': SC[5], 'sd': SC[6], 'decF': SC[1], 'decB': SC[2], 'yt': SC[4], 'zst': SC[7]},
                    c_icosO, c_isinO, 512, final)
            mixes.append(lambda si, mixv=mixv, mix_t=mix_t: (mixv, mix_t))
        G2 = g2[l]
        G2a = G2.h.ap()
        segs = [("b2", 256, 256, 0), ("o2", 0, 512, 256), ("a2", 0, 256, 768)]
        kw_t = SC[0]
        KW = bfv(kw_t)[:, 0:4096].rearrange("p (a b) -> p a b", b=1024)
        for (rg, s0, n, w0) in segs:
            dyn_dma(KW[:, :, w0:w0 + n], lambda rg=rg, s0=s0, n=n: G2a[bass.ds(DYN[rg], 512), s0:s0 + n].rearrange("(c p) t -> p c t", p=128),
                    [G2], [kw_t])
        vw_t = SC[4]
        VW = bfv(vw_t)[:, 0:4096].rearrange("p (a b) -> p a b", b=512)
        Gv = G2a[512:4096, :]
        vsegs = [("b2", 256, 2, 0), ("o2", 0, 4, 2), ("a2", 0, 2, 6)]
        for (rg, r0, nt, w0) in vsegs:
            if r0:
                src = lambda rg=rg, r0=r0, nt=nt: Gv[bass.ds(DYN[rg] + r0, nt * 128), :].rearrange("(w p) c -> p w c", p=128)
            else:
                src = lambda rg=rg, nt=nt: Gv[bass.ds(DYN[rg], nt * 128), :].rearrange("(w p) c -> p w c", p=128)
            dyn_dma(VW[:, w0:w0 + nt, :], src, [G2], [vw_t])
        vp_t = [SC[1], SC[2]]

        def vpadA(w, h):
            n = w * 8 + h
            return bfv(vp_t[n // 32])[:, (n % 32) * 128:(n % 32 + 1) * 128], vp_t[n // 32]
        for t in vp_t:
            P.op("dve", lambda e, t=t: e.memset(t[:, :], 0.0), [], [t])
        ci = 0
        for w in range(8):
            t = vp_t[(w * 8) // 32]
            base = ((w * 8) % 32) * 128
            dstv = bfv(t)[:, base:base + 8 * 128].rearrange("p (s c) -> p s c", c=128)
            srcv = VW[:, w, 0:384].rearrange("p (h c) -> p h c", c=64)
            for par in range(2):
                B.cp(dstv[:, par:6:2, par * 64:par * 64 + 64], srcv[:, par:6:2, :], [vw_t], [t], eng=("act" if ci % 2 else "dve"))
                ci += 1
        kg_t = SC[6]
        KG = bfv(kg_t)[:, 0:2048].rearrange("p (a b) -> p a b", b=1024)
        for kv in range(2):
            for hf in range(2):
                bk = banks[(kv * 2 + hf) % 4]
                B.mm(bk[:, :512], perm_b[:, kv * 128:(kv + 1) * 128], KW[:, 3, hf * 512:(hf + 1) * 512], True, True, [perm_b, kw_t], [bk])
                B.cp(KG[:, kv, hf * 512:(hf + 1) * 512], bk[:, :512], [bk], [kg_t], eng=("act" if hf else "dve"))
        vg_t = SC[5]

        def vpadG(w, kv, par):
            n = w * 4 + kv * 2 + par
            return bfv(vg_t)[:, n * 128:(n + 1) * 128], vg_t
        P.op("dve", lambda e: e.memset(vg_t[:, :], 0.0), [], [vg_t])
        for w in range(8):
            dstv = bfv(vg_t)[:, w * 4 * 128:(w * 4 + 4) * 128].rearrange("p (s c) -> p s c", c=128)
            srcv = VW[:, w, 384:512].rearrange("p (k c) -> p k c", c=64)
            for par in range(2):
                B.cp(dstv[:, par:4:2, par * 64:par * 64 + 64], srcv[:, :, :], [vw_t], [vg_t], eng=("act" if ci % 2 else "dve"))
                ci += 1
        load_ctx(l, "na")
        mixna_t = SC[3]
        for h2 in range(3):
            for hh in range(2):
                P.dma("pool", nabAs[hh][:, :], nabA[l, 2 * h2 + hh], writes=[nabAs[hh]])
            groups = []
            for hh in range(2):
                h = 2 * h2 + hh
                nbv = nabAs[hh][:, :].rearrange("p (a b) -> p a b", b=128)
                qv = qna(h2)[hh * 64:(hh + 1) * 64, :]
                for tt in range(2):
                    groups.append([(0, 512, ctxk[hh * 64:(hh + 1) * 64, h2, tt * 128:(tt + 1) * 128], qv, ctxv[:, tt, h, :], hone(hh),
                                    None, None, [ctxk, ctxv, qna_t])])
                prs = []
                pid = 0
                for mi in range(4):
                    for w in NA_W[mi]:
                        vap, vt_ = vpadA(w, h)
                        prs.append((mi * 128, 128, KW[hh * 64:(hh + 1) * 64, h2, w * 128:(w + 1) * 128], qv[:, mi * 128:(mi + 1) * 128],
                                    vap, hone(hh), nbv[:, pid, :], nabAs[hh], [kw_t, qna_t, vt_]))
                        pid += 1
                assert pid == NPA
                for g0 in range(0, len(prs), 4):
                    groups.append(prs[g0:g0 + 4])
            mo = bfv(mixna_t)[:, h2 * 512:(h2 + 1) * 512]
            run_attention(groups, 512, None, None, mo, mixna_t)
            mixes.append(lambda si, mo=mo: (mo, mixna_t))
        load_ctx(l, "gq")
        mixgq_t = SC[6]
        for h2 in range(3):
            def kvof(hh):
                return (2 * h2 + hh) // 3
            pairs = []
            for ni in range(4):
                for dj in range(3):
                    bias = None if dj == 1 else triA[:, (ni * 3 + dj) * 128:(ni * 3 + dj + 1) * 128]
                    pairs.append((ni * 128, 128, ni + 1 + dj, bias, triA))
            ctx = [(lambda hh, tt=tt: ctxk[hh * 64:(hh + 1) * 64, kvof(hh), tt * 128:(tt + 1) * 128],
                    lambda hh, tt=tt: ctxv[:, tt, kvof(hh) * 2 + hh, :]) for tt in range(2)]
            mo = bfv(mixgq_t)[:, 2048 + h2 * 512:2048 + (h2 + 1) * 512]
            attn_qblock(512,
                        lambda hh, c0, n: qgq(h2)[hh * 64:(hh + 1) * 64, c0:c0 + n],
                        lambda hh, w: KG[hh * 64:(hh + 1) * 64, kvof(hh), w * 128:(w + 1) * 128],
                        lambda hh, w: vpadG(w, kvof(hh), hh)[0],
                        ctx, pairs, hone, dv[l][:, 6 + h2:7 + h2], dv[l], mo, mixgq_t, [qgq_t, kg_t, vg_t, ctxk, ctxv])
            mixes.append(lambda si, mo=mo: (mo, mixgq_t))
        wout_all(l, v, mixes, sbs)

    for l in range(DEPTH):
        ada(l)

    def xp_k(k):
        return big[k // 4][:, (k % 4) * 512:(k % 4 + 1) * 512], big[k // 4]

    def xs_k(k):
        return big[2 + k // 4][:, (k % 4) * 512:(k % 4 + 1) * 512], big[2 + k // 4]
    load_x(xs, 4, lambda k, t4: (xs_k(k)[0], [big[2 + k // 4]]))
    load_x(xp, 4, lambda k, t4: (xp_k(k)[0], [big[k // 4]]))
    pblk = [(512, [xp_k(k) for k in range(8)])]
    sblk1 = [(512, [xs_k(k) for k in range(8)])]
    both = [(512, sblk1[0][1], 1), (512, pblk[0][1], 0)]
    for l in range(DEPTH):
        ffn(l, 0, both)
        mixer_sample_A1(l, sblk1[0][1])
        mixer(l, 0, False, pblk, 512, 256, [0, 256], (o_nk, o_nv, o_gk, o_gv), hook=SA[("x", l)])
        mixer_sample_A2(l, sblk1[0][1])
        ffn(l, 1, both)
    store_x(yp, 4, lambda k, tt: (big[k // 4][:, (k % 4) * 512 + tt * 128:(k % 4) * 512 + (tt + 1) * 128], big[k // 4]))
    store_x(ys, 4, lambda k, tt: (big[2 + k // 4][:, (k % 4) * 512 + tt * 128:(k % 4) * 512 + (tt + 1) * 128], big[2 + k // 4]))

    P.emit()
    return B


_CACHE = {}


def _bf16(a):
    import ml_dtypes
    return np.asarray(a, np.float32).astype(ml_dtypes.bfloat16)


def _consts():
    if "c" in _CACHE:
        return _CACHE["c"]
    c = {}
    c["ident"] = np.eye(128, dtype=np.float32)
    c["onesc"] = np.full((128, 128), 1.0 / 1024.0, np.float32)
    bo = np.zeros((128, 128), np.float32)
    bo[0:64, 0:64] = 1.0 / 64.0
    bo[64:128, 64:128] = 1.0 / 64.0
    c["bones"] = bo
    ho = np.zeros((128, 256), np.float32)
    ho[:, 0:64] = 1.0
    ho[:, 128 + 64:256] = 1.0
    c["hones"] = ho
    ki = np.arange(128)[:, None]
    qi = np.arange(128)[None, :]
    tri = np.zeros((128, 256), np.float32)
    tri[:, 0:128] = np.where(qi <= ki, 0.0, NEG)
    tri[:, 128:256] = np.where(ki <= qi, 0.0, NEG)
    c["tri"] = tri
    rm = np.zeros((128, 128), np.float32)
    for base in (0, 32, 64, 96):
        for d in range(16):
            rm[base + d + 16, base + d] = -1.0
            rm[base + d, base + d + 16] = 1.0
    c["rmat"] = rm
    pd = np.zeros((128, 256), np.float32)
    for d in range(64):
        pd[d, d] = 1.0
        pd[d, 64 + d] = 1.0
        pd[64 + d, 128 + d] = 1.0
        pd[64 + d, 128 + 64 + d] = 1.0
    c["permdup"] = pd
    pos = np.arange(2048)
    inv = (np.float32(10000.0) ** (-(np.arange(16, dtype=np.float32)) / np.float32(16))).astype(np.float32)
    ang_r = (pos // 64).astype(np.float32)[:, None] * inv[None, :]
    ang_c = (pos % 64).astype(np.float32)[:, None] * inv[None, :]
    C = np.zeros((128, 2048), np.float32)
    S = np.zeros((128, 2048), np.float32)
    for d in range(128):
        dd = d % 64
        a = ang_r if dd < 32 else ang_c
        C[d] = np.cos(a[:, dd % 16]).astype(np.float32)
        S[d] = np.sin(a[:, dd % 16]).astype(np.float32)
    c["ropeC"], c["ropeS"] = C, S
    for L, tag, FP in ((2048, "L", 2176), (256, "P", 384)):
        N = 2 * L
        t = np.arange(L, dtype=np.int64)
        f = np.arange(FP, dtype=np.int64)
        ph = 2.0 * np.pi * ((t[:, None] * f[None, :]) % N).astype(np.float64) / N
        valid = (f <= L)[None, :]
        c["cos" + tag] = _bf16(np.where(valid, np.cos(ph), 0.0))
        c["sin" + tag] = _bf16(np.where(valid, np.sin(ph), 0.0))
        wf = np.where((f == 0) | (f == L), 1.0, 2.0) * (f <= L) / N
        c["icos" + tag] = _bf16((np.cos(ph) * wf[None, :]).T)
        c["isin" + tag] = _bf16((np.sin(ph) * wf[None, :]).T)
        idx = np.arange(L, dtype=np.float32)
        tn = idx / np.float32(L - 1)
        bands = np.linspace(1e-4, 15, 16, dtype=np.float32)
        ang = (np.float32(2.0 * math.pi / L)) * idx[:, None] * bands[None, :]
        feats = np.concatenate([tn[:, None], np.cos(ang), -np.sin(ang)], axis=-1).astype(np.float32)
        c["feats" + tag] = np.ascontiguousarray(feats.T)
        max_decay = math.log(1e-2) / 0.3
        min_decay = math.log(1e-2) / 1.5
        deltas = np.abs(np.linspace(min_decay, max_decay, 256, dtype=np.float32))
        dec = np.exp(-tn[:, None] * deltas[None, :]).astype(np.float32)
        decb = dec.copy()
        decb[0, :] = 0.0
        c["dec" + tag] = np.stack([dec, decb]).astype(np.float32)
    def rs(r):
        return min(max(r - 4, 0), 24)
    pats = []
    for p in range(NPAT):
        if p < 5:
            m, j = 6, 6 + (p - 2)
        else:
            e, jj = divmod(p - 5, 4)
            m = [0, 1, 14, 15][e]
            j = (0 if m < 2 else 12) + jj
        pats.append((m, j))
    idx = np.full((128, NPAT, 128), 15 * 31, np.int64)
    for p, (m, j) in enumerate(pats):
        for a in range(2):
            for b in range(2):
                kr, qr = 2 * j + a, 2 * m + b
                rowok = rs(qr) <= kr < rs(qr) + 8
                dr = kr - qr
                for cq in range(64):
                    wl = min(max(cq - 8, 0), 48)
                    for kc in range(wl, wl + 16):
                        if rowok:
                            idx[a * 64 + kc, p, b * 64 + cq] = (dr + 7) * 31 + (kc - cq + 15)
    c["nab_idx"] = idx.reshape(128, NPAT * 128)
    _CACHE["c"] = c
    return c


def _percore(r):
    key = ("pc", r)
    if key in _CACHE:
        return _CACHE[key]
    c = _consts()
    pc = {}
    rb, ra = max(r - 1, 0), min(r + 1, 3)
    pc["xoffs"] = np.array([[rb * 1024, r * 1024, ra * 1024, rb * 384, ra * 384, 0, 0, 0]], np.int32)
    pc["edge"] = np.tile(np.array([[0.0 if r == 0 else 1.0, 0.0 if r == 3 else 1.0]], np.float32), (128, 1))
    pc["ropeCo"] = np.ascontiguousarray(c["ropeC"][:, r * 512:(r + 1) * 512])
    pc["ropeSo"] = np.ascontiguousarray(c["ropeS"][:, r * 512:(r + 1) * 512])
    pc["icosO"] = np.ascontiguousarray(c["icosL"][:, r * 512:(r + 1) * 512])
    pc["isinO"] = np.ascontiguousarray(c["isinL"][:, r * 512:(r + 1) * 512])

    def rs(q):
        return min(max(q - 4, 0), 24)
    NA_W = [list(range(0, 6)), list(range(1, 6)), list(range(2, 7)), list(range(2, 8))]
    idx = np.full((128, 22, 128), 15 * 31, np.int64)
    p = 0
    for mi in range(4):
        for w in NA_W[mi]:
            m_abs, j_abs = 4 * r + mi, 4 * r - 2 + w
            for a in range(2):
                for b in range(2):
                    kr, qr = 2 * j_abs + a, 2 * m_abs + b
                    if kr < 0 or kr > 31 or not (rs(qr) <= kr < rs(qr) + 8):
                        continue
                    dr = kr - qr
                    for cq in range(64):
                        wl = min(max(cq - 8, 0), 48)
                        for kc in range(wl, wl + 16):
                            idx[a * 64 + kc, p, b * 64 + cq] = (dr + 7) * 31 + (kc - cq + 15)
            p += 1
    pc["nabA_idx"] = idx.reshape(128, 22 * 128)
    ki = np.arange(128)[:, None]
    qi = np.arange(128)[None, :]
    tlow = np.where(qi <= ki, 0.0, NEG).astype(np.float32)
    tup = np.where(ki <= qi, 0.0, NEG).astype(np.float32)
    tri = np.zeros((128, 12, 128), np.float32)
    for ni in range(4):
        for dj in range(3):
            j_abs = 4 * r + ni - 1 + dj
            if j_abs < 0 or j_abs > 15:
                tri[:, ni * 3 + dj, :] = NEG
            elif dj == 0:
                tri[:, ni * 3 + dj, :] = tlow
            elif dj == 2:
                tri[:, ni * 3 + dj, :] = tup
    pc["triA"] = tri.reshape(128, 12 * 128)
    _CACHE[key] = pc
    return pc


def _vecs(inp, l):
    v = np.zeros((128, 128), np.float32)
    v[0:72] = inp["ada_b"][l].reshape(72, 128)
    v[72:96] = inp["norm_g"][l].reshape(24, 128)
    v[96:114] = inp["hy_conv_w"][l].reshape(18, 128)
    v[114:120] = inp["hy_conv_b"][l].reshape(6, 128)
    v[120:122] = inp["hy_skip"][l].reshape(2, 128)
    for i, k in enumerate(("na_q_g", "na_k_g", "gqa_q_g", "gqa_k_g")):
        v[122 + i, 0:64] = inp[k][l]
        v[122 + i, 64:128] = inp[k][l]
    return v


def _vecs2(inp, l):
    v = np.zeros((16, 128), np.float32)
    v[0, 0:64] = inp["hy_freq"][l, 0]
    v[1, 0:64] = inp["hy_freq"][l, 1]
    v[2, 0:64] = inp["hy_filt_b1"][l]
    v[3, 0:64] = inp["hy_filt_b2"][l]
    for c in range(3):
        v[4 + c, 0:64] = inp["gqa_sink"][l, 2 * c]
        v[4 + c, 64:128] = inp["gqa_sink"][l, 2 * c + 1]
    return v


def kernel(**inp):
    inp = {k: np.asarray(v) for k, v in inp.items()}
    if "B" not in _CACHE:
        _CACHE["B"] = build()
    B = _CACHE["B"]
    c = _consts()
    vecs = np.stack([_vecs(inp, l) for l in range(DEPTH)])
    vecs2 = np.stack([_vecs2(inp, l) for l in range(DEPTH)])
    rp = inp["na_rpb"].reshape(DEPTH, 6, 15 * 31).astype(np.float32)
    rp = np.concatenate([rp, np.full((DEPTH, 6, 1), NEG, np.float32)], axis=-1)
    nab = np.ascontiguousarray(rp[:, :, c["nab_idx"]])
    shared = {
        "vecs": vecs, "vecs2": vecs2,
        "ffn_w1": inp["ffn_w1"], "ffn_w3": inp["ffn_w3"], "ffn_w2": inp["ffn_w2"],
        "w_in": inp["w_in"], "w_out": inp["w_out"],
        "hy_filt_w1": inp["hy_filt_w1"], "hy_filt_w2": inp["hy_filt_w2"], "hy_filt_w3": inp["hy_filt_w3"],
        "nab": nab,
    }
    for k in ("ident", "onesc", "bones", "hones", "tri", "rmat", "permdup", "ropeC", "ropeS", "cosL", "sinL", "icosL", "isinL",
              "cosP", "sinP", "icosP", "isinP", "featsL", "featsP", "decL", "decP"):
        shared[k] = c[k]
    in_maps = []
    for cid in range(8):
        b = cid // 4
        r = cid % 4
        pc = _percore(r)
        cv = np.concatenate([inp["c_ctx"].reshape(8, 128), inp["c"][b].reshape(8, 128)], 0).astype(np.float32)
        m = dict(shared)
        m["xp"] = np.ascontiguousarray(inp["x_prompt"][2 * cid:2 * cid + 2].reshape(512, D))
        m["xs"] = np.ascontiguousarray(inp["x_sample"][b, r * 512:(r + 1) * 512])
        for k_, v_ in pc.items():
            if k_ != "nabA_idx":
                m[k_] = v_
        m["nabA"] = np.ascontiguousarray(rp[:, :, pc["nabA_idx"]])
        m["cvec"] = cv
        m["ada_wq"] = np.ascontiguousarray(inp["ada_w"][:, :, r * 2304:(r + 1) * 2304])
        m["cna_k"] = np.ascontiguousarray(inp["cache_na_k"][b].reshape(DEPTH, 256, 384))
        m["cna_v"] = np.ascontiguousarray(inp["cache_na_v"][b].reshape(DEPTH, 256, 384))
        m["cgq_k"] = np.ascontiguousarray(inp["cache_gqa_k"][b].reshape(DEPTH, 256, 128))
        m["cgq_v"] = np.ascontiguousarray(inp["cache_gqa_v"][b].reshape(DEPTH, 256, 128))
        m = {k: v for k, v in m.items() if k in B.din}
        in_maps.append(m)
    res = run_bass_kernel_spmd(B.nc, in_maps, core_ids=list(range(8)))
    R = res.results
    _CACHE["R"] = R
    y_prompt = np.concatenate([R[i]["yp"].reshape(2, 256, D) for i in range(8)], 0)
    y_sample = np.stack([np.concatenate([R[b * 4 + r]["ys"] for r in range(4)], 0) for b in range(2)], 0)
    if "nk" not in R[0]:
        return y_prompt, y_sample
    nk = np.concatenate([R[i]["nk"].reshape(2, DEPTH, 256, 6, 64) for i in range(8)], 0)
    nv = np.concatenate([R[i]["nv"].reshape(2, DEPTH, 256, 6, 64) for i in range(8)], 0)
    gk = np.concatenate([R[i]["gk"].reshape(2, DEPTH, 256, 2, 64) for i in range(8)], 0)
    gv = np.concatenate([R[i]["gv"].reshape(2, DEPTH, 256, 2, 64) for i in range(8)], 0)
    return y_prompt, y_sample, nk, nv, gk, gv
```
